# Optimizing a Trainium2 kernel written in Bass

```python
import math
import jax, jax.numpy as jnp
from jax import lax
import numpy as np

D_MODEL = 1024
BATCH = 2
SEQ = 8192
DEPTH = 2
DEC_BATCH = 128
DEC_SEQ = 4
PAST_LEN = 2048
PAGE_SIZE = 128

N_EVEN = (DEPTH + 1) // 2
N_ODD = DEPTH // 2
EPS = 1e-6
NEG = -1e30

GLA_HEADS = 4
GLA_DK = 64
GLA_DV = 128
GLA_RANK = 16
GLA_TAU = 16.0
GLA_CHUNK = 16
CONV_CH = 512
CONV_WIDTH = 31
NSA_HEADS = 16
NSA_GROUPS = 4
NSA_REP = NSA_HEADS // NSA_GROUPS
NSA_DH = 64
CMP_LEN = 32
CMP_STRIDE = 16
CMP_HIDDEN = 64
SEL_BLOCK = 64
SEL_TOPN = 16
WINDOW = 512
Q_BLOCK = 128
N_MEM = 256
X_HEADS = 4
X_DH = 128
D_FF = 2816
FFN_WIDTH = 3

GLA_HK = GLA_HEADS * GLA_DK
GLA_HV = GLA_HEADS * GLA_DV
MIX_WIDTH = GLA_HV + CONV_CH
IN_A = 2 * GLA_HK + 2 * GLA_HV + GLA_RANK + 2 * CONV_CH
NSA_KVW = NSA_GROUPS * NSA_DH
IN_C = NSA_HEADS * NSA_DH + 6 * NSA_KVW + 3 * NSA_HEADS

kernel_name = 'hybrid_gla_conformer_nsa_decoder_step'


def rmsnorm(x, g):
    xf = x.astype(jnp.float32)
    y = xf * lax.rsqrt(jnp.mean(xf * xf, axis=-1, keepdims=True) + EPS) * g.astype(jnp.float32)
    return y.astype(x.dtype)


def causal_dwconv(xpad, w, b):
    ch = w.shape[1]
    y = lax.conv_general_dilated(xpad.astype(w.dtype), w[:, None, :], window_strides=(1,),
                                 padding='VALID', dimension_numbers=('NWC', 'WIO', 'NWC'),
                                 feature_group_count=ch)
    return y + b


def gla_recurrence(q, k, v, log_a, s0):
    bsz, t_len, nh, dk = q.shape
    dv = v.shape[-1]
    c = math.gcd(t_len, GLA_CHUNK)
    n = t_len // c

    def chunks(a):
        return jnp.moveaxis(a.reshape(bsz, n, c, *a.shape[2:]), 1, 0)

    tri = jnp.tril(jnp.ones((c, c), jnp.float32))

    def step(s, inp):
        qc, kc, vc, gc = inp
        b = jnp.cumsum(gc.astype(jnp.float32), axis=1)
        b_last = b[:, -1:]
        qe = qc * jnp.exp(b)
        ke = kc * jnp.exp(-b)
        kl = kc * jnp.exp(b_last - b)
        att = jnp.einsum('bihd,bjhd->bhij', qe, ke) * tri
        o = jnp.einsum('bhij,bjhv->bihv', att, vc) + jnp.einsum('bihd,bhdv->bihv', qe, s)
        s = jnp.exp(b_last[:, 0])[..., None] * s + jnp.einsum('bjhd,bjhv->bhdv', kl, vc)
        return s, o

    s, o = lax.scan(step, s0, (chunks(q), chunks(k), chunks(v), chunks(log_a)))
    o = jnp.moveaxis(o, 0, 1).reshape(bsz, t_len, nh, dv)
    return o, s


def mixer_gla_conv(xn, s0, conv_hist, w_in, w_gate, b_gate, g_out, w_dw, b_dw, g_ln, b_ln, w_out):
    bsz, t_len, _ = xn.shape
    cuts = [int(v) for v in np.cumsum([GLA_HK, GLA_HK, GLA_HV, GLA_HV, GLA_RANK])]
    q, k, v, r, a_lr, u = jnp.split(xn @ w_in, cuts, axis=-1)
    q = q.reshape(bsz, t_len, GLA_HEADS, GLA_DK) * (GLA_DK ** -0.5)
    k = k.reshape(bsz, t_len, GLA_HEADS, GLA_DK)
    v = v.reshape(bsz, t_len, GLA_HEADS, GLA_DV)
    log_a = jax.nn.log_sigmoid((a_lr @ w_gate + b_gate).astype(jnp.float32)) / GLA_TAU
    log_a = log_a.reshape(bsz, t_len, GLA_HEADS, GLA_DK)
    o, s = gla_recurrence(q, k, v, log_a, s0.astype(jnp.float32))
    o = o * lax.rsqrt(jnp.mean(o * o, axis=-1, keepdims=True) + EPS) \
        * g_out.reshape(GLA_HEADS, GLA_DV).astype(jnp.float32)
    o = (o.reshape(bsz, t_len, GLA_HV) * jax.nn.silu(r.astype(jnp.float32))).astype(xn.dtype)
    glu = u[..., :CONV_CH] * jax.nn.sigmoid(u[..., CONV_CH:])
    gpad = jnp.concatenate([conv_hist.astype(glu.dtype), glu], axis=1)
    c = causal_dwconv(gpad, w_dw, b_dw).astype(jnp.float32)
    d = c - jnp.mean(c, axis=-1, keepdims=True)
    c = d * lax.rsqrt(jnp.mean(d * d, axis=-1, keepdims=True) + EPS) * g_ln + b_ln
    c = jax.nn.silu(c).astype(xn.dtype)
    y = jnp.concatenate([o, c], axis=-1) @ w_out
    return y, s.astype(xn.dtype), gpad[:, -(CONV_WIDTH - 1):]


def masked_softmax(s, mask):
    s = jnp.where(mask, s.astype(jnp.float32), NEG)
    p = jnp.exp(s - jnp.max(s, axis=-1, keepdims=True)) * mask
    return p / jnp.maximum(jnp.sum(p, axis=-1, keepdims=True), 1e-30)


def compress(rows, pe, w1, w2):
    bsz, t_len = rows.shape[:2]
    n = t_len // CMP_STRIDE
    pieces = rows[:, :n * CMP_STRIDE].reshape(bsz, n, CMP_STRIDE, NSA_GROUPS, NSA_DH)
    blocks = jnp.concatenate([pieces[:, :-1], pieces[:, 1:]], axis=2)
    h = jax.nn.silu(jnp.einsum('bnlgd,lde->bnge', blocks + pe[:, None, :], w1))
    return jnp.einsum('bnge,ef->bngf', h, w2)


def to_sel_blocks(rows):
    bsz, t_len = rows.shape[:2]
    ns = -(-t_len // SEL_BLOCK)
    rows = jnp.pad(rows, ((0, 0), (0, ns * SEL_BLOCK - t_len), (0, 0), (0, 0)))
    return rows.reshape(bsz, ns, SEL_BLOCK, NSA_GROUPS, NSA_DH).transpose(0, 3, 1, 2, 4)


def nsa_core(q, gates, kc, vc, ks_blk, vs_blk, kw, vw, w_pos, q_pos):
    bsz, tq_len = q.shape[:2]
    nc = kc.shape[1]
    ns = ks_blk.shape[2]
    tq = q_pos[:, None]
    c_start = CMP_STRIDE * jnp.arange(nc)
    c_end = c_start + CMP_LEN - 1
    c_mask = (c_end[None, :] <= tq)[None, :, None, None, :]
    p_c = masked_softmax(jnp.einsum('btgrd,bngd->btgrn', q, kc), c_mask)
    o_c = jnp.einsum('btgrn,bngd->btgrd', p_c.astype(vc.dtype), vc)
    s_start = SEL_BLOCK * jnp.arange(ns)
    cover = ((c_start[:, None] <= s_start[None, :] + SEL_BLOCK - 1)
             & (c_end[:, None] >= s_start[None, :])).astype(jnp.float32)
    imp = jnp.einsum('btgrn,nj->btgj', p_c, cover)
    cur = q_pos // SEL_BLOCK
    j = jnp.arange(ns)[None, :]
    forced = (j == 0) | (j == cur[:, None]) | (j == cur[:, None] - 1)
    valid = s_start[None, :] <= tq
    score = jnp.where(forced[None, :, None, :], 1e6,
                      jnp.where(valid[None, :, None, :], imp, -1e6))
    _, idx = lax.top_k(score, min(SEL_TOPN, ns))
    n_sel = idx.shape[-1]
    bi = jnp.arange(bsz)[:, None, None, None]
    gi = jnp.arange(NSA_GROUPS)[None, None, :, None]
    kg = ks_blk[bi, gi, idx]
    vg = vs_blk[bi, gi, idx]
    pos = idx[..., None] * SEL_BLOCK + jnp.arange(SEL_BLOCK)
    s_mask = (pos <= q_pos[None, :, None, None, None]).reshape(bsz, tq_len, NSA_GROUPS, 1, n_sel * SEL_BLOCK)
    s_s = jnp.einsum('btgrd,btgnsd->btgrns', q, kg).reshape(bsz, tq_len, NSA_GROUPS, NSA_REP, n_sel * SEL_BLOCK)
    p_s = masked_softmax(s_s, s_mask).reshape(bsz, tq_len, NSA_GROUPS, NSA_REP, n_sel, SEL_BLOCK)
    o_s = jnp.einsum('btgrns,btgnsd->btgrd', p_s.astype(vg.dtype), vg)
    wp = w_pos[None, :]
    w_mask = ((wp <= tq) & (wp > tq - WINDOW) & (wp >= 0))[None, :, None, None, :]
    p_w = masked_softmax(jnp.einsum('btgrd,bwgd->btgrw', q, kw), w_mask)
    o_w = jnp.einsum('btgrw,bwgd->btgrd', p_w.astype(vw.dtype), vw)
    return gates[..., 0:1] * o_c + gates[..., 1:2] * o_s + gates[..., 2:3] * o_w


def nsa_project(xn, w_in):
    bsz, t_len, _ = xn.shape
    hq = NSA_HEADS * NSA_DH
    q, kv, g = jnp.split(xn @ w_in, [hq, hq + 6 * NSA_KVW], axis=-1)
    q = q.reshape(bsz, t_len, NSA_GROUPS, NSA_REP, NSA_DH) * (NSA_DH ** -0.5)
    kv = kv.reshape(bsz, t_len, 6, NSA_GROUPS, NSA_DH)
    gates = jax.nn.sigmoid(g.reshape(bsz, t_len, NSA_GROUPS, NSA_REP, 3))
    return q, kv, gates


def nsa_prompt(xn, w_in, pe, w1, w2, w_out):
    bsz, t_len, _ = xn.shape
    q, kv, gates = nsa_project(xn, w_in)
    kc = compress(kv[:, :, 0], pe[0], w1[0], w2[0])
    vc = compress(kv[:, :, 1], pe[1], w1[1], w2[1])
    ks_blk = to_sel_blocks(kv[:, :, 2])
    vs_blk = to_sel_blocks(kv[:, :, 3])
    pad = ((0, 0), (WINDOW, 0), (0, 0), (0, 0))
    kw_pad = jnp.pad(kv[:, :, 4], pad)
    vw_pad = jnp.pad(kv[:, :, 5], pad)

    def block(i):
        start = i * Q_BLOCK
        q_pos = start + jnp.arange(Q_BLOCK)
        w_pos = start - WINDOW + jnp.arange(Q_BLOCK + WINDOW)
        qb = lax.dynamic_slice_in_dim(q, start, Q_BLOCK, axis=1)
        gb = lax.dynamic_slice_in_dim(gates, start, Q_BLOCK, axis=1)
        kw = lax.dynamic_slice_in_dim(kw_pad, start, Q_BLOCK + WINDOW, axis=1)
        vw = lax.dynamic_slice_in_dim(vw_pad, start, Q_BLOCK + WINDOW, axis=1)
        return nsa_core(qb, gb, kc, vc, ks_blk, vs_blk, kw, vw, w_pos, q_pos)

    o = lax.map(block, jnp.arange(t_len // Q_BLOCK))
    o = jnp.moveaxis(o, 0, 1).reshape(bsz, t_len, NSA_HEADS * NSA_DH).astype(xn.dtype)
    win_rows = min(WINDOW, t_len)
    return o @ w_out, kv[:, :, :4], kv[:, t_len - win_rows:, 4:]


def nsa_sample(xn, pool, page_table, win_buf, w_in, pe, w1, w2, w_out):
    bsz, t_len, _ = xn.shape
    q, kv, gates = nsa_project(xn, w_in)
    past_len = page_table.shape[1] * PAGE_SIZE
    past = pool[page_table].reshape(bsz, past_len, 4, NSA_GROUPS, NSA_DH)
    full = jnp.concatenate([past.astype(kv.dtype), kv[:, :, :4]], axis=1)
    kc = compress(full[:, :, 0], pe[0], w1[0], w2[0])
    vc = compress(full[:, :, 1], pe[1], w1[1], w2[1])
    ks_blk = to_sel_blocks(full[:, :, 2])
    vs_blk = to_sel_blocks(full[:, :, 3])
    wb = win_buf.shape[1]
    wfull = jnp.concatenate([win_buf.astype(kv.dtype), kv[:, :, 4:]], axis=1)
    w_pos = past_len - wb + jnp.arange(wb + t_len)
    q_pos = past_len + jnp.arange(t_len)
    o = nsa_core(q, gates, kc, vc, ks_blk, vs_blk, wfull[:, :, 0], wfull[:, :, 1], w_pos, q_pos)
    o = o.reshape(bsz, t_len, NSA_HEADS * NSA_DH).astype(xn.dtype)
    return o @ w_out, kv[:, :, :4], wfull[:, -wb:]


def mem_kv_proj(mem, g, w_kv):
    m = rmsnorm(mem, g) @ w_kv
    return m.reshape(m.shape[0], N_MEM, 2, X_HEADS, X_DH)


def cross_attn(xn, kv, w_q, w_o):
    bsz, t_len, _ = xn.shape
    q = (xn @ w_q).reshape(bsz, t_len, X_HEADS, X_DH) * (X_DH ** -0.5)
    s = jnp.einsum('bthd,bmhd->bhtm', q, kv[:, :, 0].astype(q.dtype)).astype(jnp.float32)
    p = jax.nn.softmax(s, axis=-1).astype(xn.dtype)
    o = jnp.einsum('bhtm,bmhd->bthd', p, kv[:, :, 1].astype(xn.dtype)).reshape(bsz, t_len, X_HEADS * X_DH)
    return o @ w_o


def conv_ffn(xn, hist, w_up, w_dw, b_dw, w_down):
    u = xn @ w_up
    upad = jnp.concatenate([hist.astype(u.dtype), u], axis=1)
    c = causal_dwconv(upad, w_dw, b_dw)
    h = jax.nn.silu(c[..., :D_FF]) * c[..., D_FF:]
    return (h @ w_down).astype(xn.dtype), upad[:, -(FFN_WIDTH - 1):]


def setup_inputs(seed: int = 0) -> dict:
    key = jax.random.key(seed)
    keys = iter(jax.random.split(key, 64))
    D = D_MODEL

    def nrm(shape, scale):
        return jax.random.normal(next(keys), shape, jnp.float32) * scale

    def gain(shape):
        return 1.0 + nrm(shape, 0.02)

    n_pages = PAST_LEN // PAGE_SIZE
    n_used = DEC_BATCH * n_pages
    n_phys = n_used + n_used // 4
    page_table = jax.random.permutation(next(keys), n_phys)[:n_used].reshape(DEC_BATCH, n_pages).astype(jnp.int32)
    wb = min(WINDOW, PAST_LEN)
    return {
        'x_prompt': nrm((BATCH, SEQ, D), 1.0),
        'x_sample': nrm((DEC_BATCH, DEC_SEQ, D), 1.0),
        'cache_gla_state': nrm((N_EVEN, DEC_BATCH, GLA_HEADS, GLA_DK, GLA_DV), 1.0),
        'cache_conv': nrm((N_EVEN, DEC_BATCH, CONV_WIDTH - 1, CONV_CH), 0.5),
        'cache_nsa_kv': nrm((N_ODD, n_phys, PAGE_SIZE, 4, NSA_GROUPS, NSA_DH), 1.0),
        'cache_nsa_win': nrm((N_ODD, DEC_BATCH, wb, 2, NSA_GROUPS, NSA_DH), 1.0),
        'cache_mem_kv': nrm((DEPTH, DEC_BATCH, N_MEM, 2, X_HEADS, X_DH), 1.0),
        'cache_ffn_conv': nrm((DEPTH, DEC_BATCH, FFN_WIDTH - 1, 2 * D_FF), 0.5),
        'page_table': page_table,
        'mem_prompt': nrm((BATCH, N_MEM, D), 1.0),
        'norm_mix': gain((DEPTH, D)),
        'norm_mem': gain((DEPTH, D)),
        'norm_x': gain((DEPTH, D)),
        'norm_ffn': gain((DEPTH, D)),
        'norm_final': gain((D,)),
        'w_in_a': nrm((N_EVEN, D, IN_A), D ** -0.5),
        'w_gate_a': nrm((N_EVEN, GLA_RANK, GLA_HK), GLA_RANK ** -0.5),
        'b_gate_a': nrm((N_EVEN, GLA_HK), 0.1),
        'g_gla_out': gain((N_EVEN, GLA_HV)),
        'w_dw_b': nrm((N_EVEN, CONV_WIDTH, CONV_CH), CONV_WIDTH ** -0.5),
        'b_dw_b': nrm((N_EVEN, CONV_CH), 0.02),
        'g_ln_b': gain((N_EVEN, CONV_CH)),
        'b_ln_b': nrm((N_EVEN, CONV_CH), 0.02),
        'w_out_a': nrm((N_EVEN, MIX_WIDTH, D), MIX_WIDTH ** -0.5),
        'w_in_c': nrm((N_ODD, D, IN_C), D ** -0.5),
        'pe_cmp': nrm((N_ODD, 2, CMP_LEN, NSA_DH), 0.1),
        'w_cmp1': nrm((N_ODD, 2, CMP_LEN, NSA_DH, CMP_HIDDEN), (CMP_LEN * NSA_DH) ** -0.5),
        'w_cmp2': nrm((N_ODD, 2, CMP_HIDDEN, NSA_DH), CMP_HIDDEN ** -0.5),
        'w_out_c': nrm((N_ODD, NSA_HEADS * NSA_DH, D), (NSA_HEADS * NSA_DH) ** -0.5),
        'w_xq': nrm((DEPTH, D, X_HEADS * X_DH), D ** -0.5),
        'w_mem_kv': nrm((DEPTH, D, 2 * X_HEADS * X_DH), D ** -0.5),
        'w_xo': nrm((DEPTH, X_HEADS * X_DH, D), (X_HEADS * X_DH) ** -0.5),
        'w_up': nrm((DEPTH, D, 2 * D_FF), D ** -0.5),
        'w_ffn_dw': nrm((DEPTH, FFN_WIDTH, 2 * D_FF), FFN_WIDTH ** -0.5),
        'b_ffn_dw': nrm((DEPTH, 2 * D_FF), 0.02),
        'w_down': nrm((DEPTH, D_FF, D), D_FF ** -0.5),
    }


def reference(x_prompt, x_sample, cache_gla_state, cache_conv, cache_nsa_kv, cache_nsa_win,
              cache_mem_kv, cache_ffn_conv, page_table, mem_prompt,
              norm_mix, norm_mem, norm_x, norm_ffn, norm_final,
              w_in_a, w_gate_a, b_gate_a, g_gla_out, w_dw_b, b_dw_b, g_ln_b, b_ln_b, w_out_a,
              w_in_c, pe_cmp, w_cmp1, w_cmp2, w_out_c,
              w_xq, w_mem_kv, w_xo,
              w_up, w_ffn_dw, b_ffn_dw, w_down):
    xp, xs = x_prompt, x_sample
    bp = xp.shape[0]
    gla_p, gla_s, conv_p, conv_s = [], [], [], []
    nsa_p, nsa_s, win_p, win_s = [], [], [], []
    mem_p, ffn_p, ffn_s = [], [], []
    for l in range(DEPTH):
        i = l // 2
        hp = rmsnorm(xp, norm_mix[l])
        hs = rmsnorm(xs, norm_mix[l])
        if l % 2 == 0:
            wa = (w_in_a[i], w_gate_a[i], b_gate_a[i], g_gla_out[i], w_dw_b[i], b_dw_b[i],
                  g_ln_b[i], b_ln_b[i], w_out_a[i])
            s0 = jnp.zeros((bp, GLA_HEADS, GLA_DK, GLA_DV), jnp.float32)
            c0 = jnp.zeros((bp, CONV_WIDTH - 1, CONV_CH), xp.dtype)
            yp, sp, cp = mixer_gla_conv(hp, s0, c0, *wa)
            ys, ss, cs = mixer_gla_conv(hs, cache_gla_state[i], cache_conv[i], *wa)
            gla_p.append(sp)
            gla_s.append(ss)
            conv_p.append(cp)
            conv_s.append(cs)
        else:
            yp, kvp, wnp = nsa_prompt(hp, w_in_c[i], pe_cmp[i], w_cmp1[i], w_cmp2[i], w_out_c[i])
            ys, kvs, wns = nsa_sample(hs, cache_nsa_kv[i], page_table, cache_nsa_win[i], w_in_c[i],
                                      pe_cmp[i], w_cmp1[i], w_cmp2[i], w_out_c[i])
            nsa_p.append(kvp)
            nsa_s.append(kvs)
            win_p.append(wnp)
            win_s.append(wns)
        xp = xp + yp
        xs = xs + ys
        mkv = mem_kv_proj(mem_prompt, norm_mem[l], w_mem_kv[l])
        mem_p.append(mkv)
        xp = xp + cross_attn(rmsnorm(xp, norm_x[l]), mkv, w_xq[l], w_xo[l])
        xs = xs + cross_attn(rmsnorm(xs, norm_x[l]), cache_mem_kv[l], w_xq[l], w_xo[l])
        h0 = jnp.zeros((bp, FFN_WIDTH - 1, 2 * D_FF), xp.dtype)
        fp, hfp = conv_ffn(rmsnorm(xp, norm_ffn[l]), h0, w_up[l], w_ffn_dw[l], b_ffn_dw[l], w_down[l])
        fs, hfs = conv_ffn(rmsnorm(xs, norm_ffn[l]), cache_ffn_conv[l], w_up[l], w_ffn_dw[l], b_ffn_dw[l], w_down[l])
        xp = xp + fp
        xs = xs + fs
        ffn_p.append(hfp)
        ffn_s.append(hfs)
    y_prompt = rmsnorm(xp, norm_final)
    y_sample = rmsnorm(xs, norm_final)
    return (y_prompt, y_sample,
            jnp.stack(gla_p), jnp.stack(gla_s),
            jnp.stack(conv_p), jnp.stack(conv_s),
            jnp.stack(nsa_p), jnp.stack(nsa_s),
            jnp.stack(win_p), jnp.stack(win_s),
            jnp.stack(mem_p),
            jnp.stack(ffn_p), jnp.stack(ffn_s))
```

```python
import os
from contextlib import ExitStack
import numpy as np
import ml_dtypes
import concourse.bass as bass
import concourse.mybir as mybir
from concourse.bass_utils import run_bass_kernel_spmd

F32 = mybir.dt.float32
BF16 = mybir.dt.bfloat16
I32 = mybir.dt.int32
AF = mybir.ActivationFunctionType
ALU = mybir.AluOpType
AX = mybir.AxisListType

D = 1024
SEQ = 8192
TP = 256
NTILE = int(os.environ.get("MK_NT", str(SEQ // TP)))
SB_ = 16
ST = 64
TTOT = SEQ + ST
EPS = 1e-6
IN_A = 2576
DFF = 2816
EPOCH = 30000
STAGE = int(os.environ.get("MK_STAGE", "4"))


class V:
    __slots__ = ("buf", "ap")

    def __init__(self, buf, ap):
        self.buf = buf
        self.ap = ap


class Buf:
    def __init__(self, name, t, kind):
        self.name = name
        self.t = t
        self.kind = kind
        self.last_w = None
        self.readers = []
        self.dsem = None
        self.dcnt = 0

    def __getitem__(self, idx):
        return V(self, self.t[idx])


class Sched:
    ENG = ["tensor", "vector", "scalar", "gpsimd", "sync"]

    def __init__(self, nc, es):
        self.nc = nc
        self.es = es
        self.ops = {e: [] for e in self.ENG}
        self.cnt = {e: 0 for e in self.ENG}
        self.sems = {}
        self.esem = {}
        self.nsem = 0
        for e in self.ENG:
            self.esem[e] = self.new_sem("p_" + e)
        self.known = {e: {} for e in self.ENG}
        self.pending = {e: set() for e in self.ENG}
        self.bufs = []
        self.dd = Buf("dram2dram", None, "x")

    def new_sem(self, name):
        self.nsem += 1
        nm = "%s_%d" % (name, self.nsem)
        self.sems[nm] = self.es.enter_context(self.nc.semaphore(nm))
        return nm

    def sb(self, name, shape, dt):
        t = self.es.enter_context(self.nc.sbuf_tensor(name, list(shape), dt))
        b = Buf(name, t, "sb")
        self.bufs.append(b)
        return b

    def ps(self, name, shape, dt=F32):
        t = self.es.enter_context(self.nc.psum_tensor(name, list(shape), dt))
        b = Buf(name, t, "ps")
        self.bufs.append(b)
        return b

    def dram(self, name, shape, dt, kind):
        t = self.nc.dram_tensor(name, list(shape), dt, kind=kind).ap()
        b = Buf(name, t, "dr")
        self.bufs.append(b)
        return b

    def _deps(self, eng, reads, writes, extra):
        deps = set(extra) | self.pending[eng]
        self.pending[eng] = set()
        for b in reads:
            if b.last_w is not None:
                deps.add(b.last_w)
        for b in writes:
            if b.last_w is not None:
                deps.add(b.last_w)
            deps.update(b.readers)
        best = {}
        for (s, v) in deps:
            if v > best.get(s, 0):
                best[s] = v
        waits = []
        for s, v in best.items():
            if eng == "tensor" and s == self.esem[eng]:
                continue
            if self.known[eng].get(s, 0) >= v:
                continue
            self.known[eng][s] = v
            waits.append((s, v))
        return waits

    def _commit(self, ev, reads, writes):
        for b in reads:
            b.readers.append(ev)
        for b in writes:
            b.last_w = ev
            b.readers = []

    def op(self, eng, fn, reads=(), writes=(), extra=()):
        reads = [r.buf if isinstance(r, V) else r for r in reads]
        writes = [w.buf if isinstance(w, V) else w for w in writes]
        if self.cnt[eng] >= EPOCH:
            self.esem[eng] = self.new_sem("p_" + eng)
            self.cnt[eng] = 0
        waits = self._deps(eng, reads, writes, extra)
        self.cnt[eng] += 1
        ev = (self.esem[eng], self.cnt[eng])
        self.ops[eng].append((fn, waits, ev[0], 1))
        self._commit(ev, reads, writes)
        return ev

    def dma(self, out, in_, eng="sync", extra=(), **kw):
        ob, ib = out.buf, in_.buf
        if ob.kind == "sb":
            owner = ob
        elif ib.kind == "sb":
            owner = ib
        else:
            owner = self.dd
        if owner.dsem is None:
            owner.dsem = self.new_sem("d_" + owner.name)
        waits = self._deps(eng, [ib], [ob], extra)
        owner.dcnt += 16
        ev = (owner.dsem, owner.dcnt)
        oa, ia = out.ap, in_.ap
        self.ops[eng].append((lambda e: e.dma_start(out=oa, in_=ia, **kw), waits, ev[0], 16))
        self._commit(ev, [ib], [ob])
        return ev

    def idma(self, out, in_, idx):
        ob, ib = out.buf, in_.buf
        owner = ob
        if owner.dsem is None:
            owner.dsem = self.new_sem("d_" + owner.name)
        waits = self._deps("gpsimd", [ib, idx.buf], [ob], ())
        owner.dcnt += 16
        ev = (owner.dsem, owner.dcnt)
        oa, ia, xa = out.ap, in_.ap, idx.ap
        self.ops["gpsimd"].append((lambda e: e.indirect_dma_start(
            out=oa, out_offset=None, in_=ia, in_offset=bass.IndirectOffsetOnAxis(ap=xa, axis=0)), waits, ev[0], 16))
        self._commit(ev, [ib, idx.buf], [ob])
        return ev

    def barrier(self):
        deps = set()
        for e in self.ENG:
            if self.cnt[e] > 0:
                deps.add((self.esem[e], self.cnt[e]))
        for b in self.bufs + [self.dd]:
            if b.dsem is not None and b.dcnt > 0:
                deps.add((b.dsem, b.dcnt))
        ev = self.dma(self.bar_dst, self.bar_src, eng="sync", extra=deps)
        for e in self.ENG:
            self.pending[e].add(ev)
        return ev

    def emit(self):
        nc = self.nc
        sems = self.sems
        with nc.Block() as block:
            def mk(eng):
                ops = self.ops[eng]

                def body(e):
                    for fn, waits, isem, inc in ops:
                        for (s, v) in waits:
                            e.wait_ge(sems[s], v)
                        fn(e).then_inc(sems[isem], inc)
                return body
            block.tensor(mk("tensor"))
            block.vector(mk("vector"))
            block.scalar(mk("scalar"))
            block.gpsimd(mk("gpsimd"))
            block.sync(mk("sync"))
        self.ops = {e: [] for e in self.ENG}

    def mm(self, out, lhsT, rhs, start=True, stop=True):
        oa, la, ra = out.ap, lhsT.ap, rhs.ap
        rd = [lhsT, rhs] + ([] if start else [out])
        return self.op("tensor", lambda e: e.matmul(oa, la, ra, start=start, stop=stop),
                       reads=rd, writes=[out])

    def act(self, out, in_, func, bias=None, scale=None, accum=None, reads=()):
        oa, ia = out.ap, in_.ap
        kw = {}
        rd = [in_] + list(reads)
        if bias is not None:
            if isinstance(bias, V):
                kw["bias"] = bias.ap
                rd.append(bias)
            else:
                kw["bias"] = bias
        if scale is not None:
            if isinstance(scale, V):
                kw["scale"] = scale.ap
                rd.append(scale)
            else:
                kw["scale"] = scale
        wr = [out]
        if accum is not None:
            kw["accum_out"] = accum.ap
            wr.append(accum)
        return self.op("scalar", lambda e: e.activation(oa, ia, func, **kw), reads=rd, writes=wr)

    def tt(self, out, a, b, op, eng="vector"):
        oa, aa, ba = out.ap, a.ap, b.ap
        return self.op(eng, lambda e: e.tensor_tensor(oa, aa, ba, op), reads=[a, b], writes=[out])

    def ts(self, out, a, s1, s2, op0, op1=None, eng="vector"):
        oa, aa = out.ap, a.ap
        rd = [a]
        if isinstance(s1, V):
            rd.append(s1)
            s1 = s1.ap
        if isinstance(s2, V):
            rd.append(s2)
            s2 = s2.ap
        if op1 is None:
            return self.op(eng, lambda e: e.tensor_scalar(oa, aa, s1, None, op0), reads=rd, writes=[out])
        return self.op(eng, lambda e: e.tensor_scalar(oa, aa, s1, s2, op0, op1), reads=rd, writes=[out])

    def stt(self, out, a, s, b, op0, op1, eng="vector"):
        eng = "vector"
        oa, aa, ba = out.ap, a.ap, b.ap
        rd = [a, b]
        if isinstance(s, V):
            rd.append(s)
            s = s.ap
        return self.op(eng, lambda e: e.scalar_tensor_tensor(oa, aa, s, ba, op0, op1), reads=rd, writes=[out])

    def recip(self, out, in_):
        oa, ia = out.ap, in_.ap
        return self.op("vector", lambda e: e.reciprocal(oa, ia), reads=[in_], writes=[out])

    def rsqrt(self, out, in_, scale, bias):
        self.act(out, in_, AF.Sqrt, bias=bias, scale=scale)
        self.recip(out, out)

    def copy(self, out, in_, eng="vector"):
        oa, ia = out.ap, in_.ap
        if eng == "scalar":
            return self.op(eng, lambda e: e.copy(oa, ia), reads=[in_], writes=[out])
        if eng == "vector" and in_.buf.kind == "ps":
            return self.op(eng, lambda e: e.tensor_scalar(oa, ia, 1.0, None, ALU.mult), reads=[in_], writes=[out])
        return self.op(eng, lambda e: e.tensor_copy(oa, ia), reads=[in_], writes=[out])

    def memset(self, out, val, eng="gpsimd"):
        oa = out.ap
        return self.op(eng, lambda e: e.memset(oa, val), writes=[out])

    def transpose(self, out, in_, ident):
        oa, ia, da = out.ap, in_.ap, ident.ap
        return self.op("tensor", lambda e: e.transpose(oa, ia, da), reads=[in_, ident], writes=[out])


def build(n_layers_run=2):
    nc = bass.Bass("TRN2", target_bir_lowering=False)
    es = ExitStack()
    S = Sched(nc, es)
    IN, OUT = "ExternalInput", "ExternalOutput"

    xT_in = S.dram("xT_in", [D, TTOT], F32, IN)
    XA = S.dram("XA", [D, TTOT], F32, "Internal")
    XB = S.dram("XB", [D, TTOT], F32, "Internal")
    c_ident = S.dram("c_ident", [128, 128], F32, IN)
    c_triN = S.dram("c_triN", [128, 128], F32, IN)
    c_triU = S.dram("c_triU", [128, 128], F32, IN)
    c_mask = S.dram("c_mask", [128, 128], F32, IN)
    vecs = S.dram("vecs", [128, 96], F32, IN)
    w_in_a = S.dram("w_in_a", [D, IN_A], F32, IN)
    w_out_a = S.dram("w_out_a", [D, D], F32, IN)
    w_gate = S.dram("w_gate", [16, 256], F32, IN)
    b_gate = S.dram("b_gate", [1, 256], F32, IN)
    w_dwT = S.dram("w_dwT", [128, 4, 31], F32, IN)
    memT = S.dram("memT", [D, 256], F32, IN)
    w_xq = [S.dram("w_xq%d" % l, [D, 512], F32, IN) for l in range(2)]
    w_mkv = [S.dram("w_mkv%d" % l, [D, 1024], F32, IN) for l in range(2)]
    w_xo = [S.dram("w_xo%d" % l, [512, D], F32, IN) for l in range(2)]
    w_up = [S.dram("w_up%d" % l, [D, 2 * DFF], F32, IN) for l in range(2)]
    w_down = [S.dram("w_down%d" % l, [DFF, D], F32, IN) for l in range(2)]
    ffnv = [S.dram("ffnv%d" % l, [128, 44, 4], F32, IN) for l in range(2)]

    o_gla_p = S.dram("o_gla_p", [4, 64, 128], F32, OUT)
    o_conv_p = S.dram("o_conv_p", [30, 512], F32, OUT)
    o_memkv = [S.dram("o_memkv%d" % l, [256, 1024], F32, OUT) for l in range(2)]
    o_ffn_p = [S.dram("o_ffn_p%d" % l, [2, 2 * DFF], F32, OUT) for l in range(2)]
    o_yT = S.dram("o_yT", [D, TTOT], F32, OUT)
    w_in_c = S.dram("w_in_c", [D, 2608], F32, IN)
    c_smp = S.dram("c_smp", [64, 208], F32, IN)
    w_out_c = S.dram("w_out_c", [D, D], F32, IN)
    WoD = S.dram("WoD", [D, D], BF16, "Internal")
    w_cmp1 = S.dram("w_cmp1", [2, 32, 64, 64], F32, IN)
    w_cmp2 = S.dram("w_cmp2", [2, 64, 64], F32, IN)
    pe_T = S.dram("pe_T", [64, 2, 32], F32, IN)
    n_ind = S.dram("n_ind", [64, SEQ], BF16, IN)
    n_trib = S.dram("n_trib", [128, 2, 512], BF16, IN)
    n_cmbig = S.dram("n_cmbig", [128, 1016], BF16, IN)
    n_cmt = S.dram("n_cmt", [128, 17, 128], BF16, IN)
    WcD = S.dram("WcD", [D, 2608], BF16, "Internal")
    NPHYS = 2560
    poolF = [S.dram("poolF%d" % k, [NPHYS * 64, 512], F32, IN) for k in range(3)]
    n_smpb = S.dram("n_smpb", [128, 32], BF16, IN)
    poolV = S.dram("poolV", [NPHYS * 128, 256], F32, IN)
    pt_bc = S.dram("pt_bc", [16, 128, 16], I32, IN)
    winKT = S.dram("winKT", [16, 64, 4, 512], F32, IN)
    winVs = S.dram("winVs", [16, 128, 4, 256], F32, IN)
    n_smp = S.dram("n_smp", [128, 128], F32, IN)
    n_fbbig = S.dram("n_fbbig", [128, 254], F32, IN)
    n_sel = S.dram("n_sel", [48, 48 * 64], BF16, IN)
    gla_s0 = S.dram("gla_s0", [16, 4, 64, 128], F32, IN)
    convhT = S.dram("convhT", [128, 4, 16, 30], F32, IN)
    memKT = [S.dram("memKT%d" % l, [16, 128, 4, 256], F32, IN) for l in range(2)]
    memV = [S.dram("memV%d" % l, [16, 128, 2, 512], F32, IN) for l in range(2)]
    ffnhT = [S.dram("ffnhT%d" % l, [128, 44, 16, 2], F32, IN) for l in range(2)]
    win_cache = S.dram("win_cache", [16, 512, 512], F32, IN)
    o_gla_s = S.dram("o_gla_s", [16, 4, 64, 128], F32, OUT)
    o_conv_s = S.dram("o_conv_s", [128, 4, 16, 30], F32, OUT)
    o_ffn_s = [S.dram("o_ffn_s%d" % l, [128, 44, 16, 2], F32, OUT) for l in range(2)]
    o_nsa_kv_s = S.dram("o_nsa_kv_s", [64, 1024], F32, OUT)
    o_win_s = S.dram("o_win_s", [16, 512, 512], F32, OUT)
    o_nsa_kv = S.dram("o_nsa_kv", [SEQ, 1024], F32, OUT)
    o_win_p = S.dram("o_win_p", [512, 512], F32, OUT)

    ident = S.sb("ident", [128, 128], F32)
    identb = S.sb("identb", [128, 128], BF16)
    triN = S.sb("triN", [128, 128], F32)
    triU = S.sb("triU", [128, 128], F32)
    maskf = S.sb("maskf", [128, 128], F32)
    ones_f = S.sb("ones_f", [128, 128], F32)
    ones_b = S.sb("ones_b", [128, 128], BF16)
    vec = S.sb("vec", [128, 96], F32)
    barbuf = S.sb("barbuf", [1, 16], F32)
    S.bar_dst = barbuf[0:1, 0:16]
    S.bar_src = c_ident[0:1, 0:16]
    S.dma(ident[:, :], c_ident[:, :])
    S.dma(triN[:, :], c_triN[:, :])
    S.dma(triU[:, :], c_triU[:, :])
    S.dma(maskf[:, :], c_mask[:, :])
    S.dma(vec[:, :], vecs[:, :])
    S.memset(ones_f[:, :], 1.0)
    S.memset(ones_b[:, :], 1.0)
    S.copy(identb[:, :], ident[:, :])
    VN_MIX, VN_MEM, VN_X, VN_FFN = 0, 8, 16, 24
    V_GOUT, V_BDW, V_GLN, V_BLN = 64, 68, 72, 76

    PS = [S.ps("psb%d" % i, [128, 512]) for i in range(8)]

    def load_w_bf16(dst, src_ap_fn, kchunks, ncols):
        for kc in range(kchunks):
            n0 = 0
            while n0 < ncols:
                n1 = min(ncols, n0 + 2048)
                S.dma(dst[:, kc, n0:n1], src_ap_fn(kc, n0, n1), eng="gpsimd")
                n0 = n1

    def rmsnorm_tile(xT, T, gcol, sq, xn, rstd, psb):
        for kc in range(8):
            S.act(sq[:, kc, :T], xT[:, kc, :T], AF.Square)
        for kc in range(8):
            S.mm(psb[:, :T], ones_b[:, :], sq[:, kc, :T], start=(kc == 0), stop=(kc == 7))
        S.rsqrt(rstd[:, :T], psb[:, :T], 1.0 / D, EPS)
        for kc in range(8):
            S.stt(xn[:, kc, :T], xT[:, kc, :T], vec[:, gcol + kc:gcol + kc + 1], rstd[:, :T],
                  ALU.mult, ALU.mult, eng=("vector" if kc % 2 == 0 else "gpsimd"))

    def xview(X, t0, T):
        return V(X, X.t.rearrange("(k p) t -> p k t", p=128)[:, :, t0:t0 + T])

    tiles = [(i * TP, TP) for i in range(NTILE)]
    if os.environ.get("MK_SAMPLE", "1") != "0":
        tiles.append((SEQ, ST))

    def phase_mixer_a(Xin, Xout):
        with ExitStack() as pes:
            def sb(name, shape, dt):
                t = pes.enter_context(nc.sbuf_tensor(name, list(shape), dt))
                b = Buf(name, t, "sb")
                S.bufs.append(b)
                return b
            Win = sb("Win", [128, 8, IN_A], BF16)
            Wout = sb("Wout", [128, 8, D], BF16)
            wg = sb("wg", [16, 256], F32)
            bg = sb("bg", [1, 256], F32)
            wdw = sb("wdw", [128, 4, 31], F32)
            dg = sb("dg", [128, 4, 31, 128], BF16)
            load_w_bf16(Win, lambda kc, a, b: w_in_a[kc * 128:(kc + 1) * 128, a:b], 8, IN_A)
            load_w_bf16(Wout, lambda kc, a, b: w_out_a[kc * 128:(kc + 1) * 128, a:b], 8, D)
            S.dma(wg[:, :], w_gate[:, :])
            S.dma(bg[:, :], b_gate[:, :])
            S.dma(wdw[:, :, :], w_dwT[:, :, :])
            for m in range(4):
                for k in range(31):
                    S.ts(dg[:, m, k, :], ident[:, :], wdw[:, m, k:k + 1], None, ALU.mult,
                         eng=("vector" if (k % 2 == 0) else "gpsimd"))

            xT = sb("xT", [128, 8, TP], F32)
            sq = sb("sq", [128, 8, TP], BF16)
            xn = sb("xn", [128, 8, TP], BF16)
            rstd = sb("rstd", [128, TP], F32)
            qk = sb("qk", [64, 8, TP], F32)
            rs = sb("rs", [128, 4, TP], BF16)
            aT = sb("aT", [16, TP], F32)
            sg = sb("sg", [128, TP], F32)
            glu = sb("glu", [128, 4, 30 + TP], BF16)
            cc = sb("cc", [128, 4, TP], F32)
            csq = sb("csq", [128, 4, TP], F32)
            actT = sb("actT", [128, 8, TP], BF16)
            mean = sb("mean", [128, TP], F32)
            msq = sb("msq", [128, TP], F32)
            crs = sb("crs", [128, TP], F32)
            dtmp = sb("dtmp", [128, TP], F32)
            vsb = sb("vsb", [128, 512], BF16)
            e1 = sb("e1", [128, 256], F32)
            la = sb("la", [128, 256], F32)
            eD = sb("eD", [128, 256], F32)
            kl = sb("kl", [128, 256], BF16)
            eB = sb("eB", [64, 4, 128], F32)
            enB = sb("enB", [64, 4, 128], F32)
            qe = sb("qe", [64, 4, 128], BF16)
            ke = sb("ke", [64, 4, 128], BF16)
            att = sb("att", [128, 4, 128], BF16)
            Sf = sb("Sf", [64, 4, 128], F32)
            Sb = sb("Sb", [64, 4, 128], BF16)
            osq = sb("osq", [128, 512], BF16)
            orstd = sb("orstd", [128, 512], F32)
            og = sb("og", [128, 4, 128], F32)
            onesr = sb("onesr", [1, 128], F32)
            tail = sb("tail", [128, 4, 32], F32)
            tailT = sb("tailT", [32, 512], F32)
            csm = sb("csm", [64, 208], F32)
            S.dma(csm[:, :], c_smp[:, :])
            gps = sb("gps", [128, 4, 16, 34], F32)
            hst = sb("hst", [128, 4, 16, 30], F32)
            S0f = [sb("S0f%d" % i, [64, 4, 128], F32) for i in range(2)]
            S0h = [sb("S0h%d" % i, [64, 4, 128], BF16) for i in range(2)]
            Sout = [sb("Sout%d" % i, [64, 4, 128], F32) for i in range(2)]
            klm = sb("klm", [64, 256], BF16)
            S.memset(onesr[:, :], 1.0)
            S.memset(Sf[:, :, :], 0.0)
            S.memset(Sb[:, :, :], 0.0)
            S.memset(glu[:, :, :], 0.0)

            for ti, (t0, T) in enumerate(tiles):
                is_sample = (t0 >= SEQ)
                S.dma(xT[:, :, :T], xview(Xin, t0, T))
                rmsnorm_tile(xT, T, VN_MIX, sq, xn, rstd, PS[7])

                def proj(psv, col0, M):
                    for kc in range(8):
                        S.mm(psv, Win[:, kc, col0:col0 + M], xn[:, kc, :T], start=(kc == 0), stop=(kc == 7))
                for h in range(8):
                    pb = PS[h % 2]
                    proj(pb[0:64, :T], h * 64, 64)
                    if h < 4:
                        S.act(qk[:, h, :T], pb[0:64, :T], AF.Copy, scale=0.125)
                    else:
                        S.copy(qk[:, h, :T], pb[0:64, :T], eng="vector")
                for h in range(4):
                    pb = PS[h % 2]
                    proj(pb[:, :T], 1024 + h * 128, 128)
                    S.act(rs[:, h, :T], pb[:, :T], AF.Silu)
                proj(PS[0][0:16, :T], 1536, 16)
                S.copy(aT[:, :T], PS[0][0:16, :T], eng="vector")
                if ti > 0 and not is_sample:
                    S.copy(glu[:, :, 0:30], glu[:, :, TP:TP + 30], eng="gpsimd")
                if not is_sample:
                    for m in range(4):
                        proj(PS[1][:, :T], 1552 + 512 + m * 128, 128)
                        S.act(sg[:, :T], PS[1][:, :T], AF.Sigmoid)
                        proj(PS[0][:, :T], 1552 + m * 128, 128)
                        S.tt(glu[:, m, 30:30 + T], PS[0][:, :T], sg[:, :T], ALU.mult)

                if not is_sample:
                    for c in range(T // 128):
                        cs = slice(c * 128, (c + 1) * 128)
                        for kc in range(8):
                            S.mm(PS[0][:, :512], xn[:, kc, cs], Win[:, kc, 512:1024], start=(kc == 0), stop=(kc == 7))
                        S.copy(vsb[:, :], PS[0][:, :512], eng="scalar")
                        for kc in range(8):
                            S.mm(PS[1][:, :256], xn[:, kc, cs], Win[:, kc, 256:512], start=(kc == 0), stop=(kc == 7))
                        S.mm(PS[2][:, 0:256], aT[:, cs], wg[:, :], start=True, stop=False)
                        S.mm(PS[2][:, 0:256], onesr[:, :], bg[:, :], start=False, stop=True)
                        S.act(e1[:, :], PS[2][:, 0:256], AF.Exp, scale=-1.0)
                        S.act(la[:, :], e1[:, :], AF.Ln, bias=1.0)
                        S.mm(PS[2][:, 256:512], triU[:, :], la[:, :])
                        for h in range(4):
                            S.mm(PS[3][0:64, h * 128:(h + 1) * 128], la[:, h * 64:(h + 1) * 64], triN[:, :])
                        S.act(eD[:, :], PS[2][:, 256:512], AF.Exp)
                        S.tt(kl[:, :], PS[1][:, :256], eD[:, :], ALU.mult)
                        ps3 = V(PS[3], PS[3].t[0:64, :].rearrange("p (h t) -> p h t", h=4))
                        S.act(eB[:, :, :], ps3, AF.Exp)
                        S.act(enB[:, :, :], ps3, AF.Exp, scale=-1.0)
                        S.tt(qe[:, :, :], qk[:, 0:4, cs], eB[:, :, :], ALU.mult)
                        S.tt(ke[:, :, :], qk[:, 4:8, cs], enB[:, :, :], ALU.mult, eng="gpsimd")
                        for h in range(4):
                            S.mm(PS[4][:, h * 128:(h + 1) * 128], ke[:, h, :], qe[:, h, :])
                        for h in range(4):
                            S.tt(att[:, h, :], PS[4][:, h * 128:(h + 1) * 128], maskf[:, :], ALU.mult)
                        for h in range(4):
                            S.mm(PS[5][:, h * 128:(h + 1) * 128], vsb[:, h * 128:(h + 1) * 128], att[:, h, :],
                                 start=True, stop=False)
                            S.mm(PS[5][:, h * 128:(h + 1) * 128], Sb[:, h, :], qe[:, h, :], start=False, stop=True)
                        for h in range(4):
                            S.mm(PS[6][0:64, h * 128:(h + 1) * 128], kl[:, h * 64:(h + 1) * 64],
                                 vsb[:, h * 128:(h + 1) * 128])
                        for h in range(4):
                            S.stt(Sf[:, h, :], Sf[:, h, :], eB[:, h, 127:128], PS[6][0:64, h * 128:(h + 1) * 128],
                                  ALU.mult, ALU.add)
                        S.copy(Sb[:, :, :], Sf[:, :, :], eng="gpsimd")
                        S.act(osq[:, :], PS[5][:, :], AF.Square)
                        S.mm(PS[7][:, :], ones_b[:, :], osq[:, :])
                        S.rsqrt(orstd[:, :], PS[7][:, :], 1.0 / 128, EPS)
                        S.tt(V(og, og.t[:, :, :].rearrange("p h t -> p (h t)")), PS[5][:, :], orstd[:, :], ALU.mult)
                        for h in range(4):
                            S.stt(actT[:, h, cs], og[:, h, :], vec[:, V_GOUT + h:V_GOUT + h + 1], rs[:, h, cs],
                                  ALU.mult, ALU.mult, eng=("vector" if h % 2 == 0 else "gpsimd"))

                    for m in range(4):
                        for k in range(31):
                            S.mm(PS[0][:, :T], dg[:, m, k, :], glu[:, m, k:k + T], start=(k == 0), stop=(k == 30))
                        S.act(cc[:, m, :T], PS[0][:, :T], AF.Identity, bias=vec[:, V_BDW + m:V_BDW + m + 1])
                else:
                    for kc in range(8):
                        S.mm(PS[0][0:64, :512], xn[:, kc, 0:64], Win[:, kc, 512:1024], start=(kc == 0), stop=(kc == 7))
                    S.copy(vsb[0:64, :], PS[0][0:64, :512], eng="scalar")
                    for kc in range(8):
                        S.mm(PS[1][0:64, :256], xn[:, kc, 0:64], Win[:, kc, 256:512], start=(kc == 0), stop=(kc == 7))
                    S.mm(PS[2][0:64, 0:256], aT[:, 0:64], wg[:, :], start=True, stop=False)
                    S.mm(PS[2][0:64, 0:256], onesr[:, 0:64], bg[:, :], start=False, stop=True)
                    S.act(e1[0:64, :], PS[2][0:64, 0:256], AF.Exp, scale=-1.0)
                    S.act(la[0:64, :], e1[0:64, :], AF.Ln, bias=1.0)
                    S.mm(PS[2][0:64, 256:512], csm[:, 64:128], la[0:64, :])
                    for h in range(4):
                        S.mm(PS[3][0:64, h * 128:h * 128 + 64], la[0:64, h * 64:(h + 1) * 64], csm[:, 0:64])
                    S.act(eD[0:64, :], PS[2][0:64, 256:512], AF.Exp)
                    S.tt(kl[0:64, :], PS[1][0:64, :256], eD[0:64, :], ALU.mult)
                    ps3s = V(PS[3], PS[3].t[0:64, :].rearrange("p (h t) -> p h t", h=4)[:, :, 0:64])
                    S.act(eB[:, :, 0:64], ps3s, AF.Exp)
                    S.act(enB[:, :, 0:64], ps3s, AF.Exp, scale=-1.0)
                    S.tt(qe[:, :, 0:64], qk[:, 0:4, 0:64], eB[:, :, 0:64], ALU.mult)
                    S.tt(ke[:, :, 0:64], qk[:, 4:8, 0:64], enB[:, :, 0:64], ALU.mult)
                    for h in range(4):
                        S.mm(PS[3][0:64, h * 128 + 64:h * 128 + 128], ke[:, h, 0:64], qe[:, h, 0:64])
                    for h in range(4):
                        S.tt(att[0:64, h, 0:64], PS[3][0:64, h * 128 + 64:h * 128 + 128], csm[:, 128:192], ALU.mult)
                    OB = [PS[0], PS[4], PS[5], PS[7]]
                    for h in range(4):
                        S.mm(OB[h][:, 0:64], vsb[0:64, h * 128:(h + 1) * 128], att[0:64, h, 0:64], start=True, stop=False)
                    for b in range(16):
                        sf, sh, so = S0f[b % 2], S0h[b % 2], Sout[b % 2]
                        S.dma(sf[:, :, :], V(gla_s0, gla_s0.t[b].rearrange("h d v -> d h v")))
                        S.copy(sh[:, :, :], sf[:, :, :], eng="gpsimd")
                        for h in range(4):
                            S.mm(OB[h][:, 4 * b:4 * b + 4], sh[:, h, :], qe[:, h, 4 * b:4 * b + 4], start=False, stop=(b == 15))
                        S.ts(klm[:, :], kl[0:64, :], csm[:, 192 + b:193 + b], None, ALU.mult)
                        for h in range(4):
                            S.mm(PS[6][0:64, h * 128:(h + 1) * 128], klm[:, h * 64:(h + 1) * 64], vsb[0:64, h * 128:(h + 1) * 128])
                        for h in range(4):
                            S.stt(so[:, h, :], sf[:, h, :], eB[:, h, 4 * b + 3:4 * b + 4], PS[6][0:64, h * 128:(h + 1) * 128],
                                  ALU.mult, ALU.add)
                        S.dma(V(o_gla_s, o_gla_s.t[b].rearrange("h d v -> d h v")), so[:, :, :])
                    for h in range(4):
                        S.act(osq[:, h * 64:(h + 1) * 64], OB[h][:, 0:64], AF.Square)
                    S.mm(PS[3][:, 0:256], ones_b[:, :], osq[:, 0:256])
                    S.rsqrt(orstd[:, 0:256], PS[3][:, 0:256], 1.0 / 128, EPS)
                    for h in range(4):
                        S.tt(og[:, h, 0:64], OB[h][:, 0:64], orstd[:, h * 64:(h + 1) * 64], ALU.mult)
                        S.stt(actT[:, h, 0:64], og[:, h, 0:64], vec[:, V_GOUT + h:V_GOUT + h + 1], rs[:, h, 0:64],
                              ALU.mult, ALU.mult)
                    S.dma(hst[:, :, :, :], convhT[:, :, :, :])
                    S.copy(gps[:, :, :, 0:30], hst[:, :, :, :], eng="gpsimd")
                    for m in range(4):
                        proj(PS[1][:, :T], 1552 + 512 + m * 128, 128)
                        S.act(sg[:, :T], PS[1][:, :T], AF.Sigmoid)
                        proj(PS[2][:, :T], 1552 + m * 128, 128)
                        S.tt(gps[:, m, :, 30:34], V(PS[2], PS[2].t[:, :T].rearrange("p (b t) -> p b t", t=4)),
                             V(sg, sg.t[:, :T].rearrange("p (b t) -> p b t", t=4)), ALU.mult)
                    S.copy(hst[:, :, :, :], gps[:, :, :, 4:34], eng="gpsimd")
                    S.dma(o_conv_s[:, :, :, :], hst[:, :, :, :])
                    for m in range(4):
                        cacc = V(cc, cc.t[:, m, 0:64].rearrange("p (b t) -> p b t", t=4))
                        S.ts(cacc, gps[:, m, :, 0:4], wdw[:, m, 0:1], vec[:, V_BDW + m:V_BDW + m + 1], ALU.mult, ALU.add)
                        for k in range(1, 31):
                            S.stt(cacc, gps[:, m, :, k:k + 4], wdw[:, m, k:k + 1], cacc, ALU.mult, ALU.add)
                for m in range(4):
                    S.act(csq[:, m, :T], cc[:, m, :T], AF.Square)
                for m in range(4):
                    S.mm(PS[1][:, :T], ones_f[:, :], cc[:, m, :T], start=(m == 0), stop=(m == 3))
                for m in range(4):
                    S.mm(PS[2][:, :T], ones_f[:, :], csq[:, m, :T], start=(m == 0), stop=(m == 3))
                S.ts(mean[:, :T], PS[1][:, :T], 1.0 / 512, None, ALU.mult)
                S.tt(msq[:, :T], mean[:, :T], mean[:, :T], ALU.mult)
                S.stt(crs[:, :T], PS[2][:, :T], 1.0 / 512, msq[:, :T], ALU.mult, ALU.subtract)
                S.rsqrt(crs[:, :T], crs[:, :T], 1.0, EPS)
                for m in range(4):
                    S.tt(dtmp[:, :T], cc[:, m, :T], mean[:, :T], ALU.subtract)
                    S.tt(dtmp[:, :T], dtmp[:, :T], crs[:, :T], ALU.mult)
                    S.act(actT[:, 4 + m, :T], dtmp[:, :T], AF.Silu,
                          bias=vec[:, V_BLN + m:V_BLN + m + 1], scale=vec[:, V_GLN + m:V_GLN + m + 1])

                for oc in range(8):
                    pb = PS[oc % 2]
                    for kc in range(8):
                        S.mm(pb[:, :T], Wout[:, kc, oc * 128:(oc + 1) * 128], actT[:, kc, :T],
                             start=(kc == 0), stop=(kc == 7))
                    S.tt(xT[:, oc, :T], xT[:, oc, :T], pb[:, :T], ALU.add)
                S.dma(xview(Xout, t0, T), xT[:, :, :T])

                if t0 + T == NTILE * TP:
                    S.dma(V(o_gla_p, o_gla_p.t.rearrange("h d v -> d h v")), Sf[:, :, :])
                    S.copy(tail[:, :, 0:30], glu[:, :, TP:TP + 30], eng="vector")
                    for m in range(4):
                        S.transpose(PS[3][0:30, m * 128:(m + 1) * 128], tail[:, m, 0:30], ident[:, :])
                    S.copy(tailT[0:30, :], PS[3][0:30, :], eng="vector")
                    S.dma(o_conv_p[:, :], tailT[0:30, :])
            S.barrier()

    def phase_cross(l, Xin, Xout, mem_only=False):
        with ExitStack() as pes:
            def sb(name, shape, dt):
                name = name + "_L%d" % l
                t = pes.enter_context(nc.sbuf_tensor(name, list(shape), dt))
                b = Buf(name, t, "sb")
                S.bufs.append(b)
                return b
            Wq = sb("Wq", [128, 8, 512], BF16)
            Wkv = sb("Wkv", [128, 8, 1024], BF16)
            Wo = sb("Wo", [128, 4, D], BF16)
            load_w_bf16(Wq, lambda kc, a, b: w_xq[l][kc * 128:(kc + 1) * 128, a:b], 8, 512)
            load_w_bf16(Wkv, lambda kc, a, b: w_mkv[l][kc * 128:(kc + 1) * 128, a:b], 8, 1024)
            load_w_bf16(Wo, lambda kc, a, b: w_xo[l][kc * 128:(kc + 1) * 128, a:b], 4, D)
            xT = sb("cxT", [128, 8, TP], F32)
            sq = sb("csq_", [128, 8, TP], BF16)
            xn = sb("cxn", [128, 8, TP], BF16)
            rstd = sb("crstd", [128, TP], F32)
            KT = sb("KT", [128, 4, 256], BF16)
            Vm = sb("Vm", [128, 2, 512], BF16)
            kvf = sb("kvf", [128, 1024], F32)
            qT = sb("qT", [128, 4, TP], BF16)
            PT = sb("PT", [128, 2, TP], BF16)
            rinv = sb("rinv", [128, TP], F32)
            OT = sb("OT", [128, 4, TP], BF16)
            kst = [sb("kst%d" % i, [128, 4, 256], F32) for i in range(2)]
            kbf = [sb("kbf%d" % i, [128, 4, 256], BF16) for i in range(2)]
            vst = [sb("vst%d" % i, [128, 2, 512], F32) for i in range(2)]
            vbf = [sb("vbf%d" % i, [128, 2, 512], BF16) for i in range(2)]
            pts = sb("pts", [128, 32], BF16)
            rs32 = sb("rs32", [128, 32], F32)
            ssum = sb("ssum", [128, 4, 4], F32)
            CUT = int(os.environ.get("MK_CUT", "9"))
            if CUT >= 2:
                S.dma(xT[:, :, :256], V(memT, memT.t.rearrange("(k p) t -> p k t", p=128)))
                rmsnorm_tile(xT, 256, VN_MEM + 32 * l, sq, xn, rstd, PS[7])
            for mc in range(2 if CUT >= 3 else 0):
                for half in range(2):
                    pb = PS[half]
                    for kc in range(8):
                        S.mm(pb[:, :], xn[:, kc, mc * 128:(mc + 1) * 128], Wkv[:, kc, half * 512:(half + 1) * 512],
                             start=(kc == 0), stop=(kc == 7))
                    SUBV = int(os.environ.get("MK_SUB", "3"))
                    if SUBV >= 1:
                        S.copy(kvf[:, half * 512:(half + 1) * 512], pb[:, :], eng="scalar")
                    if half == 1 and SUBV >= 2:
                        S.copy(Vm[:, mc, :], pb[:, :], eng="scalar")
                if not os.environ.get("MK_NOMEMDMA"):
                    S.dma(o_memkv[l][mc * 128:(mc + 1) * 128, :], kvf[:, :])
            for h in range(4 if CUT >= 4 else 0):
                pb = PS[2 + h % 2]
                for kc in range(8):
                    S.mm(pb[:, :256], Wkv[:, kc, h * 128:(h + 1) * 128], xn[:, kc, :256], start=(kc == 0), stop=(kc == 7))
                S.copy(KT[:, h, :], pb[:, :256], eng="scalar")

            for ti, (t0, T) in enumerate([] if mem_only else tiles):
                is_sample = (t0 >= SEQ)
                S.dma(xT[:, :, :T], xview(Xin, t0, T))
                rmsnorm_tile(xT, T, VN_X + 32 * l, sq, xn, rstd, PS[7])
                for h in range(4):
                    pb = PS[h % 2]
                    for kc in range(8):
                        S.mm(pb[:, :T], Wq[:, kc, h * 128:(h + 1) * 128], xn[:, kc, :T], start=(kc == 0), stop=(kc == 7))
                    S.copy(qT[:, h, :T], pb[:, :T], eng="scalar")
                if not is_sample:
                    for h in range(4):
                        for mc in range(2):
                            S.mm(PS[2 + mc][:, :T], KT[:, h, mc * 128:(mc + 1) * 128], qT[:, h, :T])
                            S.act(PT[:, mc, :T], PS[2 + mc][:, :T], AF.Exp, scale=128 ** -0.5)
                        for mc in range(2):
                            S.mm(PS[4][:, :T], ones_b[:, :], PT[:, mc, :T], start=(mc == 0), stop=(mc == 1))
                        for mc in range(2):
                            S.mm(PS[5][:, :T], Vm[:, mc, h * 128:(h + 1) * 128], PT[:, mc, :T], start=(mc == 0), stop=(mc == 1))
                        S.copy(rinv[:, :T], PS[4][:, :T], eng="vector")
                        S.recip(rinv[:, :T], rinv[:, :T])
                        S.tt(OT[:, h, :T], PS[5][:, :T], rinv[:, :T], ALU.mult)
                else:
                    for b in range(16):
                        ks, kb_, vs, vb_ = kst[b % 2], kbf[b % 2], vst[b % 2], vbf[b % 2]
                        S.dma(ks[:, :, :], memKT[l][b])
                        S.copy(kb_[:, :, :], ks[:, :, :], eng="gpsimd")
                        S.dma(vs[:, :, :], memV[l][b])
                        S.copy(vb_[:, :, :], vs[:, :, :], eng="gpsimd")
                        for h in range(4):
                            for mc in range(2):
                                c0 = (h * 2 + mc) * 4
                                S.mm(PS[2][:, c0:c0 + 4], kb_[:, h, mc * 128:(mc + 1) * 128], qT[:, h, 4 * b:4 * b + 4])
                        S.act(pts[:, 0:32], PS[2][:, 0:32], AF.Exp, scale=128 ** -0.5)
                        S.mm(PS[4][:, 0:32], ones_b[:, :], pts[:, 0:32])
                        S.copy(rs32[:, 0:32], PS[4][:, 0:32], eng="vector")
                        r4 = rs32.t[:, 0:32].rearrange("p (h m t) -> p h m t", h=4, m=2)
                        S.tt(ssum[:, :, :], V(rs32, r4[:, :, 0, :]), V(rs32, r4[:, :, 1, :]), ALU.add)
                        S.recip(ssum[:, :, :], ssum[:, :, :])
                        for h in range(4):
                            for mc in range(2):
                                c0 = (h * 2 + mc) * 4
                                S.mm(PS[5][:, h * 4:(h + 1) * 4], vb_[:, mc, h * 128:(h + 1) * 128], pts[:, c0:c0 + 4],
                                     start=(mc == 0), stop=(mc == 1))
                        S.tt(OT[:, :, 4 * b:4 * b + 4], V(PS[5], PS[5].t[:, 0:16].rearrange("p (h t) -> p h t", h=4)),
                             ssum[:, :, :], ALU.mult)
                for oc in range(8):
                    pb = PS[oc % 2]
                    for kc in range(4):
                        S.mm(pb[:, :T], Wo[:, kc, oc * 128:(oc + 1) * 128], OT[:, kc, :T], start=(kc == 0), stop=(kc == 3))
                    S.tt(xT[:, oc, :T], xT[:, oc, :T], pb[:, :T], ALU.add)
                S.dma(xview(Xout, t0, T), xT[:, :, :T])
            S.barrier()

    def phase_ffn(l, Xin, Xout):
        with ExitStack() as pes:
            def sb(name, shape, dt):
                name = name + "_L%d" % l
                t = pes.enter_context(nc.sbuf_tensor(name, list(shape), dt))
                b = Buf(name, t, "sb")
                S.bufs.append(b)
                return b
            TF = 256
            Wu = sb("Wu", [128, 8, 2 * DFF], BF16)
            Wd = sb("Wd", [128, 22, D], BF16)
            fv = sb("fv", [128, 44, 4], F32)
            load_w_bf16(Wu, lambda kc, a, b: w_up[l][kc * 128:(kc + 1) * 128, a:b], 8, 2 * DFF)
            load_w_bf16(Wd, lambda kc, a, b: w_down[l][kc * 128:(kc + 1) * 128, a:b], 22, D)
            S.dma(fv[:, :, :], ffnv[l][:, :, :])
            xT = sb("fxT", [128, 8, TF], F32)
            sq = sb("fsq", [128, 8, TF], BF16)
            xn = sb("fxn", [128, 8, TF], BF16)
            rstd = sb("frstd", [128, TF], F32)
            u = [sb("fu%d" % i, [128, 2 + TF], F32) for i in range(4)]
            utail = sb("utail", [128, 44, 2], F32)
            c1 = [sb("fc1%d" % i, [128, TF], F32) for i in range(2)]
            c2 = [sb("fc2%d" % i, [128, TF], F32) for i in range(2)]
            hT = sb("hT", [128, 22, TF], BF16)
            tT = sb("tT", [44, 2, 128], F32)
            S.memset(utail[:, :, :], 0.0)
            fh = sb("fh", [128, 44, 16, 2], F32)
            fo = sb("fo", [128, 44, 16, 2], F32)
            S.dma(fh[:, :, :, :], ffnhT[l][:, :, :, :])
            ftiles = []
            for (t0, T) in tiles:
                for s0 in range(0, T, TF):
                    ftiles.append((t0 + s0, min(TF, T - s0)))
            for ti, (t0, T) in enumerate(ftiles):
                is_sample = (t0 >= SEQ)
                S.dma(xT[:, :, :T], xview(Xin, t0, T))
                rmsnorm_tile(xT, T, VN_FFN + 32 * l, sq, xn, rstd, PS[7])
                if True:

                    def conv_chunk(j, ub, pb, cout, eng):
                        for kc in range(8):
                            S.mm(pb[:, :T], Wu[:, kc, j * 128:(j + 1) * 128], xn[:, kc, :T], start=(kc == 0), stop=(kc == 7))
                        if not is_sample:
                            S.copy(ub[:, 0:2], utail[:, j, :], eng="gpsimd")
                            S.copy(ub[:, 2:2 + T], pb[:, :T], eng="scalar")
                            S.copy(utail[:, j, :], ub[:, T:T + 2], eng="gpsimd")
                            S.ts(cout[:, :T], ub[:, 0:T], fv[:, j, 0:1], fv[:, j, 3:4], ALU.mult, ALU.add, eng=eng)
                            S.stt(cout[:, :T], ub[:, 1:T + 1], fv[:, j, 1:2], cout[:, :T], ALU.mult, ALU.add, eng=eng)
                            S.stt(cout[:, :T], ub[:, 2:T + 2], fv[:, j, 2:3], cout[:, :T], ALU.mult, ALU.add, eng=eng)
                        else:
                            u3 = ub.t[:, 0:96].rearrange("p (b t) -> p b t", t=6)
                            c3 = V(cout, cout.t[:, 0:64].rearrange("p (b t) -> p b t", t=4))
                            S.copy(V(ub, u3[:, :, 0:2]), fh[:, j, :, :], eng="gpsimd")
                            S.copy(V(ub, u3[:, :, 2:6]), V(pb, pb.t[:, 0:64].rearrange("p (b t) -> p b t", t=4)), eng="scalar")
                            S.copy(fo[:, j, :, :], V(ub, u3[:, :, 4:6]), eng="gpsimd")
                            S.ts(c3, V(ub, u3[:, :, 0:4]), fv[:, j, 0:1], fv[:, j, 3:4], ALU.mult, ALU.add)
                            S.stt(c3, V(ub, u3[:, :, 1:5]), fv[:, j, 1:2], c3, ALU.mult, ALU.add)
                            S.stt(c3, V(ub, u3[:, :, 2:6]), fv[:, j, 2:3], c3, ALU.mult, ALU.add)
                    for j in range(22):
                        conv_chunk(j, u[(2 * j) % 4], PS[(2 * j) % 4], c1[j % 2], "vector")
                        conv_chunk(22 + j, u[(2 * j + 1) % 4], PS[(2 * j + 1) % 4], c2[j % 2], "gpsimd")
                        S.act(c1[j % 2][:, :T], c1[j % 2][:, :T], AF.Silu)
                        S.tt(hT[:, j, :T], c1[j % 2][:, :T], c2[j % 2][:, :T], ALU.mult)
                    for oc in range(8):
                        pb = PS[4 + oc % 2]
                        for j in range(22):
                            S.mm(pb[:, :T], Wd[:, j, oc * 128:(oc + 1) * 128], hT[:, j, :T], start=(j == 0), stop=(j == 21))
                        S.tt(xT[:, oc, :T], xT[:, oc, :T], pb[:, :T], ALU.add)
                S.dma(xview(Xout, t0, T), xT[:, :, :T])
                if is_sample:
                    S.dma(o_ffn_s[l][:, :, :, :], fo[:, :, :, :])
                if t0 + T == NTILE * TP:
                    for r in range(2):
                        S.transpose(PS[6][0:44, r * 128:(r + 1) * 128], utail[:, :, r], ident[:, :])
                    S.copy(V(tT, tT.t[:, :, :].rearrange("j r p -> j (r p)")), PS[6][0:44, 0:256], eng="vector")
                    S.dma(V(o_ffn_p[l], o_ffn_p[l].t.rearrange("r (j p) -> j r p", p=128)), tT[:, :, :])
            S.barrier()

    def phase_nsa_kv(Xin):
        with ExitStack() as pes:
            def sb(name, shape, dt):
                t = pes.enter_context(nc.sbuf_tensor(name, list(shape), dt))
                b = Buf(name, t, "sb")
                S.bufs.append(b)
                return b
            Wc = sb("Wc", [128, 8, 1536], BF16)
            load_w_bf16(Wc, lambda kc, a, b: w_in_c[kc * 128:(kc + 1) * 128, 1024 + a:1024 + b], 8, 1536)
            xT = sb("nxT", [128, 8, TP], F32)
            sq = sb("nsq", [128, 8, TP], BF16)
            xn = sb("nxn", [128, 8, TP], BF16)
            rstd = sb("nrstd", [128, TP], F32)
            kvs = [sb("kvs%d" % i, [128, 1536], F32) for i in range(2)]
            n = 0
            for ti, (t0, T) in enumerate(tiles):
                S.dma(xT[:, :, :T], xview(Xin, t0, T))
                rmsnorm_tile(xT, T, VN_MIX + 32, sq, xn, rstd, PS[7])
                if t0 >= SEQ:
                    kb = kvs[n % 2]
                    n += 1
                    for g3 in range(3):
                        pb = PS[g3]
                        for kc in range(8):
                            S.mm(pb[0:64, :], xn[:, kc, 0:64], Wc[:, kc, g3 * 512:(g3 + 1) * 512], start=(kc == 0), stop=(kc == 7))
                        S.copy(kb[0:64, g3 * 512:(g3 + 1) * 512], pb[0:64, :], eng="scalar")
                    S.dma(o_nsa_kv_s[:, :], kb[0:64, 0:1024])
                    for b in range(16):
                        S.dma(o_win_s[b, 508:512, :], kb[4 * b:4 * b + 4, 1024:1536])
                        S.dma(o_win_s[b, 0:508, :], win_cache[b, 4:512, :])
                    continue
                for c in range(T // 128):
                    cs = slice(c * 128, (c + 1) * 128)
                    tok = t0 + c * 128
                    inwin = tok >= NTILE * TP - 512
                    kb = kvs[n % 2]
                    n += 1
                    for g3 in range(3 if inwin else 2):
                        pb = PS[g3]
                        for kc in range(8):
                            S.mm(pb[:, :], xn[:, kc, cs], Wc[:, kc, g3 * 512:(g3 + 1) * 512], start=(kc == 0), stop=(kc == 7))
                        S.copy(kb[:, g3 * 512:(g3 + 1) * 512], pb[:, :], eng=("scalar" if g3 % 2 == 0 else "vector"))
                    S.dma(o_nsa_kv[tok:tok + 128, :], kb[:, 0:1024])
                    if inwin:
                        w0 = tok - (NTILE * TP - 512)
                        S.dma(o_win_p[w0:w0 + 128, :], kb[:, 1024:1536])
            S.barrier()

    def phase_nsa(Xin, Xout):
        TQ = 128
        NQ = NTILE * TP // TQ
        NEG = -30000.0
        with ExitStack() as pes:
            def sb(name, shape, dt):
                t = pes.enter_context(nc.sbuf_tensor(name, list(shape), dt))
                b = Buf(name, t, "sb")
                S.bufs.append(b)
                return b
            Wc = sb("NWc", [128, 8, 1584], BF16)
            for r0_ in range(0, D, 128):
                S.dma(WcD[r0_:r0_ + 128, 0:2048], w_in_c[r0_:r0_ + 128, 0:2048], eng="gpsimd")
                S.dma(WcD[r0_:r0_ + 128, 2048:2608], w_in_c[r0_:r0_ + 128, 2048:2608], eng="gpsimd")
            S.dma(WoD[:, :], w_out_c[:, :], eng="gpsimd")
            S.barrier()
            wcv = WcD.t.rearrange("(k p) n -> p k n", p=128)
            WoS = [sb("WoS%d" % i, [64, 16, 128], BF16) for i in range(2)]
            W1 = sb("NW1", [64, 2, 32, 64], BF16)
            W2 = sb("NW2", [64, 2, 64], BF16)
            PEf = sb("PEf", [64, 2, 32], F32)
            PEb = sb("PEb", [64, 2, 32], BF16)
            cb = sb("ncb", [64, 2], F32)
            for kd in range(2):
                for l0 in range(0, 32, 8):
                    S.dma(W1[:, kd, l0:l0 + 8, :], V(w_cmp1, w_cmp1.t[kd, l0:l0 + 8].rearrange("l d e -> d l e")), eng="gpsimd")
                S.dma(W2[:, kd, :], w_cmp2[kd], eng="gpsimd")
            S.dma(PEf[:, :, :], pe_T[:, :, :])
            S.copy(PEb[:, :, :], PEf[:, :, :], eng="gpsimd")
            for kd in range(2):
                for l in range(32):
                    S.mm(PS[6][0:64, kd:kd + 1], W1[:, kd, l, :], PEb[:, kd, l:l + 1], start=(l == 0), stop=(l == 31))
            S.copy(cb[:, :], PS[6][0:64, 0:2], eng="vector")
            KS = sb("KS", [128, 4, SEQ], BF16)
            for g in range(4):
                S.dma(KS[64:128, g, :], n_ind[:, :])
            VS = sb("VS", [128, SEQ // 128, 4, 64], BF16)
            KW = sb("KW", [64, 4, 5 * 128], BF16)
            VW = sb("VW", [128, 5, 4, 64], BF16)
            KC = sb("KC", [64, 4, 512], BF16)
            VCb = sb("VCb", [128, 4, 4, 64], BF16)
            S.memset(KC[:, :, :], 0.0)
            S.memset(VCb[:, :, :, :], 0.0)
            TRIB = sb("TRIB", [128, 2, 512], BF16)
            CMB = sb("CMB", [128, 1016], BF16)
            CMT = sb("CMT", [128, 17, 128], BF16)
            FBB = sb("FBB", [128, 254], F32)
            SEL = sb("SEL", [48, 48 * 64], BF16)
            S.dma(TRIB[:, :, :], n_trib[:, :, :])
            S.dma(CMB[:, :], n_cmbig[:, :])
            S.dma(CMT[:, :, :], n_cmt[:, :, :])
            S.dma(FBB[:, :], n_fbbig[:, :])
            S.dma(SEL[:, :], n_sel[:, :])
            xT = sb("qxT", [128, 8, TQ], F32)
            sq = sb("qsq", [128, 8, TQ], BF16)
            xn = sb("qxn", [128, 8, TQ], BF16)
            rstd = sb("qrstd", [128, TQ], F32)
            QT = sb("QT", [64, 16, TQ], BF16)
            QA = [sb("QA%d" % i, [128, 4, TQ], BF16) for i in range(2)]
            CR = sb("CR", [64, 2, 4, 9, 16], BF16)
            hT = sb("nhT", [64, 2, 4, 8], BF16)
            hpad = sb("hpad", [64, 4, 128], BF16)
            GT = sb("GT", [48, TQ], BF16)
            EA = sb("EA", [128, 512], F32)
            Pg = sb("Pg", [128, 512], F32)
            rsum = sb("rsum", [128, 4], F32)
            imp = sb("imp", [128, 128], F32)
            sc2 = sb("sc2", [128, 128], F32)
            m8 = sb("m8", [128, 16], F32)
            sbias = sb("sbias", [128, 128], F32)
            sbb = sb("sbb", [128, 128], BF16)
            ET = [sb("ET%d" % i, [128, 512], BF16) for i in range(2)]
            rden = sb("nrden", [64, 512], F32)
            acc = sb("nacc", [64, 512], F32)
            OA = sb("OA", [64, 16, TQ], BF16)
            S.memset(CR[:, :, :, :, :], 0.0)
            etn = [0]

            def attend(qrhs_fn, chunks, acc_first, g, br, W=512, gt=None):
                n = len(chunks)
                gt = GT[:, :] if gt is None else gt
                for ci, chk in enumerate(chunks):
                    kl_, vl_, bias, qsel, zr0, addb = chk[:6]
                    rows = chk[6] if len(chk) > 6 else 128
                    pb = PS[2 + ci % 2]
                    S.mm(pb[0:rows, 0:W], kl_, qrhs_fn(qsel), start=True, stop=(bias is None))
                    if bias is not None:
                        S.mm(pb[0:rows, 0:W], identb[0:rows, 0:rows], bias, start=False, stop=True)
                    et = ET[etn[0] % 2]
                    etn[0] += 1
                    if addb is not None:
                        for r_ in range(4):
                            S.tt(EA[:, r_ * 128:(r_ + 1) * 128], pb[:, r_ * 128:(r_ + 1) * 128], addb, ALU.add)
                        S.act(et[:, :], EA[:, :], AF.Exp)
                    else:
                        S.act(et[0:rows, 0:W], pb[0:rows, 0:W], AF.Exp)
                    if zr0:
                        S.memset(et[0:1, 0:W], 0.0, eng="vector")
                    S.mm(PS[4][0:64, 0:W], ones_b[0:rows, 0:64], et[0:rows, 0:W], start=(ci == 0), stop=(ci == n - 1))
                    S.mm(PS[5][0:64, 0:W], vl_, et[0:rows, 0:W], start=(ci == 0), stop=(ci == n - 1))
                S.ts(rden[:, 0:W], PS[4][0:64, 0:W], 1e-30, None, ALU.max)
                S.recip(rden[:, 0:W], rden[:, 0:W])
                S.tt(rden[:, 0:W], PS[5][0:64, 0:W], rden[:, 0:W], ALU.mult)
                wq = W // 4
                for r in range(4):
                    col = (g * 4 + r) * 3 + br
                    S.mm(PS[6][0:64, r * wq:(r + 1) * wq], SEL[:, col * 64:(col + 1) * 64], gt)
                if acc_first:
                    S.tt(acc[:, 0:W], rden[:, 0:W], PS[6][0:64, 0:W], ALU.mult)
                else:
                    S.tt(rden[:, 0:W], rden[:, 0:W], PS[6][0:64, 0:W], ALU.mult)
                    S.tt(acc[:, 0:W], acc[:, 0:W], rden[:, 0:W], ALU.add)

            def compress_tile(i):
                for kd in range(2):
                    for g in range(4):
                        for l in range(32):
                            j0 = 0 if l < 16 else 1
                            S.mm(PS[6][0:64, (kd * 4 + g) * 8:(kd * 4 + g) * 8 + 8], W1[:, kd, l, :],
                                 CR[:, kd, g, j0:j0 + 8, l % 16], start=(l == 0), stop=(l == 31))
                    S.act(V(hT, hT.t[:, kd, :, :].rearrange("p g j -> p (g j)")), PS[6][0:64, kd * 32:kd * 32 + 32], AF.Silu,
                          bias=cb[:, kd:kd + 1])
                for g in range(4):
                    S.mm(PS[7][0:64, g * 8:g * 8 + 8], W2[:, 0, :], hT[:, 0, g, :])
                S.copy(KC[:, :, 8 * i:8 * i + 8], V(PS[7], PS[7].t[0:64, 0:32].rearrange("p (g j) -> p g j", g=4)), eng="scalar")
                r0 = (8 * i) % 128
                S.memset(hpad[:, :, :], 0.0)
                S.copy(hpad[:, :, r0:r0 + 8], hT[:, 1, :, :], eng="gpsimd")
                for g in range(4):
                    S.mm(PS[7][:, 64 + g * 64:128 + g * 64], hpad[:, g, :], W2[:, 1, :])
                vcv = V(VCb, VCb.t[:, i // 16, :, :].rearrange("p g d -> p (g d)"))
                S.tt(vcv, vcv, PS[7][:, 64:320], ALU.add)

            for i in range(NQ):
                t0 = i * TQ
                slot = i % 5
                S.dma(xT[:, :, :], xview(Xin, t0, TQ))
                rmsnorm_tile(xT, TQ, VN_MIX + 32, sq, xn, rstd, PS[7])

                woff = [0]

                def pj(psv, col0, M):
                    c0_ = col0 - woff[0]
                    for kc in range(8):
                        S.mm(psv, Wc[:, kc, c0_:c0_ + M], xn[:, kc, :], start=(kc == 0), stop=(kc == 7))
                S.dma(Wc[:, :, 0:1024], V(WcD, wcv[:, :, 0:1024]))
                for g in range(4):
                    pb = PS[g % 2]
                    for r in range(4):
                        pj(pb[0:64, r * 128:(r + 1) * 128], (g * 4 + r) * 64, 64)
                    S.act(V(QT, QT.t[:, 4 * g:4 * g + 4, :].rearrange("p h q -> p (h q)")), pb[0:64, :], AF.Copy, scale=0.125)
                S.dma(Wc[:, :, 0:1584], V(WcD, wcv[:, :, 1024:2608]))
                woff[0] = 1024
                for g in range(4):
                    pj(PS[0][0:64, g * 128:(g + 1) * 128], 1536 + g * 64, 64)
                S.copy(KS[0:64, :, t0:t0 + TQ], V(PS[0], PS[0].t[0:64, :].rearrange("p (g t) -> p g t", g=4)), eng="scalar")
                for g in range(4):
                    pj(PS[1][0:64, g * 128:(g + 1) * 128], 2048 + g * 64, 64)
                S.copy(KW[:, :, slot * 128:(slot + 1) * 128], V(PS[1], PS[1].t[0:64, :].rearrange("p (g t) -> p g t", g=4)), eng="scalar")
                if i > 0:
                    S.copy(CR[:, :, :, 0, :], CR[:, :, :, 8, :], eng="gpsimd")
                for kd in range(2):
                    pb = PS[kd]
                    for g in range(4):
                        pj(pb[0:64, g * 128:(g + 1) * 128], 1024 + kd * 256 + g * 64, 64)
                    S.copy(CR[:, kd, :, 1:9, :], V(pb, pb.t[0:64, :].rearrange("p (g j l) -> p g j l", g=4, j=8)), eng="scalar")
                for kc in range(8):
                    S.mm(PS[0][:, 0:256], xn[:, kc, :], Wc[:, kc, 768:1024], start=(kc == 0), stop=(kc == 7))
                for kc in range(8):
                    S.mm(PS[0][:, 256:512], xn[:, kc, :], Wc[:, kc, 1280:1536], start=(kc == 0), stop=(kc == 7))
                S.copy(V(VS, VS.t[:, i, :, :].rearrange("p g d -> p (g d)")), PS[0][:, 0:256], eng="scalar")
                S.copy(V(VW, VW.t[:, slot, :, :].rearrange("p g d -> p (g d)")), PS[0][:, 256:512], eng="scalar")
                pj(PS[1][0:48, 0:TQ], 2560, 48)
                S.act(GT[:, :], PS[1][0:48, 0:TQ], AF.Sigmoid)
                compress_tile(i)

                nch = (8 * i + 7) // 128 + 1
                for g in range(4):
                    for r in range(4):
                        pb = PS[r % 2]
                        S.mm(pb[:, :], QT[:, 4 * g + r, :], KC[:, g, :])
                        S.tt(EA[:, :], pb[:, :], CMB[:, 504 - 8 * i:504 - 8 * i + 512], ALU.add)
                        S.memset(EA[:, 0:1], NEG, eng="vector")
                        S.act(EA[:, :], EA[:, :], AF.Exp, accum=rsum[:, r:r + 1])
                        S.ts(rsum[:, r:r + 1], rsum[:, r:r + 1], 1e-30, None, ALU.max)
                        S.recip(rsum[:, r:r + 1], rsum[:, r:r + 1])
                        if r == 0:
                            S.ts(Pg[:, :], EA[:, :], rsum[:, 0:1], None, ALU.mult)
                        else:
                            S.stt(Pg[:, :], EA[:, :], rsum[:, r:r + 1], Pg[:, :], ALU.mult, ALU.add)
                    pgv = Pg.t[:, :].rearrange("p (j f) -> p j f", f=4)
                    S.op("vector", (lambda oa, ia: (lambda e: e.tensor_reduce(oa, ia, AX.X, ALU.add)))(imp.t[:, :], pgv),
                         reads=[Pg], writes=[imp])
                    S.tt(imp[:, 0:127], imp[:, 0:127], V(Pg, pgv[:, 1:128, 0]), ALU.add)
                    S.tt(imp[:, :], imp[:, :], FBB[:, 126 - 2 * i:126 - 2 * i + 128], ALU.add)
                    S.memset(imp[:, 0:1], 1e6, eng="vector")
                    S.op("vector", (lambda oa, ia: (lambda e: e.max(oa, ia)))(m8.t[:, 0:8], imp.t[:, :]), reads=[imp], writes=[m8])
                    S.op("vector", (lambda oa, a1, a2: (lambda e: e.match_replace(oa, a1, a2, -1e9)))(sc2.t[:, :], m8.t[:, 0:8], imp.t[:, :]),
                         reads=[imp, m8], writes=[sc2])
                    S.op("vector", (lambda oa, ia: (lambda e: e.max(oa, ia)))(m8.t[:, 8:16], sc2.t[:, :]), reads=[sc2], writes=[m8])
                    S.ts(sbias[:, :], imp[:, :], m8[:, 15:16], None, ALU.is_ge)
                    S.ts(sbias[:, :], sbias[:, :], -NEG, NEG, ALU.mult, ALU.add)
                    S.copy(sbb[:, :], sbias[:, :], eng="gpsimd")
                    for hf in range(2):
                        S.mm(PS[6][64:128, hf * 128:(hf + 1) * 128], sbb[:, hf * 64:(hf + 1) * 64], identb[:, :])
                    for hf in range(2):
                        S.copy(QA[hf][0:64, :, :], QT[:, 4 * g:4 * g + 4, :], eng="gpsimd")
                        for r in range(4):
                            S.copy(QA[hf][64:128, r, :], PS[6][64:128, hf * 128:(hf + 1) * 128], eng="scalar")
                    qg = V(QT, QT.t[:, 4 * g:4 * g + 4, :].rearrange("p h q -> p (h q)"))

                    ch = []
                    for m in range(nch):
                        dl = i - 16 * m
                        addb = CMT[:, dl, :] if dl <= 16 else None
                        ch.append((KC[:, g, m * 128:(m + 1) * 128], VCb[:, m, g, :], None, 0, (m == 0), addb))
                    attend(lambda q_: qg, ch, True, g, 0)
                    ch = []
                    for c in range(i + 1):
                        bias = V(TRIB, TRIB.t[:, 0, :]) if c == i else None
                        ch.append((KS[:, g, c * 128:(c + 1) * 128], VS[:, c, g, :], bias, (0 if c < 32 else 1), False, None))
                    attend(lambda q_: V(QA[q_], QA[q_].t[:, :, :].rearrange("p h q -> p (h q)")), ch, False, g, 1)
                    ch = []
                    for c in range(max(0, i - 4), i + 1):
                        bias = None
                        if c == i:
                            bias = V(TRIB, TRIB.t[:, 0, :])
                        elif c == i - 4:
                            bias = V(TRIB, TRIB.t[:, 1, :])
                        sl = c % 5
                        ch.append((KW[:, g, sl * 128:(sl + 1) * 128], VW[:, sl, g, :], bias, 0, False, None))
                    attend(lambda q_: qg, ch, False, g, 2)
                    S.copy(V(OA, OA.t[:, 4 * g:4 * g + 4, :].rearrange("p h q -> p (h q)")), acc[:, :], eng="gpsimd")
                for oc in range(8):
                    ws = WoS[oc % 2]
                    S.dma(ws[:, :, :], V(WoD, WoD.t.rearrange("(h d) n -> d h n", d=64)[:, :, oc * 128:(oc + 1) * 128]))
                    pb = PS[oc % 2]
                    for h in range(16):
                        S.mm(pb[:, 0:TQ], ws[:, h, :], OA[:, h, :], start=(h == 0), stop=(h == 15))
                    S.tt(xT[:, oc, :], xT[:, oc, :], pb[:, 0:TQ], ALU.add)
                S.dma(xview(Xout, t0, TQ), xT[:, :, :])
            if tiles[-1][0] >= SEQ:
                nsmp = sb("nsmp", [128, 128], F32)
                nsb = sb("nsb", [128, 32], BF16)
                S.dma(nsmp[:, :], n_smp[:, :])
                S.dma(nsb[:, :], n_smpb[:, :])
                KNs = sb("KNs", [64, 4, ST], BF16)
                WNs = sb("WNs", [64, 4, ST], BF16)
                GTs = sb("GTs", [48, ST], BF16)
                Vnb = sb("Vnb", [4, 512], BF16)
                ptb = sb("ptb", [128, 16], I32)
                ptf = sb("ptf", [128, 16], F32)
                ptf2 = sb("ptf2", [128, 16], F32)
                idxF = sb("idxF", [128, 16], I32)
                idxV = sb("idxV", [128, 16], I32)
                QAb = sb("QAb", [128, 4, 4], BF16)
                S.dma(xT[:, :, 0:ST], xview(Xin, SEQ, ST))
                rmsnorm_tile(xT, ST, VN_MIX + 32, sq, xn, rstd, PS[7])

                def pjS(psv, c0_, M):
                    for kc in range(8):
                        S.mm(psv, Wc[:, kc, c0_:c0_ + M], xn[:, kc, 0:ST], start=(kc == 0), stop=(kc == 7))
                S.dma(Wc[:, :, 0:1024], V(WcD, wcv[:, :, 0:1024]))
                for g in range(4):
                    pb = PS[g % 2]
                    for r in range(4):
                        pjS(pb[0:64, r * 64:(r + 1) * 64], (g * 4 + r) * 64, 64)
                    S.act(QT[:, 4 * g:4 * g + 4, 0:ST], V(pb, pb.t[0:64, 0:256].rearrange("p (h q) -> p h q", h=4)), AF.Copy, scale=0.125)
                S.dma(Wc[:, :, 0:1584], V(WcD, wcv[:, :, 1024:2608]))
                for g in range(4):
                    pjS(PS[0][0:64, g * 64:(g + 1) * 64], 512 + g * 64, 64)
                S.copy(KNs[:, :, :], V(PS[0], PS[0].t[0:64, 0:256].rearrange("p (g t) -> p g t", g=4)), eng="scalar")
                for g in range(4):
                    pjS(PS[1][0:64, g * 64:(g + 1) * 64], 1024 + g * 64, 64)
                S.copy(WNs[:, :, :], V(PS[1], PS[1].t[0:64, 0:256].rearrange("p (g t) -> p g t", g=4)), eng="scalar")
                pjS(PS[0][0:48, 256:256 + ST], 1536, 48)
                S.act(GTs[:, :], PS[0][0:48, 256:256 + ST], AF.Sigmoid)
                tri4 = nsb[0:4, 16:32]
                wbias = nsb[:, 0:16]
                for b in range(16):
                    bs = slice(4 * b, 4 * b + 4)
                    S.dma(ptb[:, :], pt_bc[b])
                    S.copy(ptf[:, :], ptb[:, :], eng="vector")
                    S.ts(ptf2[:, :], ptf[:, :], 64.0, nsmp[:, 0:1], ALU.mult, ALU.add)
                    S.copy(idxF[:, :], ptf2[:, :], eng="vector")
                    S.ts(ptf2[:, :], ptf[:, :], 128.0, nsmp[:, 0:1], ALU.mult, ALU.add)
                    S.copy(idxV[:, :], ptf2[:, :], eng="vector")
                    S.memset(VCb[:, 0, :, :], 0.0)
                    for j in range(16):
                        if j == 0:
                            S.memset(CR[:, :, :, 0, :], 0.0)
                        else:
                            S.copy(CR[:, :, :, 0, :], CR[:, :, :, 8, :], eng="gpsimd")
                        for kd3 in range(3):
                            S.idma(EA[0:64, :], poolF[kd3][:, :], idxF[0:64, j:j + 1])
                            if kd3 < 2:
                                S.copy(CR[:, kd3, :, 1:9, :], V(EA, EA.t[0:64, :].rearrange("p (g j l) -> p g j l", g=4, j=8)), eng="scalar")
                            else:
                                S.copy(KS[0:64, :, 128 * j:128 * j + 128], V(EA, EA.t[0:64, :].rearrange("p (g t) -> p g t", g=4)), eng="scalar")
                        S.idma(Pg[:, 0:256], poolV[:, :], idxV[:, j:j + 1])
                        S.copy(V(VS, VS.t[:, j, :, :].rearrange("p g d -> p (g d)")), Pg[:, 0:256], eng="scalar")
                        compress_tile(j)
                    for kc in range(8):
                        S.mm(PS[0][0:4, 0:256], xn[:, kc, bs], Wc[:, kc, 768:1024], start=(kc == 0), stop=(kc == 7))
                    for kc in range(8):
                        S.mm(PS[0][0:4, 256:512], xn[:, kc, bs], Wc[:, kc, 1280:1536], start=(kc == 0), stop=(kc == 7))
                    S.copy(Vnb[:, :], PS[0][0:4, :], eng="scalar")
                    S.dma(KW[:, :, 0:512], winKT[b], eng="gpsimd")
                    S.dma(V(VW, VW.t[:, 0:4, :, :].rearrange("p c g d -> p c (g d)")), winVs[b], eng="gpsimd")
                    gtb = GTs[:, bs]
                    for g in range(4):
                        for r in range(4):
                            pb = PS[r % 2]
                            S.mm(pb[0:4, 0:128], QT[:, 4 * g + r, bs], KC[:, g, 0:128])
                            S.ts(EA[0:4, 0:128], pb[0:4, 0:128], 1.0, None, ALU.mult)
                            S.memset(EA[0:4, 0:1], NEG, eng="vector")
                            S.act(EA[0:4, 0:128], EA[0:4, 0:128], AF.Exp, accum=rsum[0:4, r:r + 1])
                            S.ts(rsum[0:4, r:r + 1], rsum[0:4, r:r + 1], 1e-30, None, ALU.max)
                            S.recip(rsum[0:4, r:r + 1], rsum[0:4, r:r + 1])
                            if r == 0:
                                S.ts(Pg[0:4, 0:128], EA[0:4, 0:128], rsum[0:4, 0:1], None, ALU.mult)
                            else:
                                S.stt(Pg[0:4, 0:128], EA[0:4, 0:128], rsum[0:4, r:r + 1], Pg[0:4, 0:128], ALU.mult, ALU.add)
                        pgs = Pg.t[0:4, 0:128].rearrange("p (j f) -> p j f", f=4)
                        S.memset(imp[0:4, 0:64], 0.0, eng="vector")
                        S.op("vector", (lambda oa, ia: (lambda e: e.tensor_reduce(oa, ia, AX.X, ALU.add)))(imp.t[0:4, 0:32], pgs),
                             reads=[Pg], writes=[imp])
                        S.tt(imp[0:4, 0:31], imp[0:4, 0:31], V(Pg, pgs[:, 1:32, 0]), ALU.add)
                        S.tt(imp[0:4, 0:64], imp[0:4, 0:64], nsmp[0:4, 64:128], ALU.add)
                        S.op("vector", (lambda oa, ia: (lambda e: e.max(oa, ia)))(m8.t[0:4, 0:8], imp.t[0:4, 0:64]), reads=[imp], writes=[m8])
                        S.op("vector", (lambda oa, a1, a2: (lambda e: e.match_replace(oa, a1, a2, -1e9)))(sc2.t[0:4, 0:64], m8.t[0:4, 0:8], imp.t[0:4, 0:64]),
                             reads=[imp, m8], writes=[sc2])
                        S.op("vector", (lambda oa, ia: (lambda e: e.max(oa, ia)))(m8.t[0:4, 8:16], sc2.t[0:4, 0:64]), reads=[sc2], writes=[m8])
                        S.ts(sbias[0:4, 0:64], imp[0:4, 0:64], m8[0:4, 15:16], None, ALU.is_ge)
                        S.ts(sbias[0:4, 0:64], sbias[0:4, 0:64], -NEG, NEG, ALU.mult, ALU.add)
                        S.copy(sbb[0:4, 0:64], sbias[0:4, 0:64], eng="gpsimd")
                        S.mm(PS[6][64:128, 0:4], sbb[0:4, 0:64], identb[0:4, 0:4])
                        S.copy(QAb[0:64, :, :], QT[:, 4 * g:4 * g + 4, bs], eng="gpsimd")
                        for r in range(4):
                            S.copy(QAb[64:128, r, :], PS[6][64:128, 0:4], eng="scalar")
                        qgb = V(QAb, QAb.t[0:64, :, :].rearrange("p r t -> p (r t)"))
                        qab = V(QAb, QAb.t[:, :, :].rearrange("p r t -> p (r t)"))
                        attend(lambda q_: qgb, [(KC[:, g, 0:128], VCb[:, 0, g, :], None, 0, True, None, 128)], True, g, 0, W=16, gt=gtb)
                        ch = [(KS[:, g, c * 128:(c + 1) * 128], VS[:, c, g, :], None, 1, False, None, 128) for c in range(16)]
                        ch.append((KNs[:, g, bs], Vnb[0:4, g * 64:(g + 1) * 64], tri4, 0, False, None, 4))
                        attend(lambda q_: (qab if q_ == 1 else qgb), ch, False, g, 1, W=16, gt=gtb)
                        ch = [(KW[:, g, c * 128:(c + 1) * 128], VW[:, c, g, :], (wbias if c == 0 else None), 0, False, None, 128)
                              for c in range(4)]
                        ch.append((WNs[:, g, bs], Vnb[0:4, 256 + g * 64:256 + (g + 1) * 64], tri4, 0, False, None, 4))
                        attend(lambda q_: qgb, ch, False, g, 2, W=16, gt=gtb)
                        S.copy(OA[:, 4 * g:4 * g + 4, bs], V(acc, acc.t[:, 0:16].rearrange("p (r t) -> p r t", r=4)), eng="gpsimd")
                for oc in range(8):
                    ws = WoS[oc % 2]
                    S.dma(ws[:, :, :], V(WoD, WoD.t.rearrange("(h d) n -> d h n", d=64)[:, :, oc * 128:(oc + 1) * 128]))
                    pb = PS[oc % 2]
                    for h in range(16):
                        S.mm(pb[:, 0:ST], ws[:, h, :], OA[:, h, 0:ST], start=(h == 0), stop=(h == 15))
                    S.tt(xT[:, oc, 0:ST], xT[:, oc, 0:ST], pb[:, 0:ST], ALU.add)
                S.dma(xview(Xout, SEQ, ST), xT[:, :, 0:ST])
            S.barrier()

    def phase_final(Xin):
        with ExitStack() as pes:
            def sb(name, shape, dt):
                t = pes.enter_context(nc.sbuf_tensor(name, list(shape), dt))
                b = Buf(name, t, "sb")
                S.bufs.append(b)
                return b
            xT = sb("zxT", [128, 8, TP], F32)
            sq = sb("zsq", [128, 8, TP], BF16)
            xn = sb("zxn", [128, 8, TP], BF16)
            rstd = sb("zrstd", [128, TP], F32)
            for ti, (t0, T) in enumerate(tiles):
                S.dma(xT[:, :, :T], xview(Xin, t0, T))
                rmsnorm_tile(xT, T, 80, sq, xn, rstd, PS[7])
                for kc in range(8):
                    S.stt(xT[:, kc, :T], xT[:, kc, :T], vec[:, 80 + kc:81 + kc], rstd[:, :T], ALU.mult, ALU.mult)
                S.dma(xview(o_yT, t0, T), xT[:, :, :T])
            S.barrier()

    phase_mixer_a(xT_in, XA)
    phase_cross(0, XA, XB)
    phase_ffn(0, XB, XA)
    phase_nsa_kv(XA)
    if os.environ.get("MK_NSA", "1") != "0":
        phase_nsa(XA, XB)
        if os.environ.get("MK_NSAONLY"):
            S.dma(o_yT[:, :], XB[:, :])
            S.barrier()
        else:
            phase_cross(1, XB, XA)
            phase_ffn(1, XA, XB)
            phase_final(XB)
    else:
        phase_cross(1, XA, XB)
        phase_ffn(1, XB, XA)
        phase_final(XA)
    S.emit()
    es.close()
    return nc


def kernel(**inp):
    f32 = np.float32
    g = lambda k: np.asarray(inp[k])
    nc = build()
    x_prompt = g("x_prompt")
    x_sample = g("x_sample")
    ident = np.eye(128, dtype=f32)
    jj, ii = np.meshgrid(np.arange(128), np.arange(128), indexing="ij")
    mask = (jj <= ii).astype(f32)
    triN = (-mask / 16.0).astype(f32)
    triU = (-(jj > ii).astype(f32) / 16.0).astype(f32)

    def pk(v):
        return np.asarray(v, f32).reshape(-1, 128).T

    vecs = np.zeros((128, 96), f32)
    for l in range(2):
        vecs[:, 32 * l + 0:32 * l + 8] = pk(g("norm_mix")[l])
        vecs[:, 32 * l + 8:32 * l + 16] = pk(g("norm_mem")[l])
        vecs[:, 32 * l + 16:32 * l + 24] = pk(g("norm_x")[l])
        vecs[:, 32 * l + 24:32 * l + 32] = pk(g("norm_ffn")[l])
    vecs[:, 80:88] = pk(g("norm_final"))
    vecs[:, 64:68] = pk(g("g_gla_out")[0])
    vecs[:, 68:72] = pk(g("b_dw_b")[0])
    vecs[:, 72:76] = pk(g("g_ln_b")[0])
    vecs[:, 76:80] = pk(g("b_ln_b")[0])

    w_dwT = np.ascontiguousarray(g("w_dw_b")[0].reshape(31, 4, 128).transpose(2, 1, 0)).astype(f32)
    ffnv = np.zeros((2, 128, 44, 4), f32)
    for l in range(2):
        wd = g("w_ffn_dw")[l].reshape(3, 44, 128)
        ffnv[l, :, :, 0:3] = wd.transpose(2, 1, 0)
        ffnv[l, :, :, 3] = g("b_ffn_dw")[l].reshape(44, 128).T
    xsT = np.ascontiguousarray(x_sample.reshape(128, 4, D))
    tj, ti_ = np.meshgrid(np.arange(64), np.arange(64), indexing="ij")
    same = (tj // 4 == ti_ // 4)
    c_smp = np.zeros((64, 208), f32)
    c_smp[:, 0:64] = -(same & (tj <= ti_)).astype(f32) / 16.0
    c_smp[:, 64:128] = -(same & (tj > ti_)).astype(f32) / 16.0
    c_smp[:, 128:192] = (same & (tj <= ti_)).astype(f32)
    c_smp[:, 192:208] = (np.arange(64)[:, None] // 4 == np.arange(16)[None, :]).astype(f32)
    ca = np.ascontiguousarray
    bf = ml_dtypes.bfloat16
    NEG = -30000.0
    keys = np.arange(SEQ)
    n_ind = ((keys[None, :] // 64) % 64 == np.arange(64)[:, None]).astype(bf)
    kk, qq = np.meshgrid(np.arange(128), np.arange(128), indexing="ij")
    trib = np.zeros((128, 2, 4, 128), f32)
    trib[:, 0] = np.where(kk > qq, NEG, 0.0)[:, None, :]
    trib[:, 1] = np.where(kk <= qq, NEG, 0.0)[:, None, :]
    n_trib = trib.reshape(128, 2, 512).astype(bf)
    ql = np.arange(128)[:, None]
    u = (np.arange(1016) - 504)[None, :]
    n_cmbig = np.where(16 * u + 15 <= ql, 0.0, NEG).astype(bf)
    sl_ = np.arange(128)[:, None, None]
    dl_ = np.arange(17)[None, :, None]
    qv = np.arange(128)[None, None, :]
    n_cmt = np.where(16 * sl_ + 15 - qv <= 128 * dl_, 0.0, NEG).astype(bf)
    v_ = (np.arange(254) - 126)[None, :]
    curl = (ql >= 64).astype(np.int64)
    n_fbbig = np.where((v_ == curl) | (v_ == curl - 1), 1e6, np.where(v_ > curl, -1e6, 0.0)).astype(f32)
    n_sel = (np.arange(48 * 64)[None, :] // 64 == np.arange(48)[:, None]).astype(bf)
    pe_T = ca(g("pe_cmp")[0].transpose(2, 0, 1)).astype(f32)
    pool = g("cache_nsa_kv")[0]
    poolF = [ca(pool[:, :, k].transpose(0, 3, 2, 1)).reshape(-1, 512) for k in range(3)]
    poolV = ca(pool[:, :, 3]).reshape(-1, 256)
    n_smp = np.zeros((128, 128), f32)
    n_smp[:, 0] = np.arange(128)
    n_smp[:, 64 + 0] = 1e6
    n_smp[:, 64 + 31] = 1e6
    n_smp[:, 64 + 32] = 1e6
    n_smp[:, 64 + 33:128] = -1e9
    nsb_ = np.zeros((128, 32), f32)
    tt_ = np.arange(16) % 4
    nsb_[:, 0:16] = np.where(np.arange(128)[:, None] <= tt_[None, :], NEG, 0.0)
    nsb_[0:4, 16:32] = np.where(np.arange(4)[:, None] > tt_[None, :], NEG, 0.0)
    n_smpb = nsb_.astype(bf)
    ptab = g("page_table").astype(np.int32)
    wincache = g("cache_nsa_win")[0]
    in_maps = []
    for c in range(8):
        b = c // 4
        xs = xsT[c * 16:(c + 1) * 16].reshape(64, D)
        xT = np.concatenate([x_prompt[b].T, xs.T], axis=1)
        in_maps.append({
            "xT_in": np.ascontiguousarray(xT, f32),
            "c_ident": ident, "c_triN": triN, "c_triU": triU, "c_mask": mask,
            "vecs": vecs,
            "w_in_a": g("w_in_a")[0], "w_out_a": g("w_out_a")[0],
            "w_gate": g("w_gate_a")[0], "b_gate": g("b_gate_a")[0].reshape(1, 256),
            "w_dwT": w_dwT,
            "memT": np.ascontiguousarray(g("mem_prompt")[b].T),
            "w_xq0": g("w_xq")[0], "w_mkv0": g("w_mem_kv")[0], "w_xo0": g("w_xo")[0],
            "w_xq1": g("w_xq")[1], "w_mkv1": g("w_mem_kv")[1], "w_xo1": g("w_xo")[1],
            "w_up0": g("w_up")[0], "w_down0": g("w_down")[0], "ffnv0": ffnv[0],
            "w_up1": g("w_up")[1], "w_down1": g("w_down")[1], "ffnv1": ffnv[1],
            "w_in_c": g("w_in_c")[0],
            "c_smp": c_smp,
            "poolF0": poolF[0], "poolF1": poolF[1], "poolF2": poolF[2], "poolV": poolV,
            "n_smp": n_smp, "n_smpb": n_smpb,
            "pt_bc": ca(np.broadcast_to(ptab[c * 16:(c + 1) * 16][:, None, :], (16, 128, 16))),
            "winKT": ca(wincache[c * 16:(c + 1) * 16, :, 0].transpose(0, 3, 2, 1)),
            "winVs": ca(wincache[c * 16:(c + 1) * 16, :, 1].reshape(16, 4, 128, 256).transpose(0, 2, 1, 3)),
            "w_out_c": g("w_out_c")[0], "w_cmp1": g("w_cmp1")[0], "w_cmp2": g("w_cmp2")[0], "pe_T": pe_T,
            "n_ind": n_ind, "n_trib": n_trib, "n_cmbig": n_cmbig, "n_cmt": n_cmt, "n_fbbig": n_fbbig, "n_sel": n_sel,
            "gla_s0": ca(g("cache_gla_state")[0, c * 16:(c + 1) * 16]),
            "convhT": ca(g("cache_conv")[0, c * 16:(c + 1) * 16].reshape(16, 30, 4, 128).transpose(3, 2, 0, 1)),
            "memKT0": ca(g("cache_mem_kv")[0, c * 16:(c + 1) * 16, :, 0].transpose(0, 3, 2, 1)),
            "memKT1": ca(g("cache_mem_kv")[1, c * 16:(c + 1) * 16, :, 0].transpose(0, 3, 2, 1)),
            "memV0": ca(g("cache_mem_kv")[0, c * 16:(c + 1) * 16, :, 1].reshape(16, 2, 128, 512).transpose(0, 2, 1, 3)),
            "memV1": ca(g("cache_mem_kv")[1, c * 16:(c + 1) * 16, :, 1].reshape(16, 2, 128, 512).transpose(0, 2, 1, 3)),
            "ffnhT0": ca(g("cache_ffn_conv")[0, c * 16:(c + 1) * 16].reshape(16, 2, 44, 128).transpose(3, 2, 0, 1)),
            "ffnhT1": ca(g("cache_ffn_conv")[1, c * 16:(c + 1) * 16].reshape(16, 2, 44, 128).transpose(3, 2, 0, 1)),
            "win_cache": ca(g("cache_nsa_win")[0, c * 16:(c + 1) * 16].reshape(16, 512, 512)),
        })
    res = run_bass_kernel_spmd(nc, in_maps, core_ids=list(range(8)))
    R = res.results
    if os.environ.get("MK_RAW"):
        return R
    P = [R[0], R[4]]
    y_prompt = np.stack([np.ascontiguousarray(r["o_yT"][:, :SEQ].T) for r in P]).astype(f32)
    y_sample = np.concatenate([np.ascontiguousarray(r["o_yT"][:, SEQ:].T).reshape(16, 4, D) for r in R]).astype(f32)
    gla_p = np.stack([r["o_gla_p"] for r in P])[None].astype(f32)
    gla_s = np.concatenate([r["o_gla_s"] for r in R])[None].astype(f32)
    conv_p = np.stack([r["o_conv_p"] for r in P])[None].astype(f32)
    conv_s = np.concatenate([r["o_conv_s"].transpose(2, 3, 1, 0).reshape(16, 30, 512) for r in R])[None].astype(f32)
    nsa_p = np.stack([r["o_nsa_kv"].reshape(SEQ, 4, 4, 64) for r in P])[None].astype(f32)
    nsa_s = np.concatenate([r["o_nsa_kv_s"].reshape(16, 4, 4, 4, 64) for r in R])[None].astype(f32)
    win_p = np.stack([r["o_win_p"].reshape(512, 2, 4, 64) for r in P])[None].astype(f32)
    win_s = np.concatenate([r["o_win_s"].reshape(16, 512, 2, 4, 64) for r in R])[None].astype(f32)
    mem_p = np.stack([np.stack([r["o_memkv%d" % l].reshape(256, 2, 4, 128) for r in P]) for l in range(2)]).astype(f32)
    ffn_p = np.stack([np.stack([r["o_ffn_p%d" % l] for r in P]) for l in range(2)]).astype(f32)
    ffn_s = np.stack([np.concatenate([r["o_ffn_s%d" % l].transpose(2, 3, 1, 0).reshape(16, 2, 2 * DFF) for r in R])
                      for l in range(2)]).astype(f32)
    return (y_prompt, y_sample, gla_p, gla_s, conv_p, conv_s, nsa_p, nsa_s, win_p, win_s, mem_p, ffn_p, ffn_s)
```

```python
import os
from contextlib import ExitStack
import numpy as np
import ml_dtypes
import concourse.bass as bass
import concourse.mybir as mybir
from concourse.bass_utils import run_bass_kernel_spmd

F32 = mybir.dt.float32
BF16 = mybir.dt.bfloat16
I32 = mybir.dt.int32
AF = mybir.ActivationFunctionType
ALU = mybir.AluOpType
AX = mybir.AxisListType

D = 1024
SEQ = 8192
TP = 256
NTILE = int(os.environ.get("MK_NT", str(SEQ // TP)))
SB_ = 16
ST = 64
TTOT = SEQ + ST
EPS = 1e-6
IN_A = 2576
DFF = 2816
EPOCH = 30000
STAGE = int(os.environ.get("MK_STAGE", "4"))


class V:
    __slots__ = ("buf", "ap")

    def __init__(self, buf, ap):
        self.buf = buf
        self.ap = ap


class Buf:
    def __init__(self, name, t, kind):
        self.name = name
        self.t = t
        self.kind = kind
        self.last_w = None
        self.readers = []
        self.dsem = None
        self.dcnt = 0

    def __getitem__(self, idx):
        return V(self, self.t[idx])


class Sched:
    ENG = ["tensor", "vector", "scalar", "gpsimd", "sync"]

    def __init__(self, nc, es):
        self.nc = nc
        self.es = es
        self.ops = {e: [] for e in self.ENG}
        self.cnt = {e: 0 for e in self.ENG}
        self.sems = {}
        self.esem = {}
        self.nsem = 0
        for e in self.ENG:
            self.esem[e] = self.new_sem("p_" + e)
        self.known = {e: {} for e in self.ENG}
        self.pending = {e: set() for e in self.ENG}
        self.bufs = []
        self.dd = Buf("dram2dram", None, "x")

    def new_sem(self, name):
        self.nsem += 1
        nm = "%s_%d" % (name, self.nsem)
        self.sems[nm] = self.es.enter_context(self.nc.semaphore(nm))
        return nm

    def sb(self, name, shape, dt):
        t = self.es.enter_context(self.nc.sbuf_tensor(name, list(shape), dt))
        b = Buf(name, t, "sb")
        self.bufs.append(b)
        return b

    def ps(self, name, shape, dt=F32):
        t = self.es.enter_context(self.nc.psum_tensor(name, list(shape), dt))
        b = Buf(name, t, "ps")
        self.bufs.append(b)
        return b

    def dram(self, name, shape, dt, kind):
        t = self.nc.dram_tensor(name, list(shape), dt, kind=kind).ap()
        b = Buf(name, t, "dr")
        self.bufs.append(b)
        return b

    def _deps(self, eng, reads, writes, extra):
        deps = set(extra) | self.pending[eng]
        self.pending[eng] = set()
        for b in reads:
            if b.last_w is not None:
                deps.add(b.last_w)
        for b in writes:
            if b.last_w is not None:
                deps.add(b.last_w)
            deps.update(b.readers)
        best = {}
        for (s, v) in deps:
            if v > best.get(s, 0):
                best[s] = v
        waits = []
        for s, v in best.items():
            if eng == "tensor" and s == self.esem[eng]:
                continue
            if self.known[eng].get(s, 0) >= v:
                continue
            self.known[eng][s] = v
            waits.append((s, v))
        return waits

    def _commit(self, ev, reads, writes):
        for b in reads:
            b.readers.append(ev)
        for b in writes:
            b.last_w = ev
            b.readers = []

    def op(self, eng, fn, reads=(), writes=(), extra=()):
        reads = [r.buf if isinstance(r, V) else r for r in reads]
        writes = [w.buf if isinstance(w, V) else w for w in writes]
        if self.cnt[eng] >= EPOCH:
            self.esem[eng] = self.new_sem("p_" + eng)
            self.cnt[eng] = 0
        waits = self._deps(eng, reads, writes, extra)
        self.cnt[eng] += 1
        ev = (self.esem[eng], self.cnt[eng])
        self.ops[eng].append((fn, waits, ev[0], 1))
        self._commit(ev, reads, writes)
        return ev

    def dma(self, out, in_, eng="sync", extra=(), **kw):
        ob, ib = out.buf, in_.buf
        if ob.kind == "sb":
            owner = ob
        elif ib.kind == "sb":
            owner = ib
        else:
            owner = self.dd
        if owner.dsem is None:
            owner.dsem = self.new_sem("d_" + owner.name)
        waits = self._deps(eng, [ib], [ob], extra)
        owner.dcnt += 16
        ev = (owner.dsem, owner.dcnt)
        oa, ia = out.ap, in_.ap
        self.ops[eng].append((lambda e: e.dma_start(out=oa, in_=ia, **kw), waits, ev[0], 16))
        self._commit(ev, [ib], [ob])
        return ev

    def idma(self, out, in_, idx):
        ob, ib = out.buf, in_.buf
        owner = ob
        if owner.dsem is None:
            owner.dsem = self.new_sem("d_" + owner.name)
        waits = self._deps("gpsimd", [ib, idx.buf], [ob], ())
        owner.dcnt += 16
        ev = (owner.dsem, owner.dcnt)
        oa, ia, xa = out.ap, in_.ap, idx.ap
        self.ops["gpsimd"].append((lambda e: e.indirect_dma_start(
            out=oa, out_offset=None, in_=ia, in_offset=bass.IndirectOffsetOnAxis(ap=xa, axis=0)), waits, ev[0], 16))
        self._commit(ev, [ib, idx.buf], [ob])
        return ev

    def barrier(self):
        deps = set()
        for e in self.ENG:
            if self.cnt[e] > 0:
                deps.add((self.esem[e], self.cnt[e]))
        for b in self.bufs + [self.dd]:
            if b.dsem is not None and b.dcnt > 0:
                deps.add((b.dsem, b.dcnt))
        ev = self.dma(self.bar_dst, self.bar_src, eng="sync", extra=deps)
        for e in self.ENG:
            self.pending[e].add(ev)
        return ev

    def emit(self):
        nc = self.nc
        sems = self.sems
        with nc.Block() as block:
            def mk(eng):
                ops = self.ops[eng]

                def body(e):
                    for fn, waits, isem, inc in ops:
                        for (s, v) in waits:
                            e.wait_ge(sems[s], v)
                        fn(e).then_inc(sems[isem], inc)
                return body
            block.tensor(mk("tensor"))
            block.vector(mk("vector"))
            block.scalar(mk("scalar"))
            block.gpsimd(mk("gpsimd"))
            block.sync(mk("sync"))
        self.ops = {e: [] for e in self.ENG}

    def mm(self, out, lhsT, rhs, start=True, stop=True):
        oa, la, ra = out.ap, lhsT.ap, rhs.ap
        rd = [lhsT, rhs] + ([] if start else [out])
        return self.op("tensor", lambda e: e.matmul(oa, la, ra, start=start, stop=stop),
                       reads=rd, writes=[out])

    def act(self, out, in_, func, bias=None, scale=None, accum=None, reads=()):
        oa, ia = out.ap, in_.ap
        kw = {}
        rd = [in_] + list(reads)
        if bias is not None:
            if isinstance(bias, V):
                kw["bias"] = bias.ap
                rd.append(bias)
            else:
                kw["bias"] = bias
        if scale is not None:
            if isinstance(scale, V):
                kw["scale"] = scale.ap
                rd.append(scale)
            else:
                kw["scale"] = scale
        wr = [out]
        if accum is not None:
            kw["accum_out"] = accum.ap
            wr.append(accum)
        return self.op("scalar", lambda e: e.activation(oa, ia, func, **kw), reads=rd, writes=wr)

    def tt(self, out, a, b, op, eng="vector"):
        oa, aa, ba = out.ap, a.ap, b.ap
        return self.op(eng, lambda e: e.tensor_tensor(oa, aa, ba, op), reads=[a, b], writes=[out])

    def ts(self, out, a, s1, s2, op0, op1=None, eng="vector"):
        oa, aa = out.ap, a.ap
        rd = [a]
        if isinstance(s1, V):
            rd.append(s1)
            s1 = s1.ap
        if isinstance(s2, V):
            rd.append(s2)
            s2 = s2.ap
        if op1 is None:
            return self.op(eng, lambda e: e.tensor_scalar(oa, aa, s1, None, op0), reads=rd, writes=[out])
        return self.op(eng, lambda e: e.tensor_scalar(oa, aa, s1, s2, op0, op1), reads=rd, writes=[out])

    def stt(self, out, a, s, b, op0, op1, eng="vector"):
        eng = "vector"
        oa, aa, ba = out.ap, a.ap, b.ap
        rd = [a, b]
        if isinstance(s, V):
            rd.append(s)
            s = s.ap
        return self.op(eng, lambda e: e.scalar_tensor_tensor(oa, aa, s, ba, op0, op1), reads=rd, writes=[out])

    def recip(self, out, in_):
        oa, ia = out.ap, in_.ap
        return self.op("vector", lambda e: e.reciprocal(oa, ia), reads=[in_], writes=[out])

    def rsqrt(self, out, in_, scale, bias):
        self.act(out, in_, AF.Sqrt, bias=bias, scale=scale)
        self.recip(out, out)

    def copy(self, out, in_, eng="vector"):
        oa, ia = out.ap, in_.ap
        if eng == "scalar":
            return self.op(eng, lambda e: e.copy(oa, ia), reads=[in_], writes=[out])
        if eng == "vector" and in_.buf.kind == "ps":
            return self.op(eng, lambda e: e.tensor_scalar(oa, ia, 1.0, None, ALU.mult), reads=[in_], writes=[out])
        return self.op(eng, lambda e: e.tensor_copy(oa, ia), reads=[in_], writes=[out])

    def memset(self, out, val, eng="gpsimd"):
        oa = out.ap
        return self.op(eng, lambda e: e.memset(oa, val), writes=[out])

    def transpose(self, out, in_, ident):
        oa, ia, da = out.ap, in_.ap, ident.ap
        return self.op("tensor", lambda e: e.transpose(oa, ia, da), reads=[in_, ident], writes=[out])


def build(n_layers_run=2):
    nc = bass.Bass("TRN2", target_bir_lowering=False)
    es = ExitStack()
    S = Sched(nc, es)
    IN, OUT = "ExternalInput", "ExternalOutput"

    xT_in = S.dram("xT_in", [D, TTOT], F32, IN)
    XA = S.dram("XA", [D, TTOT], F32, "Internal")
    XB = S.dram("XB", [D, TTOT], F32, "Internal")
    c_ident = S.dram("c_ident", [128, 128], F32, IN)
    c_triN = S.dram("c_triN", [128, 128], F32, IN)
    c_triU = S.dram("c_triU", [128, 128], F32, IN)
    c_mask = S.dram("c_mask", [128, 128], F32, IN)
    vecs = S.dram("vecs", [128, 96], F32, IN)
    w_in_a = S.dram("w_in_a", [D, IN_A], F32, IN)
    w_out_a = S.dram("w_out_a", [D, D], F32, IN)
    w_gate = S.dram("w_gate", [16, 256], F32, IN)
    b_gate = S.dram("b_gate", [1, 256], F32, IN)
    w_dwT = S.dram("w_dwT", [128, 4, 31], F32, IN)
    memT = S.dram("memT", [D, 256], F32, IN)
    w_xq = [S.dram("w_xq%d" % l, [D, 512], F32, IN) for l in range(2)]
    w_mkv = [S.dram("w_mkv%d" % l, [D, 1024], F32, IN) for l in range(2)]
    w_xo = [S.dram("w_xo%d" % l, [512, D], F32, IN) for l in range(2)]
    w_up = [S.dram("w_up%d" % l, [D, 2 * DFF], F32, IN) for l in range(2)]
    w_down = [S.dram("w_down%d" % l, [DFF, D], F32, IN) for l in range(2)]
    ffnv = [S.dram("ffnv%d" % l, [128, 44, 4], F32, IN) for l in range(2)]

    o_gla_p = S.dram("o_gla_p", [4, 64, 128], F32, OUT)
    o_conv_p = S.dram("o_conv_p", [30, 512], F32, OUT)
    o_memkv = [S.dram("o_memkv%d" % l, [256, 1024], F32, OUT) for l in range(2)]
    o_ffn_p = [S.dram("o_ffn_p%d" % l, [2, 2 * DFF], F32, OUT) for l in range(2)]
    o_yT = S.dram("o_yT", [D, TTOT], F32, OUT)
    w_in_c = S.dram("w_in_c", [D, 2608], F32, IN)
    c_smp = S.dram("c_smp", [64, 208], F32, IN)
    w_out_c = S.dram("w_out_c", [D, D], F32, IN)
    WoD = S.dram("WoD", [D, D], BF16, "Internal")
    w_cmp1 = S.dram("w_cmp1", [2, 32, 64, 64], F32, IN)
    w_cmp2 = S.dram("w_cmp2", [2, 64, 64], F32, IN)
    pe_T = S.dram("pe_T", [64, 2, 32], F32, IN)
    n_ind = S.dram("n_ind", [64, SEQ], BF16, IN)
    n_trib = S.dram("n_trib", [128, 2, 512], BF16, IN)
    n_cmbig = S.dram("n_cmbig", [128, 1016], BF16, IN)
    n_cmt = S.dram("n_cmt", [128, 17, 128], BF16, IN)
    WcD = S.dram("WcD", [D, 2608], BF16, "Internal")
    NPHYS = 2560
    poolF = [S.dram("poolF%d" % k, [NPHYS * 64, 512], F32, IN) for k in range(3)]
    n_smpb = S.dram("n_smpb", [128, 32], BF16, IN)
    poolV = S.dram("poolV", [NPHYS * 128, 256], F32, IN)
    pt_bc = S.dram("pt_bc", [16, 128, 16], I32, IN)
    winKT = S.dram("winKT", [16, 64, 4, 512], F32, IN)
    winVs = S.dram("winVs", [16, 128, 4, 256], F32, IN)
    n_smp = S.dram("n_smp", [128, 128], F32, IN)
    n_fbbig = S.dram("n_fbbig", [128, 254], F32, IN)
    n_sel = S.dram("n_sel", [48, 48 * 64], BF16, IN)
    gla_s0 = S.dram("gla_s0", [16, 4, 64, 128], F32, IN)
    convhT = S.dram("convhT", [128, 4, 16, 30], F32, IN)
    memKT = [S.dram("memKT%d" % l, [16, 128, 4, 256], F32, IN) for l in range(2)]
    memV = [S.dram("memV%d" % l, [16, 128, 2, 512], F32, IN) for l in range(2)]
    ffnhT = [S.dram("ffnhT%d" % l, [128, 44, 16, 2], F32, IN) for l in range(2)]
    win_cache = S.dram("win_cache", [16, 512, 512], F32, IN)
    o_gla_s = S.dram("o_gla_s", [16, 4, 64, 128], F32, OUT)
    o_conv_s = S.dram("o_conv_s", [128, 4, 16, 30], F32, OUT)
    o_ffn_s = [S.dram("o_ffn_s%d" % l, [128, 44, 16, 2], F32, OUT) for l in range(2)]
    o_nsa_kv_s = S.dram("o_nsa_kv_s", [64, 1024], F32, OUT)
    o_win_s = S.dram("o_win_s", [16, 512, 512], F32, OUT)
    o_nsa_kv = S.dram("o_nsa_kv", [SEQ, 1024], F32, OUT)
    o_win_p = S.dram("o_win_p", [512, 512], F32, OUT)

    ident = S.sb("ident", [128, 128], F32)
    identb = S.sb("identb", [128, 128], BF16)
    triN = S.sb("triN", [128, 128], F32)
    triU = S.sb("triU", [128, 128], F32)
    maskf = S.sb("maskf", [128, 128], F32)
    ones_f = S.sb("ones_f", [128, 128], F32)
    ones_b = S.sb("ones_b", [128, 128], BF16)
    vec = S.sb("vec", [128, 96], F32)
    barbuf = S.sb("barbuf", [1, 16], F32)
    S.bar_dst = barbuf[0:1, 0:16]
    S.bar_src = c_ident[0:1, 0:16]
    S.dma(ident[:, :], c_ident[:, :])
    S.dma(triN[:, :], c_triN[:, :])
    S.dma(triU[:, :], c_triU[:, :])
    S.dma(maskf[:, :], c_mask[:, :])
    S.dma(vec[:, :], vecs[:, :])
    S.memset(ones_f[:, :], 1.0)
    S.memset(ones_b[:, :], 1.0)
    S.copy(identb[:, :], ident[:, :])
    VN_MIX, VN_MEM, VN_X, VN_FFN = 0, 8, 16, 24
    V_GOUT, V_BDW, V_GLN, V_BLN = 64, 68, 72, 76

    PS = [S.ps("psb%d" % i, [128, 512]) for i in range(8)]

    def load_w_bf16(dst, src_ap_fn, kchunks, ncols):
        for kc in range(kchunks):
            n0 = 0
            while n0 < ncols:
                n1 = min(ncols, n0 + 2048)
                S.dma(dst[:, kc, n0:n1], src_ap_fn(kc, n0, n1), eng="gpsimd")
                n0 = n1

    def rmsnorm_tile(xT, T, gcol, sq, xn, rstd, psb):
        for kc in range(8):
            S.act(sq[:, kc, :T], xT[:, kc, :T], AF.Square)
        for kc in range(8):
            S.mm(psb[:, :T], ones_b[:, :], sq[:, kc, :T], start=(kc == 0), stop=(kc == 7))
        S.rsqrt(rstd[:, :T], psb[:, :T], 1.0 / D, EPS)
        for kc in range(8):
            S.stt(xn[:, kc, :T], xT[:, kc, :T], vec[:, gcol + kc:gcol + kc + 1], rstd[:, :T],
                  ALU.mult, ALU.mult, eng=("vector" if kc % 2 == 0 else "gpsimd"))

    def xview(X, t0, T):
        return V(X, X.t.rearrange("(k p) t -> p k t", p=128)[:, :, t0:t0 + T])

    tiles = [(i * TP, TP) for i in range(NTILE)]
    if os.environ.get("MK_SAMPLE", "1") != "0":
        tiles.append((SEQ, ST))

    def phase_mixer_a(Xin, Xout):
        with ExitStack() as pes:
            def sb(name, shape, dt):
                t = pes.enter_context(nc.sbuf_tensor(name, list(shape), dt))
                b = Buf(name, t, "sb")
                S.bufs.append(b)
                return b
            Win = sb("Win", [128, 8, IN_A], BF16)
            Wout = sb("Wout", [128, 8, D], BF16)
            wg = sb("wg", [16, 256], F32)
            bg = sb("bg", [1, 256], F32)
            wdw = sb("wdw", [128, 4, 31], F32)
            dg = sb("dg", [128, 4, 31, 128], BF16)
            load_w_bf16(Win, lambda kc, a, b: w_in_a[kc * 128:(kc + 1) * 128, a:b], 8, IN_A)
            load_w_bf16(Wout, lambda kc, a, b: w_out_a[kc * 128:(kc + 1) * 128, a:b], 8, D)
            S.dma(wg[:, :], w_gate[:, :])
            S.dma(bg[:, :], b_gate[:, :])
            S.dma(wdw[:, :, :], w_dwT[:, :, :])
            for m in range(4):
                for k in range(31):
                    S.ts(dg[:, m, k, :], ident[:, :], wdw[:, m, k:k + 1], None, ALU.mult,
                         eng=("vector" if (k % 2 == 0) else "gpsimd"))

            xT = sb("xT", [128, 8, TP], F32)
            sq = sb("sq", [128, 8, TP], BF16)
            xn = sb("xn", [128, 8, TP], BF16)
            rstd = sb("rstd", [128, TP], F32)
            qk = sb("qk", [64, 8, TP], F32)
            rs = sb("rs", [128, 4, TP], BF16)
            aT = sb("aT", [16, TP], F32)
            sg = sb("sg", [128, TP], F32)
            glu = sb("glu", [128, 4, 30 + TP], BF16)
            cc = sb("cc", [128, 4, TP], F32)
            csq = sb("csq", [128, 4, TP], F32)
            actT = sb("actT", [128, 8, TP], BF16)
            mean = sb("mean", [128, TP], F32)
            msq = sb("msq", [128, TP], F32)
            crs = sb("crs", [128, TP], F32)
            dtmp = sb("dtmp", [128, TP], F32)
            vsb = sb("vsb", [128, 512], BF16)
            e1 = sb("e1", [128, 256], F32)
            la = sb("la", [128, 256], F32)
            eD = sb("eD", [128, 256], F32)
            kl = sb("kl", [128, 256], BF16)
            eB = sb("eB", [64, 4, 128], F32)
            enB = sb("enB", [64, 4, 128], F32)
            qe = sb("qe", [64, 4, 128], BF16)
            ke = sb("ke", [64, 4, 128], BF16)
            att = sb("att", [128, 4, 128], BF16)
            Sf = sb("Sf", [64, 4, 128], F32)
            Sb = sb("Sb", [64, 4, 128], BF16)
            osq = sb("osq", [128, 512], BF16)
            orstd = sb("orstd", [128, 512], F32)
            og = sb("og", [128, 4, 128], F32)
            onesr = sb("onesr", [1, 128], F32)
            tail = sb("tail", [128, 4, 32], F32)
            tailT = sb("tailT", [32, 512], F32)
            csm = sb("csm", [64, 208], F32)
            S.dma(csm[:, :], c_smp[:, :])
            gps = sb("gps", [128, 4, 16, 34], F32)
            hst = sb("hst", [128, 4, 16, 30], F32)
            S0f = [sb("S0f%d" % i, [64, 4, 128], F32) for i in range(2)]
            S0h = [sb("S0h%d" % i, [64, 4, 128], BF16) for i in range(2)]
            Sout = [sb("Sout%d" % i, [64, 4, 128], F32) for i in range(2)]
            klm = sb("klm", [64, 256], BF16)
            S.memset(onesr[:, :], 1.0)
            S.memset(Sf[:, :, :], 0.0)
            S.memset(Sb[:, :, :], 0.0)
            S.memset(glu[:, :, :], 0.0)

            for ti, (t0, T) in enumerate(tiles):
                is_sample = (t0 >= SEQ)
                S.dma(xT[:, :, :T], xview(Xin, t0, T))
                rmsnorm_tile(xT, T, VN_MIX, sq, xn, rstd, PS[7])

                def proj(psv, col0, M):
                    for kc in range(8):
                        S.mm(psv, Win[:, kc, col0:col0 + M], xn[:, kc, :T], start=(kc == 0), stop=(kc == 7))
                for h in range(8):
                    pb = PS[h % 2]
                    proj(pb[0:64, :T], h * 64, 64)
                    if h < 4:
                        S.act(qk[:, h, :T], pb[0:64, :T], AF.Copy, scale=0.125)
                    else:
                        S.copy(qk[:, h, :T], pb[0:64, :T], eng="vector")
                for h in range(4):
                    pb = PS[h % 2]
                    proj(pb[:, :T], 1024 + h * 128, 128)
                    S.act(rs[:, h, :T], pb[:, :T], AF.Silu)
                proj(PS[0][0:16, :T], 1536, 16)
                S.copy(aT[:, :T], PS[0][0:16, :T], eng="vector")
                if ti > 0 and not is_sample:
                    S.copy(glu[:, :, 0:30], glu[:, :, TP:TP + 30], eng="gpsimd")
                if not is_sample:
                    for m in range(4):
                        proj(PS[1][:, :T], 1552 + 512 + m * 128, 128)
                        S.act(sg[:, :T], PS[1][:, :T], AF.Sigmoid)
                        proj(PS[0][:, :T], 1552 + m * 128, 128)
                        S.tt(glu[:, m, 30:30 + T], PS[0][:, :T], sg[:, :T], ALU.mult)

                if not is_sample:
                    for c in range(T // 128):
                        cs = slice(c * 128, (c + 1) * 128)
                        for kc in range(8):
                            S.mm(PS[0][:, :512], xn[:, kc, cs], Win[:, kc, 512:1024], start=(kc == 0), stop=(kc == 7))
                        S.copy(vsb[:, :], PS[0][:, :512], eng="scalar")
                        for kc in range(8):
                            S.mm(PS[1][:, :256], xn[:, kc, cs], Win[:, kc, 256:512], start=(kc == 0), stop=(kc == 7))
                        S.mm(PS[2][:, 0:256], aT[:, cs], wg[:, :], start=True, stop=False)
                        S.mm(PS[2][:, 0:256], onesr[:, :], bg[:, :], start=False, stop=True)
                        S.act(e1[:, :], PS[2][:, 0:256], AF.Exp, scale=-1.0)
                        S.act(la[:, :], e1[:, :], AF.Ln, bias=1.0)
                        S.mm(PS[2][:, 256:512], triU[:, :], la[:, :])
                        for h in range(4):
                            S.mm(PS[3][0:64, h * 128:(h + 1) * 128], la[:, h * 64:(h + 1) * 64], triN[:, :])
                        S.act(eD[:, :], PS[2][:, 256:512], AF.Exp)
                        S.tt(kl[:, :], PS[1][:, :256], eD[:, :], ALU.mult)
                        ps3 = V(PS[3], PS[3].t[0:64, :].rearrange("p (h t) -> p h t", h=4))
                        S.act(eB[:, :, :], ps3, AF.Exp)
                        S.act(enB[:, :, :], ps3, AF.Exp, scale=-1.0)
                        S.tt(qe[:, :, :], qk[:, 0:4, cs], eB[:, :, :], ALU.mult)
                        S.tt(ke[:, :, :], qk[:, 4:8, cs], enB[:, :, :], ALU.mult, eng="gpsimd")
                        for h in range(4):
                            S.mm(PS[4][:, h * 128:(h + 1) * 128], ke[:, h, :], qe[:, h, :])
                        for h in range(4):
                            S.tt(att[:, h, :], PS[4][:, h * 128:(h + 1) * 128], maskf[:, :], ALU.mult)
                        for h in range(4):
                            S.mm(PS[5][:, h * 128:(h + 1) * 128], vsb[:, h * 128:(h + 1) * 128], att[:, h, :],
                                 start=True, stop=False)
                            S.mm(PS[5][:, h * 128:(h + 1) * 128], Sb[:, h, :], qe[:, h, :], start=False, stop=True)
                        for h in range(4):
                            S.mm(PS[6][0:64, h * 128:(h + 1) * 128], kl[:, h * 64:(h + 1) * 64],
                                 vsb[:, h * 128:(h + 1) * 128])
                        for h in range(4):
                            S.stt(Sf[:, h, :], Sf[:, h, :], eB[:, h, 127:128], PS[6][0:64, h * 128:(h + 1) * 128],
                                  ALU.mult, ALU.add)
                        S.copy(Sb[:, :, :], Sf[:, :, :], eng="gpsimd")
                        S.act(osq[:, :], PS[5][:, :], AF.Square)
                        S.mm(PS[7][:, :], ones_b[:, :], osq[:, :])
                        S.rsqrt(orstd[:, :], PS[7][:, :], 1.0 / 128, EPS)
                        S.tt(V(og, og.t[:, :, :].rearrange("p h t -> p (h t)")), PS[5][:, :], orstd[:, :], ALU.mult)
                        for h in range(4):
                            S.stt(actT[:, h, cs], og[:, h, :], vec[:, V_GOUT + h:V_GOUT + h + 1], rs[:, h, cs],
                                  ALU.mult, ALU.mult, eng=("vector" if h % 2 == 0 else "gpsimd"))

                    for m in range(4):
                        for k in range(31):
                            S.mm(PS[0][:, :T], dg[:, m, k, :], glu[:, m, k:k + T], start=(k == 0), stop=(k == 30))
                        S.act(cc[:, m, :T], PS[0][:, :T], AF.Identity, bias=vec[:, V_BDW + m:V_BDW + m + 1])
                else:
                    for kc in range(8):
                        S.mm(PS[0][0:64, :512], xn[:, kc, 0:64], Win[:, kc, 512:1024], start=(kc == 0), stop=(kc == 7))
                    S.copy(vsb[0:64, :], PS[0][0:64, :512], eng="scalar")
                    for kc in range(8):
                        S.mm(PS[1][0:64, :256], xn[:, kc, 0:64], Win[:, kc, 256:512], start=(kc == 0), stop=(kc == 7))
                    S.mm(PS[2][0:64, 0:256], aT[:, 0:64], wg[:, :], start=True, stop=False)
                    S.mm(PS[2][0:64, 0:256], onesr[:, 0:64], bg[:, :], start=False, stop=True)
                    S.act(e1[0:64, :], PS[2][0:64, 0:256], AF.Exp, scale=-1.0)
                    S.act(la[0:64, :], e1[0:64, :], AF.Ln, bias=1.0)
                    S.mm(PS[2][0:64, 256:512], csm[:, 64:128], la[0:64, :])
                    for h in range(4):
                        S.mm(PS[3][0:64, h * 128:h * 128 + 64], la[0:64, h * 64:(h + 1) * 64], csm[:, 0:64])
                    S.act(eD[0:64, :], PS[2][0:64, 256:512], AF.Exp)
                    S.tt(kl[0:64, :], PS[1][0:64, :256], eD[0:64, :], ALU.mult)
                    ps3s = V(PS[3], PS[3].t[0:64, :].rearrange("p (h t) -> p h t", h=4)[:, :, 0:64])
                    S.act(eB[:, :, 0:64], ps3s, AF.Exp)
                    S.act(enB[:, :, 0:64], ps3s, AF.Exp, scale=-1.0)
                    S.tt(qe[:, :, 0:64], qk[:, 0:4, 0:64], eB[:, :, 0:64], ALU.mult)
                    S.tt(ke[:, :, 0:64], qk[:, 4:8, 0:64], enB[:, :, 0:64], ALU.mult)
                    for h in range(4):
                        S.mm(PS[3][0:64, h * 128 + 64:h * 128 + 128], ke[:, h, 0:64], qe[:, h, 0:64])
                    for h in range(4):
                        S.tt(att[0:64, h, 0:64], PS[3][0:64, h * 128 + 64:h * 128 + 128], csm[:, 128:192], ALU.mult)
                    OB = [PS[0], PS[4], PS[5], PS[7]]
                    for h in range(4):
                        S.mm(OB[h][:, 0:64], vsb[0:64, h * 128:(h + 1) * 128], att[0:64, h, 0:64], start=True, stop=False)
                    for b in range(16):
                        sf, sh, so = S0f[b % 2], S0h[b % 2], Sout[b % 2]
                        S.dma(sf[:, :, :], V(gla_s0, gla_s0.t[b].rearrange("h d v -> d h v")))
                        S.copy(sh[:, :, :], sf[:, :, :], eng="gpsimd")
                        for h in range(4):
                            S.mm(OB[h][:, 4 * b:4 * b + 4], sh[:, h, :], qe[:, h, 4 * b:4 * b + 4], start=False, stop=(b == 15))
                        S.ts(klm[:, :], kl[0:64, :], csm[:, 192 + b:193 + b], None, ALU.mult)
                        for h in range(4):
                            S.mm(PS[6][0:64, h * 128:(h + 1) * 128], klm[:, h * 64:(h + 1) * 64], vsb[0:64, h * 128:(h + 1) * 128])
                        for h in range(4):
                            S.stt(so[:, h, :], sf[:, h, :], eB[:, h, 4 * b + 3:4 * b + 4], PS[6][0:64, h * 128:(h + 1) * 128],
                                  ALU.mult, ALU.add)
                        S.dma(V(o_gla_s, o_gla_s.t[b].rearrange("h d v -> d h v")), so[:, :, :])
                    for h in range(4):
                        S.act(osq[:, h * 64:(h + 1) * 64], OB[h][:, 0:64], AF.Square)
                    S.mm(PS[3][:, 0:256], ones_b[:, :], osq[:, 0:256])
                    S.rsqrt(orstd[:, 0:256], PS[3][:, 0:256], 1.0 / 128, EPS)
                    for h in range(4):
                        S.tt(og[:, h, 0:64], OB[h][:, 0:64], orstd[:, h * 64:(h + 1) * 64], ALU.mult)
                        S.stt(actT[:, h, 0:64], og[:, h, 0:64], vec[:, V_GOUT + h:V_GOUT + h + 1], rs[:, h, 0:64],
                              ALU.mult, ALU.mult)
                    S.dma(hst[:, :, :, :], convhT[:, :, :, :])
                    S.copy(gps[:, :, :, 0:30], hst[:, :, :, :], eng="gpsimd")
                    for m in range(4):
                        proj(PS[1][:, :T], 1552 + 512 + m * 128, 128)
                        S.act(sg[:, :T], PS[1][:, :T], AF.Sigmoid)
                        proj(PS[2][:, :T], 1552 + m * 128, 128)
                        S.tt(gps[:, m, :, 30:34], V(PS[2], PS[2].t[:, :T].rearrange("p (b t) -> p b t", t=4)),
                             V(sg, sg.t[:, :T].rearrange("p (b t) -> p b t", t=4)), ALU.mult)
                    S.copy(hst[:, :, :, :], gps[:, :, :, 4:34], eng="gpsimd")
                    S.dma(o_conv_s[:, :, :, :], hst[:, :, :, :])
                    for m in range(4):
                        cacc = V(cc, cc.t[:, m, 0:64].rearrange("p (b t) -> p b t", t=4))
                        S.ts(cacc, gps[:, m, :, 0:4], wdw[:, m, 0:1], vec[:, V_BDW + m:V_BDW + m + 1], ALU.mult, ALU.add)
                        for k in range(1, 31):
                            S.stt(cacc, gps[:, m, :, k:k + 4], wdw[:, m, k:k + 1], cacc, ALU.mult, ALU.add)
                for m in range(4):
                    S.act(csq[:, m, :T], cc[:, m, :T], AF.Square)
                for m in range(4):
                    S.mm(PS[1][:, :T], ones_f[:, :], cc[:, m, :T], start=(m == 0), stop=(m == 3))
                for m in range(4):
                    S.mm(PS[2][:, :T], ones_f[:, :], csq[:, m, :T], start=(m == 0), stop=(m == 3))
                S.ts(mean[:, :T], PS[1][:, :T], 1.0 / 512, None, ALU.mult)
                S.tt(msq[:, :T], mean[:, :T], mean[:, :T], ALU.mult)
                S.stt(crs[:, :T], PS[2][:, :T], 1.0 / 512, msq[:, :T], ALU.mult, ALU.subtract)
                S.rsqrt(crs[:, :T], crs[:, :T], 1.0, EPS)
                for m in range(4):
                    S.tt(dtmp[:, :T], cc[:, m, :T], mean[:, :T], ALU.subtract)
                    S.tt(dtmp[:, :T], dtmp[:, :T], crs[:, :T], ALU.mult)
                    S.act(actT[:, 4 + m, :T], dtmp[:, :T], AF.Silu,
                          bias=vec[:, V_BLN + m:V_BLN + m + 1], scale=vec[:, V_GLN + m:V_GLN + m + 1])

                for oc in range(8):
                    pb = PS[oc % 2]
                    for kc in range(8):
                        S.mm(pb[:, :T], Wout[:, kc, oc * 128:(oc + 1) * 128], actT[:, kc, :T],
                             start=(kc == 0), stop=(kc == 7))
                    S.tt(xT[:, oc, :T], xT[:, oc, :T], pb[:, :T], ALU.add)
                S.dma(xview(Xout, t0, T), xT[:, :, :T])

                if t0 + T == NTILE * TP:
                    S.dma(V(o_gla_p, o_gla_p.t.rearrange("h d v -> d h v")), Sf[:, :, :])
                    S.copy(tail[:, :, 0:30], glu[:, :, TP:TP + 30], eng="vector")
                    for m in range(4):
                        S.transpose(PS[3][0:30, m * 128:(m + 1) * 128], tail[:, m, 0:30], ident[:, :])
                    S.copy(tailT[0:30, :], PS[3][0:30, :], eng="vector")
                    S.dma(o_conv_p[:, :], tailT[0:30, :])
            S.barrier()

    def phase_cross(l, Xin, Xout, mem_only=False):
        with ExitStack() as pes:
            def sb(name, shape, dt):
                name = name + "_L%d" % l
                t = pes.enter_context(nc.sbuf_tensor(name, list(shape), dt))
                b = Buf(name, t, "sb")
                S.bufs.append(b)
                return b
            Wq = sb("Wq", [128, 8, 512], BF16)
            Wkv = sb("Wkv", [128, 8, 1024], BF16)
            Wo = sb("Wo", [128, 4, D], BF16)
            load_w_bf16(Wq, lambda kc, a, b: w_xq[l][kc * 128:(kc + 1) * 128, a:b], 8, 512)
            load_w_bf16(Wkv, lambda kc, a, b: w_mkv[l][kc * 128:(kc + 1) * 128, a:b], 8, 1024)
            load_w_bf16(Wo, lambda kc, a, b: w_xo[l][kc * 128:(kc + 1) * 128, a:b], 4, D)
            xT = sb("cxT", [128, 8, TP], F32)
            sq = sb("csq_", [128, 8, TP], BF16)
            xn = sb("cxn", [128, 8, TP], BF16)
            rstd = sb("crstd", [128, TP], F32)
            KT = sb("KT", [128, 4, 256], BF16)
            Vm = sb("Vm", [128, 2, 512], BF16)
            kvf = sb("kvf", [128, 1024], F32)
            qT = sb("qT", [128, 4, TP], BF16)
            PT = sb("PT", [128, 2, TP], BF16)
            rinv = sb("rinv", [128, TP], F32)
            OT = sb("OT", [128, 4, TP], BF16)
            kst = [sb("kst%d" % i, [128, 4, 256], F32) for i in range(2)]
            kbf = [sb("kbf%d" % i, [128, 4, 256], BF16) for i in range(2)]
            vst = [sb("vst%d" % i, [128, 2, 512], F32) for i in range(2)]
            vbf = [sb("vbf%d" % i, [128, 2, 512], BF16) for i in range(2)]
            pts = sb("pts", [128, 32], BF16)
            rs32 = sb("rs32", [128, 32], F32)
            ssum = sb("ssum", [128, 4, 4], F32)
            CUT = int(os.environ.get("MK_CUT", "9"))
            if CUT >= 2:
                S.dma(xT[:, :, :256], V(memT, memT.t.rearrange("(k p) t -> p k t", p=128)))
                rmsnorm_tile(xT, 256, VN_MEM + 32 * l, sq, xn, rstd, PS[7])
            for mc in range(2 if CUT >= 3 else 0):
                for half in range(2):
                    pb = PS[half]
                    for kc in range(8):
                        S.mm(pb[:, :], xn[:, kc, mc * 128:(mc + 1) * 128], Wkv[:, kc, half * 512:(half + 1) * 512],
                             start=(kc == 0), stop=(kc == 7))
                    SUBV = int(os.environ.get("MK_SUB", "3"))
                    if SUBV >= 1:
                        S.copy(kvf[:, half * 512:(half + 1) * 512], pb[:, :], eng="scalar")
                    if half == 1 and SUBV >= 2:
                        S.copy(Vm[:, mc, :], pb[:, :], eng="scalar")
                if not os.environ.get("MK_NOMEMDMA"):
                    S.dma(o_memkv[l][mc * 128:(mc + 1) * 128, :], kvf[:, :])
            for h in range(4 if CUT >= 4 else 0):
                pb = PS[2 + h % 2]
                for kc in range(8):
                    S.mm(pb[:, :256], Wkv[:, kc, h * 128:(h + 1) * 128], xn[:, kc, :256], start=(kc == 0), stop=(kc == 7))
                S.copy(KT[:, h, :], pb[:, :256], eng="scalar")

            for ti, (t0, T) in enumerate([] if mem_only else tiles):
                is_sample = (t0 >= SEQ)
                S.dma(xT[:, :, :T], xview(Xin, t0, T))
                rmsnorm_tile(xT, T, VN_X + 32 * l, sq, xn, rstd, PS[7])
                for h in range(4):
                    pb = PS[h % 2]
                    for kc in range(8):
                        S.mm(pb[:, :T], Wq[:, kc, h * 128:(h + 1) * 128], xn[:, kc, :T], start=(kc == 0), stop=(kc == 7))
                    S.copy(qT[:, h, :T], pb[:, :T], eng="scalar")
                if not is_sample:
                    for h in range(4):
                        for mc in range(2):
                            S.mm(PS[2 + mc][:, :T], KT[:, h, mc * 128:(mc + 1) * 128], qT[:, h, :T])
                            S.act(PT[:, mc, :T], PS[2 + mc][:, :T], AF.Exp, scale=128 ** -0.5)
                        for mc in range(2):
                            S.mm(PS[4][:, :T], ones_b[:, :], PT[:, mc, :T], start=(mc == 0), stop=(mc == 1))
                        for mc in range(2):
                            S.mm(PS[5][:, :T], Vm[:, mc, h * 128:(h + 1) * 128], PT[:, mc, :T], start=(mc == 0), stop=(mc == 1))
                        S.copy(rinv[:, :T], PS[4][:, :T], eng="vector")
                        S.recip(rinv[:, :T], rinv[:, :T])
                        S.tt(OT[:, h, :T], PS[5][:, :T], rinv[:, :T], ALU.mult)
                else:
                    for b in range(16):
                        ks, kb_, vs, vb_ = kst[b % 2], kbf[b % 2], vst[b % 2], vbf[b % 2]
                        S.dma(ks[:, :, :], memKT[l][b])
                        S.copy(kb_[:, :, :], ks[:, :, :], eng="gpsimd")
                        S.dma(vs[:, :, :], memV[l][b])
                        S.copy(vb_[:, :, :], vs[:, :, :], eng="gpsimd")
                        for h in range(4):
                            for mc in range(2):
                                c0 = (h * 2 + mc) * 4
                                S.mm(PS[2][:, c0:c0 + 4], kb_[:, h, mc * 128:(mc + 1) * 128], qT[:, h, 4 * b:4 * b + 4])
                        S.act(pts[:, 0:32], PS[2][:, 0:32], AF.Exp, scale=128 ** -0.5)
                        S.mm(PS[4][:, 0:32], ones_b[:, :], pts[:, 0:32])
                        S.copy(rs32[:, 0:32], PS[4][:, 0:32], eng="vector")
                        r4 = rs32.t[:, 0:32].rearrange("p (h m t) -> p h m t", h=4, m=2)
                        S.tt(ssum[:, :, :], V(rs32, r4[:, :, 0, :]), V(rs32, r4[:, :, 1, :]), ALU.add)
                        S.recip(ssum[:, :, :], ssum[:, :, :])
                        for h in range(4):
                            for mc in range(2):
                                c0 = (h * 2 + mc) * 4
                                S.mm(PS[5][:, h * 4:(h + 1) * 4], vb_[:, mc, h * 128:(h + 1) * 128], pts[:, c0:c0 + 4],
                                     start=(mc == 0), stop=(mc == 1))
                        S.tt(OT[:, :, 4 * b:4 * b + 4], V(PS[5], PS[5].t[:, 0:16].rearrange("p (h t) -> p h t", h=4)),
                             ssum[:, :, :], ALU.mult)
                for oc in range(8):
                    pb = PS[oc % 2]
                    for kc in range(4):
                        S.mm(pb[:, :T], Wo[:, kc, oc * 128:(oc + 1) * 128], OT[:, kc, :T], start=(kc == 0), stop=(kc == 3))
                    S.tt(xT[:, oc, :T], xT[:, oc, :T], pb[:, :T], ALU.add)
                S.dma(xview(Xout, t0, T), xT[:, :, :T])
            S.barrier()

    def phase_ffn(l, Xin, Xout):
        with ExitStack() as pes:
            def sb(name, shape, dt):
                name = name + "_L%d" % l
                t = pes.enter_context(nc.sbuf_tensor(name, list(shape), dt))
                b = Buf(name, t, "sb")
                S.bufs.append(b)
                return b
            TF = 256
            Wu = sb("Wu", [128, 8, 2 * DFF], BF16)
            Wd = sb("Wd", [128, 22, D], BF16)
            fv = sb("fv", [128, 44, 4], F32)
            load_w_bf16(Wu, lambda kc, a, b: w_up[l][kc * 128:(kc + 1) * 128, a:b], 8, 2 * DFF)
            load_w_bf16(Wd, lambda kc, a, b: w_down[l][kc * 128:(kc + 1) * 128, a:b], 22, D)
            S.dma(fv[:, :, :], ffnv[l][:, :, :])
            xT = sb("fxT", [128, 8, TF], F32)
            sq = sb("fsq", [128, 8, TF], BF16)
            xn = sb("fxn", [128, 8, TF], BF16)
            rstd = sb("frstd", [128, TF], F32)
            u = [sb("fu%d" % i, [128, 2 + TF], F32) for i in range(4)]
            utail = sb("utail", [128, 44, 2], F32)
            c1 = [sb("fc1%d" % i, [128, TF], F32) for i in range(2)]
            c2 = [sb("fc2%d" % i, [128, TF], F32) for i in range(2)]
            hT = sb("hT", [128, 22, TF], BF16)
            tT = sb("tT", [44, 2, 128], F32)
            S.memset(utail[:, :, :], 0.0)
            fh = sb("fh", [128, 44, 16, 2], F32)
            fo = sb("fo", [128, 44, 16, 2], F32)
            S.dma(fh[:, :, :, :], ffnhT[l][:, :, :, :])
            ftiles = []
            for (t0, T) in tiles:
                for s0 in range(0, T, TF):
                    ftiles.append((t0 + s0, min(TF, T - s0)))
            for ti, (t0, T) in enumerate(ftiles):
                is_sample = (t0 >= SEQ)
                S.dma(xT[:, :, :T], xview(Xin, t0, T))
                rmsnorm_tile(xT, T, VN_FFN + 32 * l, sq, xn, rstd, PS[7])
                if True:

                    def conv_chunk(j, ub, pb, cout, eng):
                        for kc in range(8):
                            S.mm(pb[:, :T], Wu[:, kc, j * 128:(j + 1) * 128], xn[:, kc, :T], start=(kc == 0), stop=(kc == 7))
                        if not is_sample:
                            S.copy(ub[:, 0:2], utail[:, j, :], eng="gpsimd")
                            S.copy(ub[:, 2:2 + T], pb[:, :T], eng="scalar")
                            S.copy(utail[:, j, :], ub[:, T:T + 2], eng="gpsimd")
                            S.ts(cout[:, :T], ub[:, 0:T], fv[:, j, 0:1], fv[:, j, 3:4], ALU.mult, ALU.add, eng=eng)
                            S.stt(cout[:, :T], ub[:, 1:T + 1], fv[:, j, 1:2], cout[:, :T], ALU.mult, ALU.add, eng=eng)
                            S.stt(cout[:, :T], ub[:, 2:T + 2], fv[:, j, 2:3], cout[:, :T], ALU.mult, ALU.add, eng=eng)
                        else:
                            u3 = ub.t[:, 0:96].rearrange("p (b t) -> p b t", t=6)
                            c3 = V(cout, cout.t[:, 0:64].rearrange("p (b t) -> p b t", t=4))
                            S.copy(V(ub, u3[:, :, 0:2]), fh[:, j, :, :], eng="gpsimd")
                            S.copy(V(ub, u3[:, :, 2:6]), V(pb, pb.t[:, 0:64].rearrange("p (b t) -> p b t", t=4)), eng="scalar")
                            S.copy(fo[:, j, :, :], V(ub, u3[:, :, 4:6]), eng="gpsimd")
                            S.ts(c3, V(ub, u3[:, :, 0:4]), fv[:, j, 0:1], fv[:, j, 3:4], ALU.mult, ALU.add)
                            S.stt(c3, V(ub, u3[:, :, 1:5]), fv[:, j, 1:2], c3, ALU.mult, ALU.add)
                            S.stt(c3, V(ub, u3[:, :, 2:6]), fv[:, j, 2:3], c3, ALU.mult, ALU.add)
                    for j in range(22):
                        conv_chunk(j, u[(2 * j) % 4], PS[(2 * j) % 4], c1[j % 2], "vector")
                        conv_chunk(22 + j, u[(2 * j + 1) % 4], PS[(2 * j + 1) % 4], c2[j % 2], "gpsimd")
                        S.act(c1[j % 2][:, :T], c1[j % 2][:, :T], AF.Silu)
                        S.tt(hT[:, j, :T], c1[j % 2][:, :T], c2[j % 2][:, :T], ALU.mult)
                    for oc in range(8):
                        pb = PS[4 + oc % 2]
                        for j in range(22):
                            S.mm(pb[:, :T], Wd[:, j, oc * 128:(oc + 1) * 128], hT[:, j, :T], start=(j == 0), stop=(j == 21))
                        S.tt(xT[:, oc, :T], xT[:, oc, :T], pb[:, :T], ALU.add)
                S.dma(xview(Xout, t0, T), xT[:, :, :T])
                if is_sample:
                    S.dma(o_ffn_s[l][:, :, :, :], fo[:, :, :, :])
                if t0 + T == NTILE * TP:
                    for r in range(2):
                        S.transpose(PS[6][0:44, r * 128:(r + 1) * 128], utail[:, :, r], ident[:, :])
                    S.copy(V(tT, tT.t[:, :, :].rearrange("j r p -> j (r p)")), PS[6][0:44, 0:256], eng="vector")
                    S.dma(V(o_ffn_p[l], o_ffn_p[l].t.rearrange("r (j p) -> j r p", p=128)), tT[:, :, :])
            S.barrier()

    def phase_nsa_kv(Xin):
        with ExitStack() as pes:
            def sb(name, shape, dt):
                t = pes.enter_context(nc.sbuf_tensor(name, list(shape), dt))
                b = Buf(name, t, "sb")
                S.bufs.append(b)
                return b
            Wc = sb("Wc", [128, 8, 1536], BF16)
            load_w_bf16(Wc, lambda kc, a, b: w_in_c[kc * 128:(kc + 1) * 128, 1024 + a:1024 + b], 8, 1536)
            xT = sb("nxT", [128, 8, TP], F32)
            sq = sb("nsq", [128, 8, TP], BF16)
            xn = sb("nxn", [128, 8, TP], BF16)
            rstd = sb("nrstd", [128, TP], F32)
            kvs = [sb("kvs%d" % i, [128, 1536], F32) for i in range(2)]
            n = 0
            for ti, (t0, T) in enumerate(tiles):
                S.dma(xT[:, :, :T], xview(Xin, t0, T))
                rmsnorm_tile(xT, T, VN_MIX + 32, sq, xn, rstd, PS[7])
                if t0 >= SEQ:
                    kb = kvs[n % 2]
                    n += 1
                    for g3 in range(3):
                        pb = PS[g3]
                        for kc in range(8):
                            S.mm(pb[0:64, :], xn[:, kc, 0:64], Wc[:, kc, g3 * 512:(g3 + 1) * 512], start=(kc == 0), stop=(kc == 7))
                        S.copy(kb[0:64, g3 * 512:(g3 + 1) * 512], pb[0:64, :], eng="scalar")
                    S.dma(o_nsa_kv_s[:, :], kb[0:64, 0:1024])
                    for b in range(16):
                        S.dma(o_win_s[b, 508:512, :], kb[4 * b:4 * b + 4, 1024:1536])
                        S.dma(o_win_s[b, 0:508, :], win_cache[b, 4:512, :])
                    continue
                for c in range(T // 128):
                    cs = slice(c * 128, (c + 1) * 128)
                    tok = t0 + c * 128
                    inwin = tok >= NTILE * TP - 512
                    kb = kvs[n % 2]
                    n += 1
                    for g3 in range(3 if inwin else 2):
                        pb = PS[g3]
                        for kc in range(8):
                            S.mm(pb[:, :], xn[:, kc, cs], Wc[:, kc, g3 * 512:(g3 + 1) * 512], start=(kc == 0), stop=(kc == 7))
                        S.copy(kb[:, g3 * 512:(g3 + 1) * 512], pb[:, :], eng=("scalar" if g3 % 2 == 0 else "vector"))
                    S.dma(o_nsa_kv[tok:tok + 128, :], kb[:, 0:1024])
                    if inwin:
                        w0 = tok - (NTILE * TP - 512)
                        S.dma(o_win_p[w0:w0 + 128, :], kb[:, 1024:1536])
            S.barrier()

    def phase_nsa(Xin, Xout):
        TQ = 128
        NQ = NTILE * TP // TQ
        NEG = -30000.0
        with ExitStack() as pes:
            def sb(name, shape, dt):
                t = pes.enter_context(nc.sbuf_tensor(name, list(shape), dt))
                b = Buf(name, t, "sb")
                S.bufs.append(b)
                return b
            Wc = sb("NWc", [128, 8, 1584], BF16)
            for r0_ in range(0, D, 128):
                S.dma(WcD[r0_:r0_ + 128, 0:2048], w_in_c[r0_:r0_ + 128, 0:2048], eng="gpsimd")
                S.dma(WcD[r0_:r0_ + 128, 2048:2608], w_in_c[r0_:r0_ + 128, 2048:2608], eng="gpsimd")
            S.dma(WoD[:, :], w_out_c[:, :], eng="gpsimd")
            S.barrier()
            wcv = WcD.t.rearrange("(k p) n -> p k n", p=128)
            WoS = [sb("WoS%d" % i, [64, 16, 128], BF16) for i in range(2)]
            W1 = sb("NW1", [64, 2, 32, 64], BF16)
            W2 = sb("NW2", [64, 2, 64], BF16)
            PEf = sb("PEf", [64, 2, 32], F32)
            PEb = sb("PEb", [64, 2, 32], BF16)
            cb = sb("ncb", [64, 2], F32)
            for kd in range(2):
                for l0 in range(0, 32, 8):
                    S.dma(W1[:, kd, l0:l0 + 8, :], V(w_cmp1, w_cmp1.t[kd, l0:l0 + 8].rearrange("l d e -> d l e")), eng="gpsimd")
                S.dma(W2[:, kd, :], w_cmp2[kd], eng="gpsimd")
            S.dma(PEf[:, :, :], pe_T[:, :, :])
            S.copy(PEb[:, :, :], PEf[:, :, :], eng="gpsimd")
            for kd in range(2):
                for l in range(32):
                    S.mm(PS[6][0:64, kd:kd + 1], W1[:, kd, l, :], PEb[:, kd, l:l + 1], start=(l == 0), stop=(l == 31))
            S.copy(cb[:, :], PS[6][0:64, 0:2], eng="vector")
            KS = sb("KS", [128, 4, SEQ], BF16)
            for g in range(4):
                S.dma(KS[64:128, g, :], n_ind[:, :])
            VS = sb("VS", [128, SEQ // 128, 4, 64], BF16)
            KW = sb("KW", [64, 4, 5 * 128], BF16)
            VW = sb("VW", [128, 5, 4, 64], BF16)
            KC = sb("KC", [64, 4, 512], BF16)
            VCb = sb("VCb", [128, 4, 4, 64], BF16)
            S.memset(KC[:, :, :], 0.0)
            S.memset(VCb[:, :, :, :], 0.0)
            TRIB = sb("TRIB", [128, 2, 512], BF16)
            CMB = sb("CMB", [128, 1016], BF16)
            CMT = sb("CMT", [128, 17, 128], BF16)
            FBB = sb("FBB", [128, 254], F32)
            SEL = sb("SEL", [48, 48 * 64], BF16)
            S.dma(TRIB[:, :, :], n_trib[:, :, :])
            S.dma(CMB[:, :], n_cmbig[:, :])
            S.dma(CMT[:, :, :], n_cmt[:, :, :])
            S.dma(FBB[:, :], n_fbbig[:, :])
            S.dma(SEL[:, :], n_sel[:, :])
            xT = sb("qxT", [128, 8, TQ], F32)
            sq = sb("qsq", [128, 8, TQ], BF16)
            xn = sb("qxn", [128, 8, TQ], BF16)
            rstd = sb("qrstd", [128, TQ], F32)
            QT = sb("QT", [64, 16, TQ], BF16)
            QA = [sb("QA%d" % i, [128, 4, TQ], BF16) for i in range(2)]
            CR = sb("CR", [64, 2, 4, 9, 16], BF16)
            hT = sb("nhT", [64, 2, 4, 8], BF16)
            hpad = sb("hpad", [64, 4, 128], BF16)
            GT = sb("GT", [48, TQ], BF16)
            EA = sb("EA", [128, 512], F32)
            Pg = sb("Pg", [128, 512], F32)
            rsum = sb("rsum", [128, 4], F32)
            imp = sb("imp", [128, 128], F32)
            sc2 = sb("sc2", [128, 128], F32)
            m8 = sb("m8", [128, 16], F32)
            sbias = sb("sbias", [128, 128], F32)
            sbb = sb("sbb", [128, 128], BF16)
            ET = [sb("ET%d" % i, [128, 512], BF16) for i in range(2)]
            rden = sb("nrden", [64, 512], F32)
            acc = sb("nacc", [64, 512], F32)
            OA = sb("OA", [64, 16, TQ], BF16)
            S.memset(CR[:, :, :, :, :], 0.0)
            etn = [0]

            def attend(qrhs_fn, chunks, acc_first, g, br, W=512, gt=None):
                n = len(chunks)
                gt = GT[:, :] if gt is None else gt

                SBK = [PS[2], PS[3], PS[0], PS[1]]

                def score(ci):
                    kl_, vl_, bias, qsel = chunks[ci][:4]
                    rows = chunks[ci][6] if len(chunks[ci]) > 6 else 128
                    pb = SBK[ci % 4]
                    S.mm(pb[0:rows, 0:W], kl_, qrhs_fn(qsel), start=True, stop=(bias is None))
                    if bias is not None:
                        S.mm(pb[0:rows, 0:W], identb[0:rows, 0:rows], bias, start=False, stop=True)
                score(0)
                if n > 1:
                    score(1)
                for ci, chk in enumerate(chunks):
                    kl_, vl_, bias, qsel, zr0, addb = chk[:6]
                    rows = chk[6] if len(chk) > 6 else 128
                    pb = SBK[ci % 4]
                    if ci + 2 < n:
                        score(ci + 2)
                    et = ET[etn[0] % 2]
                    etn[0] += 1
                    if addb is not None:
                        for r_ in range(4):
                            S.tt(EA[:, r_ * 128:(r_ + 1) * 128], pb[:, r_ * 128:(r_ + 1) * 128], addb, ALU.add)
                        S.act(et[:, :], EA[:, :], AF.Exp)
                    else:
                        S.act(et[0:rows, 0:W], pb[0:rows, 0:W], AF.Exp)
                    if zr0:
                        S.memset(et[0:1, 0:W], 0.0, eng="vector")
                    S.mm(PS[4][0:64, 0:W], ones_b[0:rows, 0:64], et[0:rows, 0:W], start=(ci == 0), stop=(ci == n - 1))
                    S.mm(PS[5][0:64, 0:W], vl_, et[0:rows, 0:W], start=(ci == 0), stop=(ci == n - 1))
                S.ts(rden[:, 0:W], PS[4][0:64, 0:W], 1e-30, None, ALU.max)
                S.recip(rden[:, 0:W], rden[:, 0:W])
                S.tt(rden[:, 0:W], PS[5][0:64, 0:W], rden[:, 0:W], ALU.mult)
                wq = W // 4
                for r in range(4):
                    col = (g * 4 + r) * 3 + br
                    S.mm(PS[6][0:64, r * wq:(r + 1) * wq], SEL[:, col * 64:(col + 1) * 64], gt)
                if acc_first:
                    S.tt(acc[:, 0:W], rden[:, 0:W], PS[6][0:64, 0:W], ALU.mult)
                else:
                    S.tt(rden[:, 0:W], rden[:, 0:W], PS[6][0:64, 0:W], ALU.mult)
                    S.tt(acc[:, 0:W], acc[:, 0:W], rden[:, 0:W], ALU.add)

            def compress_tile(i):
                for kd in range(2):
                    for g in range(4):
                        for l in range(32):
                            j0 = 0 if l < 16 else 1
                            S.mm(PS[6][0:64, (kd * 4 + g) * 8:(kd * 4 + g) * 8 + 8], W1[:, kd, l, :],
                                 CR[:, kd, g, j0:j0 + 8, l % 16], start=(l == 0), stop=(l == 31))
                    S.act(V(hT, hT.t[:, kd, :, :].rearrange("p g j -> p (g j)")), PS[6][0:64, kd * 32:kd * 32 + 32], AF.Silu,
                          bias=cb[:, kd:kd + 1])
                for g in range(4):
                    S.mm(PS[7][0:64, g * 8:g * 8 + 8], W2[:, 0, :], hT[:, 0, g, :])
                S.copy(KC[:, :, 8 * i:8 * i + 8], V(PS[7], PS[7].t[0:64, 0:32].rearrange("p (g j) -> p g j", g=4)), eng="scalar")
                r0 = (8 * i) % 128
                S.memset(hpad[:, :, :], 0.0)
                S.copy(hpad[:, :, r0:r0 + 8], hT[:, 1, :, :], eng="gpsimd")
                for g in range(4):
                    S.mm(PS[7][:, 64 + g * 64:128 + g * 64], hpad[:, g, :], W2[:, 1, :])
                vcv = V(VCb, VCb.t[:, i // 16, :, :].rearrange("p g d -> p (g d)"))
                S.tt(vcv, vcv, PS[7][:, 64:320], ALU.add)

            for i in range(NQ):
                t0 = i * TQ
                slot = i % 5
                S.dma(xT[:, :, :], xview(Xin, t0, TQ))
                rmsnorm_tile(xT, TQ, VN_MIX + 32, sq, xn, rstd, PS[7])

                woff = [0]

                def pj(psv, col0, M):
                    c0_ = col0 - woff[0]
                    for kc in range(8):
                        S.mm(psv, Wc[:, kc, c0_:c0_ + M], xn[:, kc, :], start=(kc == 0), stop=(kc == 7))
                S.dma(Wc[:, :, 0:1024], V(WcD, wcv[:, :, 0:1024]))
                for g in range(4):
                    pb = PS[g % 2]
                    for r in range(4):
                        pj(pb[0:64, r * 128:(r + 1) * 128], (g * 4 + r) * 64, 64)
                    S.act(V(QT, QT.t[:, 4 * g:4 * g + 4, :].rearrange("p h q -> p (h q)")), pb[0:64, :], AF.Copy, scale=0.125)
                S.dma(Wc[:, :, 0:1584], V(WcD, wcv[:, :, 1024:2608]))
                woff[0] = 1024
                for g in range(4):
                    pj(PS[0][0:64, g * 128:(g + 1) * 128], 1536 + g * 64, 64)
                S.copy(KS[0:64, :, t0:t0 + TQ], V(PS[0], PS[0].t[0:64, :].rearrange("p (g t) -> p g t", g=4)), eng="scalar")
                for g in range(4):
                    pj(PS[1][0:64, g * 128:(g + 1) * 128], 2048 + g * 64, 64)
                S.copy(KW[:, :, slot * 128:(slot + 1) * 128], V(PS[1], PS[1].t[0:64, :].rearrange("p (g t) -> p g t", g=4)), eng="scalar")
                if i > 0:
                    S.copy(CR[:, :, :, 0, :], CR[:, :, :, 8, :], eng="gpsimd")
                for kd in range(2):
                    pb = PS[kd]
                    for g in range(4):
                        pj(pb[0:64, g * 128:(g + 1) * 128], 1024 + kd * 256 + g * 64, 64)
                    S.copy(CR[:, kd, :, 1:9, :], V(pb, pb.t[0:64, :].rearrange("p (g j l) -> p g j l", g=4, j=8)), eng="scalar")
                for kc in range(8):
                    S.mm(PS[0][:, 0:256], xn[:, kc, :], Wc[:, kc, 768:1024], start=(kc == 0), stop=(kc == 7))
                for kc in range(8):
                    S.mm(PS[0][:, 256:512], xn[:, kc, :], Wc[:, kc, 1280:1536], start=(kc == 0), stop=(kc == 7))
                S.copy(V(VS, VS.t[:, i, :, :].rearrange("p g d -> p (g d)")), PS[0][:, 0:256], eng="scalar")
                S.copy(V(VW, VW.t[:, slot, :, :].rearrange("p g d -> p (g d)")), PS[0][:, 256:512], eng="scalar")
                pj(PS[1][0:48, 0:TQ], 2560, 48)
                S.act(GT[:, :], PS[1][0:48, 0:TQ], AF.Sigmoid)
                compress_tile(i)

                nch = (8 * i + 7) // 128 + 1
                for g in range(4):
                    for r in range(4):
                        pb = PS[r % 2]
                        S.mm(pb[:, :], QT[:, 4 * g + r, :], KC[:, g, :])
                        S.tt(EA[:, :], pb[:, :], CMB[:, 504 - 8 * i:504 - 8 * i + 512], ALU.add)
                        S.memset(EA[:, 0:1], NEG, eng="vector")
                        S.act(EA[:, :], EA[:, :], AF.Exp, accum=rsum[:, r:r + 1])
                        S.ts(rsum[:, r:r + 1], rsum[:, r:r + 1], 1e-30, None, ALU.max)
                        S.recip(rsum[:, r:r + 1], rsum[:, r:r + 1])
                        if r == 0:
                            S.ts(Pg[:, :], EA[:, :], rsum[:, 0:1], None, ALU.mult)
                        else:
                            S.stt(Pg[:, :], EA[:, :], rsum[:, r:r + 1], Pg[:, :], ALU.mult, ALU.add)
                    pgv = Pg.t[:, :].rearrange("p (j f) -> p j f", f=4)
                    S.op("vector", (lambda oa, ia: (lambda e: e.tensor_reduce(oa, ia, AX.X, ALU.add)))(imp.t[:, :], pgv),
                         reads=[Pg], writes=[imp])
                    S.tt(imp[:, 0:127], imp[:, 0:127], V(Pg, pgv[:, 1:128, 0]), ALU.add)
                    S.tt(imp[:, :], imp[:, :], FBB[:, 126 - 2 * i:126 - 2 * i + 128], ALU.add)
                    S.memset(imp[:, 0:1], 1e6, eng="vector")
                    S.op("vector", (lambda oa, ia: (lambda e: e.max(oa, ia)))(m8.t[:, 0:8], imp.t[:, :]), reads=[imp], writes=[m8])
                    S.op("vector", (lambda oa, a1, a2: (lambda e: e.match_replace(oa, a1, a2, -1e9)))(sc2.t[:, :], m8.t[:, 0:8], imp.t[:, :]),
                         reads=[imp, m8], writes=[sc2])
                    S.op("vector", (lambda oa, ia: (lambda e: e.max(oa, ia)))(m8.t[:, 8:16], sc2.t[:, :]), reads=[sc2], writes=[m8])
                    S.ts(sbias[:, :], imp[:, :], m8[:, 15:16], None, ALU.is_ge)
                    S.ts(sbias[:, :], sbias[:, :], -NEG, NEG, ALU.mult, ALU.add)
                    S.copy(sbb[:, :], sbias[:, :], eng="gpsimd")
                    for hf in range(2):
                        S.mm(PS[6][64:128, hf * 128:(hf + 1) * 128], sbb[:, hf * 64:(hf + 1) * 64], identb[:, :])
                    for hf in range(2):
                        S.copy(QA[hf][0:64, :, :], QT[:, 4 * g:4 * g + 4, :], eng="gpsimd")
                        for r in range(4):
                            S.copy(QA[hf][64:128, r, :], PS[6][64:128, hf * 128:(hf + 1) * 128], eng="scalar")
                    qg = V(QT, QT.t[:, 4 * g:4 * g + 4, :].rearrange("p h q -> p (h q)"))

                    ch = []
                    for m in range(nch):
                        dl = i - 16 * m
                        addb = CMT[:, dl, :] if dl <= 16 else None
                        ch.append((KC[:, g, m * 128:(m + 1) * 128], VCb[:, m, g, :], None, 0, (m == 0), addb))
                    attend(lambda q_: qg, ch, True, g, 0)
                    ch = []
                    for c in range(i + 1):
                        bias = V(TRIB, TRIB.t[:, 0, :]) if c == i else None
                        ch.append((KS[:, g, c * 128:(c + 1) * 128], VS[:, c, g, :], bias, (0 if c < 32 else 1), False, None))
                    attend(lambda q_: V(QA[q_], QA[q_].t[:, :, :].rearrange("p h q -> p (h q)")), ch, False, g, 1)
                    ch = []
                    for c in range(max(0, i - 4), i + 1):
                        bias = None
                        if c == i:
                            bias = V(TRIB, TRIB.t[:, 0, :])
                        elif c == i - 4:
                            bias = V(TRIB, TRIB.t[:, 1, :])
                        sl = c % 5
                        ch.append((KW[:, g, sl * 128:(sl + 1) * 128], VW[:, sl, g, :], bias, 0, False, None))
                    attend(lambda q_: qg, ch, False, g, 2)
                    S.copy(V(OA, OA.t[:, 4 * g:4 * g + 4, :].rearrange("p h q -> p (h q)")), acc[:, :], eng="gpsimd")
                for oc in range(8):
                    ws = WoS[oc % 2]
                    S.dma(ws[:, :, :], V(WoD, WoD.t.rearrange("(h d) n -> d h n", d=64)[:, :, oc * 128:(oc + 1) * 128]))
                    pb = PS[oc % 2]
                    for h in range(16):
                        S.mm(pb[:, 0:TQ], ws[:, h, :], OA[:, h, :], start=(h == 0), stop=(h == 15))
                    S.tt(xT[:, oc, :], xT[:, oc, :], pb[:, 0:TQ], ALU.add)
                S.dma(xview(Xout, t0, TQ), xT[:, :, :])
            if tiles[-1][0] >= SEQ:
                nsmp = sb("nsmp", [128, 128], F32)
                nsb = sb("nsb", [128, 32], BF16)
                S.dma(nsmp[:, :], n_smp[:, :])
                S.dma(nsb[:, :], n_smpb[:, :])
                KNs = sb("KNs", [64, 4, ST], BF16)
                WNs = sb("WNs", [64, 4, ST], BF16)
                GTs = sb("GTs", [48, ST], BF16)
                Vnb = sb("Vnb", [4, 512], BF16)
                ptb = sb("ptb", [128, 16], I32)
                ptf = sb("ptf", [128, 16], F32)
                ptf2 = sb("ptf2", [128, 16], F32)
                idxF = sb("idxF", [128, 16], I32)
                idxV = sb("idxV", [128, 16], I32)
                QAb = sb("QAb", [128, 4, 4], BF16)
                S.dma(xT[:, :, 0:ST], xview(Xin, SEQ, ST))
                rmsnorm_tile(xT, ST, VN_MIX + 32, sq, xn, rstd, PS[7])

                def pjS(psv, c0_, M):
                    for kc in range(8):
                        S.mm(psv, Wc[:, kc, c0_:c0_ + M], xn[:, kc, 0:ST], start=(kc == 0), stop=(kc == 7))
                S.dma(Wc[:, :, 0:1024], V(WcD, wcv[:, :, 0:1024]))
                for g in range(4):
                    pb = PS[g % 2]
                    for r in range(4):
                        pjS(pb[0:64, r * 64:(r + 1) * 64], (g * 4 + r) * 64, 64)
                    S.act(QT[:, 4 * g:4 * g + 4, 0:ST], V(pb, pb.t[0:64, 0:256].rearrange("p (h q) -> p h q", h=4)), AF.Copy, scale=0.125)
                S.dma(Wc[:, :, 0:1584], V(WcD, wcv[:, :, 1024:2608]))
                for g in range(4):
                    pjS(PS[0][0:64, g * 64:(g + 1) * 64], 512 + g * 64, 64)
                S.copy(KNs[:, :, :], V(PS[0], PS[0].t[0:64, 0:256].rearrange("p (g t) -> p g t", g=4)), eng="scalar")
                for g in range(4):
                    pjS(PS[1][0:64, g * 64:(g + 1) * 64], 1024 + g * 64, 64)
                S.copy(WNs[:, :, :], V(PS[1], PS[1].t[0:64, 0:256].rearrange("p (g t) -> p g t", g=4)), eng="scalar")
                pjS(PS[0][0:48, 256:256 + ST], 1536, 48)
                S.act(GTs[:, :], PS[0][0:48, 256:256 + ST], AF.Sigmoid)
                tri4 = nsb[0:4, 16:32]
                wbias = nsb[:, 0:16]
                for b in range(16):
                    bs = slice(4 * b, 4 * b + 4)
                    S.dma(ptb[:, :], pt_bc[b])
                    S.copy(ptf[:, :], ptb[:, :], eng="vector")
                    S.ts(ptf2[:, :], ptf[:, :], 64.0, nsmp[:, 0:1], ALU.mult, ALU.add)
                    S.copy(idxF[:, :], ptf2[:, :], eng="vector")
                    S.ts(ptf2[:, :], ptf[:, :], 128.0, nsmp[:, 0:1], ALU.mult, ALU.add)
                    S.copy(idxV[:, :], ptf2[:, :], eng="vector")
                    S.memset(VCb[:, 0, :, :], 0.0)
                    for j in range(16):
                        if j == 0:
                            S.memset(CR[:, :, :, 0, :], 0.0)
                        else:
                            S.copy(CR[:, :, :, 0, :], CR[:, :, :, 8, :], eng="gpsimd")
                        for kd3 in range(3):
                            S.idma(EA[0:64, :], poolF[kd3][:, :], idxF[0:64, j:j + 1])
                            if kd3 < 2:
                                S.copy(CR[:, kd3, :, 1:9, :], V(EA, EA.t[0:64, :].rearrange("p (g j l) -> p g j l", g=4, j=8)), eng="scalar")
                            else:
                                S.copy(KS[0:64, :, 128 * j:128 * j + 128], V(EA, EA.t[0:64, :].rearrange("p (g t) -> p g t", g=4)), eng="scalar")
                        S.idma(Pg[:, 0:256], poolV[:, :], idxV[:, j:j + 1])
                        S.copy(V(VS, VS.t[:, j, :, :].rearrange("p g d -> p (g d)")), Pg[:, 0:256], eng="scalar")
                        compress_tile(j)
                    for kc in range(8):
                        S.mm(PS[0][0:4, 0:256], xn[:, kc, bs], Wc[:, kc, 768:1024], start=(kc == 0), stop=(kc == 7))
                    for kc in range(8):
                        S.mm(PS[0][0:4, 256:512], xn[:, kc, bs], Wc[:, kc, 1280:1536], start=(kc == 0), stop=(kc == 7))
                    S.copy(Vnb[:, :], PS[0][0:4, :], eng="scalar")
                    S.dma(KW[:, :, 0:512], winKT[b], eng="gpsimd")
                    S.dma(V(VW, VW.t[:, 0:4, :, :].rearrange("p c g d -> p c (g d)")), winVs[b], eng="gpsimd")
                    gtb = GTs[:, bs]
                    for g in range(4):
                        for r in range(4):
                            pb = PS[r % 2]
                            S.mm(pb[0:4, 0:128], QT[:, 4 * g + r, bs], KC[:, g, 0:128])
                            S.ts(EA[0:4, 0:128], pb[0:4, 0:128], 1.0, None, ALU.mult)
                            S.memset(EA[0:4, 0:1], NEG, eng="vector")
                            S.act(EA[0:4, 0:128], EA[0:4, 0:128], AF.Exp, accum=rsum[0:4, r:r + 1])
                            S.ts(rsum[0:4, r:r + 1], rsum[0:4, r:r + 1], 1e-30, None, ALU.max)
                            S.recip(rsum[0:4, r:r + 1], rsum[0:4, r:r + 1])
                            if r == 0:
                                S.ts(Pg[0:4, 0:128], EA[0:4, 0:128], rsum[0:4, 0:1], None, ALU.mult)
                            else:
                                S.stt(Pg[0:4, 0:128], EA[0:4, 0:128], rsum[0:4, r:r + 1], Pg[0:4, 0:128], ALU.mult, ALU.add)
                        pgs = Pg.t[0:4, 0:128].rearrange("p (j f) -> p j f", f=4)
                        S.memset(imp[0:4, 0:64], 0.0, eng="vector")
                        S.op("vector", (lambda oa, ia: (lambda e: e.tensor_reduce(oa, ia, AX.X, ALU.add)))(imp.t[0:4, 0:32], pgs),
                             reads=[Pg], writes=[imp])
                        S.tt(imp[0:4, 0:31], imp[0:4, 0:31], V(Pg, pgs[:, 1:32, 0]), ALU.add)
                        S.tt(imp[0:4, 0:64], imp[0:4, 0:64], nsmp[0:4, 64:128], ALU.add)
                        S.op("vector", (lambda oa, ia: (lambda e: e.max(oa, ia)))(m8.t[0:4, 0:8], imp.t[0:4, 0:64]), reads=[imp], writes=[m8])
                        S.op("vector", (lambda oa, a1, a2: (lambda e: e.match_replace(oa, a1, a2, -1e9)))(sc2.t[0:4, 0:64], m8.t[0:4, 0:8], imp.t[0:4, 0:64]),
                             reads=[imp, m8], writes=[sc2])
                        S.op("vector", (lambda oa, ia: (lambda e: e.max(oa, ia)))(m8.t[0:4, 8:16], sc2.t[0:4, 0:64]), reads=[sc2], writes=[m8])
                        S.ts(sbias[0:4, 0:64], imp[0:4, 0:64], m8[0:4, 15:16], None, ALU.is_ge)
                        S.ts(sbias[0:4, 0:64], sbias[0:4, 0:64], -NEG, NEG, ALU.mult, ALU.add)
                        S.copy(sbb[0:4, 0:64], sbias[0:4, 0:64], eng="gpsimd")
                        S.mm(PS[6][64:128, 0:4], sbb[0:4, 0:64], identb[0:4, 0:4])
                        S.copy(QAb[0:64, :, :], QT[:, 4 * g:4 * g + 4, bs], eng="gpsimd")
                        for r in range(4):
                            S.copy(QAb[64:128, r, :], PS[6][64:128, 0:4], eng="scalar")
                        qgb = V(QAb, QAb.t[0:64, :, :].rearrange("p r t -> p (r t)"))
                        qab = V(QAb, QAb.t[:, :, :].rearrange("p r t -> p (r t)"))
                        attend(lambda q_: qgb, [(KC[:, g, 0:128], VCb[:, 0, g, :], None, 0, True, None, 128)], True, g, 0, W=16, gt=gtb)
                        ch = [(KS[:, g, c * 128:(c + 1) * 128], VS[:, c, g, :], None, 1, False, None, 128) for c in range(16)]
                        ch.append((KNs[:, g, bs], Vnb[0:4, g * 64:(g + 1) * 64], tri4, 0, False, None, 4))
                        attend(lambda q_: (qab if q_ == 1 else qgb), ch, False, g, 1, W=16, gt=gtb)
                        ch = [(KW[:, g, c * 128:(c + 1) * 128], VW[:, c, g, :], (wbias if c == 0 else None), 0, False, None, 128)
                              for c in range(4)]
                        ch.append((WNs[:, g, bs], Vnb[0:4, 256 + g * 64:256 + (g + 1) * 64], tri4, 0, False, None, 4))
                        attend(lambda q_: qgb, ch, False, g, 2, W=16, gt=gtb)
                        S.copy(OA[:, 4 * g:4 * g + 4, bs], V(acc, acc.t[:, 0:16].rearrange("p (r t) -> p r t", r=4)), eng="gpsimd")
                for oc in range(8):
                    ws = WoS[oc % 2]
                    S.dma(ws[:, :, :], V(WoD, WoD.t.rearrange("(h d) n -> d h n", d=64)[:, :, oc * 128:(oc + 1) * 128]))
                    pb = PS[oc % 2]
                    for h in range(16):
                        S.mm(pb[:, 0:ST], ws[:, h, :], OA[:, h, 0:ST], start=(h == 0), stop=(h == 15))
                    S.tt(xT[:, oc, 0:ST], xT[:, oc, 0:ST], pb[:, 0:ST], ALU.add)
                S.dma(xview(Xout, SEQ, ST), xT[:, :, 0:ST])
            S.barrier()

    def phase_final(Xin):
        with ExitStack() as pes:
            def sb(name, shape, dt):
                t = pes.enter_context(nc.sbuf_tensor(name, list(shape), dt))
                b = Buf(name, t, "sb")
                S.bufs.append(b)
                return b
            xT = sb("zxT", [128, 8, TP], F32)
            sq = sb("zsq", [128, 8, TP], BF16)
            xn = sb("zxn", [128, 8, TP], BF16)
            rstd = sb("zrstd", [128, TP], F32)
            for ti, (t0, T) in enumerate(tiles):
                S.dma(xT[:, :, :T], xview(Xin, t0, T))
                rmsnorm_tile(xT, T, 80, sq, xn, rstd, PS[7])
                for kc in range(8):
                    S.stt(xT[:, kc, :T], xT[:, kc, :T], vec[:, 80 + kc:81 + kc], rstd[:, :T], ALU.mult, ALU.mult)
                S.dma(xview(o_yT, t0, T), xT[:, :, :T])
            S.barrier()

    phase_mixer_a(xT_in, XA)
    phase_cross(0, XA, XB)
    phase_ffn(0, XB, XA)
    phase_nsa_kv(XA)
    if os.environ.get("MK_NSA", "1") != "0":
        phase_nsa(XA, XB)
        if os.environ.get("MK_NSAONLY"):
            S.dma(o_yT[:, :], XB[:, :])
            S.barrier()
        else:
            phase_cross(1, XB, XA)
            phase_ffn(1, XA, XB)
            phase_final(XB)
    else:
        phase_cross(1, XA, XB)
        phase_ffn(1, XB, XA)
        phase_final(XA)
    S.emit()
    es.close()
    return nc


def kernel(**inp):
    f32 = np.float32
    g = lambda k: np.asarray(inp[k])
    nc = build()
    x_prompt = g("x_prompt")
    x_sample = g("x_sample")
    ident = np.eye(128, dtype=f32)
    jj, ii = np.meshgrid(np.arange(128), np.arange(128), indexing="ij")
    mask = (jj <= ii).astype(f32)
    triN = (-mask / 16.0).astype(f32)
    triU = (-(jj > ii).astype(f32) / 16.0).astype(f32)

    def pk(v):
        return np.asarray(v, f32).reshape(-1, 128).T

    vecs = np.zeros((128, 96), f32)
    for l in range(2):
        vecs[:, 32 * l + 0:32 * l + 8] = pk(g("norm_mix")[l])
        vecs[:, 32 * l + 8:32 * l + 16] = pk(g("norm_mem")[l])
        vecs[:, 32 * l + 16:32 * l + 24] = pk(g("norm_x")[l])
        vecs[:, 32 * l + 24:32 * l + 32] = pk(g("norm_ffn")[l])
    vecs[:, 80:88] = pk(g("norm_final"))
    vecs[:, 64:68] = pk(g("g_gla_out")[0])
    vecs[:, 68:72] = pk(g("b_dw_b")[0])
    vecs[:, 72:76] = pk(g("g_ln_b")[0])
    vecs[:, 76:80] = pk(g("b_ln_b")[0])

    w_dwT = np.ascontiguousarray(g("w_dw_b")[0].reshape(31, 4, 128).transpose(2, 1, 0)).astype(f32)
    ffnv = np.zeros((2, 128, 44, 4), f32)
    for l in range(2):
        wd = g("w_ffn_dw")[l].reshape(3, 44, 128)
        ffnv[l, :, :, 0:3] = wd.transpose(2, 1, 0)
        ffnv[l, :, :, 3] = g("b_ffn_dw")[l].reshape(44, 128).T
    xsT = np.ascontiguousarray(x_sample.reshape(128, 4, D))
    tj, ti_ = np.meshgrid(np.arange(64), np.arange(64), indexing="ij")
    same = (tj // 4 == ti_ // 4)
    c_smp = np.zeros((64, 208), f32)
    c_smp[:, 0:64] = -(same & (tj <= ti_)).astype(f32) / 16.0
    c_smp[:, 64:128] = -(same & (tj > ti_)).astype(f32) / 16.0
    c_smp[:, 128:192] = (same & (tj <= ti_)).astype(f32)
    c_smp[:, 192:208] = (np.arange(64)[:, None] // 4 == np.arange(16)[None, :]).astype(f32)
    ca = np.ascontiguousarray
    bf = ml_dtypes.bfloat16
    NEG = -30000.0
    keys = np.arange(SEQ)
    n_ind = ((keys[None, :] // 64) % 64 == np.arange(64)[:, None]).astype(bf)
    kk, qq = np.meshgrid(np.arange(128), np.arange(128), indexing="ij")
    trib = np.zeros((128, 2, 4, 128), f32)
    trib[:, 0] = np.where(kk > qq, NEG, 0.0)[:, None, :]
    trib[:, 1] = np.where(kk <= qq, NEG, 0.0)[:, None, :]
    n_trib = trib.reshape(128, 2, 512).astype(bf)
    ql = np.arange(128)[:, None]
    u = (np.arange(1016) - 504)[None, :]
    n_cmbig = np.where(16 * u + 15 <= ql, 0.0, NEG).astype(bf)
    sl_ = np.arange(128)[:, None, None]
    dl_ = np.arange(17)[None, :, None]
    qv = np.arange(128)[None, None, :]
    n_cmt = np.where(16 * sl_ + 15 - qv <= 128 * dl_, 0.0, NEG).astype(bf)
    v_ = (np.arange(254) - 126)[None, :]
    curl = (ql >= 64).astype(np.int64)
    n_fbbig = np.where((v_ == curl) | (v_ == curl - 1), 1e6, np.where(v_ > curl, -1e6, 0.0)).astype(f32)
    n_sel = (np.arange(48 * 64)[None, :] // 64 == np.arange(48)[:, None]).astype(bf)
    pe_T = ca(g("pe_cmp")[0].transpose(2, 0, 1)).astype(f32)
    pool = g("cache_nsa_kv")[0]
    poolF = [ca(pool[:, :, k].transpose(0, 3, 2, 1)).reshape(-1, 512) for k in range(3)]
    poolV = ca(pool[:, :, 3]).reshape(-1, 256)
    n_smp = np.zeros((128, 128), f32)
    n_smp[:, 0] = np.arange(128)
    n_smp[:, 64 + 0] = 1e6
    n_smp[:, 64 + 31] = 1e6
    n_smp[:, 64 + 32] = 1e6
    n_smp[:, 64 + 33:128] = -1e9
    nsb_ = np.zeros((128, 32), f32)
    tt_ = np.arange(16) % 4
    nsb_[:, 0:16] = np.where(np.arange(128)[:, None] <= tt_[None, :], NEG, 0.0)
    nsb_[0:4, 16:32] = np.where(np.arange(4)[:, None] > tt_[None, :], NEG, 0.0)
    n_smpb = nsb_.astype(bf)
    ptab = g("page_table").astype(np.int32)
    wincache = g("cache_nsa_win")[0]
    in_maps = []
    for c in range(8):
        b = c // 4
        xs = xsT[c * 16:(c + 1) * 16].reshape(64, D)
        xT = np.concatenate([x_prompt[b].T, xs.T], axis=1)
        in_maps.append({
            "xT_in": np.ascontiguousarray(xT, f32),
            "c_ident": ident, "c_triN": triN, "c_triU": triU, "c_mask": mask,
            "vecs": vecs,
            "w_in_a": g("w_in_a")[0], "w_out_a": g("w_out_a")[0],
            "w_gate": g("w_gate_a")[0], "b_gate": g("b_gate_a")[0].reshape(1, 256),
            "w_dwT": w_dwT,
            "memT": np.ascontiguousarray(g("mem_prompt")[b].T),
            "w_xq0": g("w_xq")[0], "w_mkv0": g("w_mem_kv")[0], "w_xo0": g("w_xo")[0],
            "w_xq1": g("w_xq")[1], "w_mkv1": g("w_mem_kv")[1], "w_xo1": g("w_xo")[1],
            "w_up0": g("w_up")[0], "w_down0": g("w_down")[0], "ffnv0": ffnv[0],
            "w_up1": g("w_up")[1], "w_down1": g("w_down")[1], "ffnv1": ffnv[1],
            "w_in_c": g("w_in_c")[0],
            "c_smp": c_smp,
            "poolF0": poolF[0], "poolF1": poolF[1], "poolF2": poolF[2], "poolV": poolV,
            "n_smp": n_smp, "n_smpb": n_smpb,
            "pt_bc": ca(np.broadcast_to(ptab[c * 16:(c + 1) * 16][:, None, :], (16, 128, 16))),
            "winKT": ca(wincache[c * 16:(c + 1) * 16, :, 0].transpose(0, 3, 2, 1)),
            "winVs": ca(wincache[c * 16:(c + 1) * 16, :, 1].reshape(16, 4, 128, 256).transpose(0, 2, 1, 3)),
            "w_out_c": g("w_out_c")[0], "w_cmp1": g("w_cmp1")[0], "w_cmp2": g("w_cmp2")[0], "pe_T": pe_T,
            "n_ind": n_ind, "n_trib": n_trib, "n_cmbig": n_cmbig, "n_cmt": n_cmt, "n_fbbig": n_fbbig, "n_sel": n_sel,
            "gla_s0": ca(g("cache_gla_state")[0, c * 16:(c + 1) * 16]),
            "convhT": ca(g("cache_conv")[0, c * 16:(c + 1) * 16].reshape(16, 30, 4, 128).transpose(3, 2, 0, 1)),
            "memKT0": ca(g("cache_mem_kv")[0, c * 16:(c + 1) * 16, :, 0].transpose(0, 3, 2, 1)),
            "memKT1": ca(g("cache_mem_kv")[1, c * 16:(c + 1) * 16, :, 0].transpose(0, 3, 2, 1)),
            "memV0": ca(g("cache_mem_kv")[0, c * 16:(c + 1) * 16, :, 1].reshape(16, 2, 128, 512).transpose(0, 2, 1, 3)),
            "memV1": ca(g("cache_mem_kv")[1, c * 16:(c + 1) * 16, :, 1].reshape(16, 2, 128, 512).transpose(0, 2, 1, 3)),
            "ffnhT0": ca(g("cache_ffn_conv")[0, c * 16:(c + 1) * 16].reshape(16, 2, 44, 128).transpose(3, 2, 0, 1)),
            "ffnhT1": ca(g("cache_ffn_conv")[1, c * 16:(c + 1) * 16].reshape(16, 2, 44, 128).transpose(3, 2, 0, 1)),
            "win_cache": ca(g("cache_nsa_win")[0, c * 16:(c + 1) * 16].reshape(16, 512, 512)),
        })
    res = run_bass_kernel_spmd(nc, in_maps, core_ids=list(range(8)))
    R = res.results
    if os.environ.get("MK_RAW"):
        return R
    P = [R[0], R[4]]
    y_prompt = np.stack([np.ascontiguousarray(r["o_yT"][:, :SEQ].T) for r in P]).astype(f32)
    y_sample = np.concatenate([np.ascontiguousarray(r["o_yT"][:, SEQ:].T).reshape(16, 4, D) for r in R]).astype(f32)
    gla_p = np.stack([r["o_gla_p"] for r in P])[None].astype(f32)
    gla_s = np.concatenate([r["o_gla_s"] for r in R])[None].astype(f32)
    conv_p = np.stack([r["o_conv_p"] for r in P])[None].astype(f32)
    conv_s = np.concatenate([r["o_conv_s"].transpose(2, 3, 1, 0).reshape(16, 30, 512) for r in R])[None].astype(f32)
    nsa_p = np.stack([r["o_nsa_kv"].reshape(SEQ, 4, 4, 64) for r in P])[None].astype(f32)
    nsa_s = np.concatenate([r["o_nsa_kv_s"].reshape(16, 4, 4, 4, 64) for r in R])[None].astype(f32)
    win_p = np.stack([r["o_win_p"].reshape(512, 2, 4, 64) for r in P])[None].astype(f32)
    win_s = np.concatenate([r["o_win_s"].reshape(16, 512, 2, 4, 64) for r in R])[None].astype(f32)
    mem_p = np.stack([np.stack([r["o_memkv%d" % l].reshape(256, 2, 4, 128) for r in P]) for l in range(2)]).astype(f32)
    ffn_p = np.stack([np.stack([r["o_ffn_p%d" % l] for r in P]) for l in range(2)]).astype(f32)
    ffn_s = np.stack([np.concatenate([r["o_ffn_s%d" % l].transpose(2, 3, 1, 0).reshape(16, 2, 2 * DFF) for r in R])
                      for l in range(2)]).astype(f32)
    return (y_prompt, y_sample, gla_p, gla_s, conv_p, conv_s, nsa_p, nsa_s, win_p, win_s, mem_p, ffn_p, ffn_s)
```
